# Optimizing a Trainium2 kernel written in Bass

```python
import jax, jax.numpy as jnp
from jax import lax
import numpy as np

D_MODEL = 4096
BATCH = 4
SEQ = 4096
DEPTH = 2

N_MIXERS = 2
EPS = 1e-6
GM_CHUNK = 128
GM_HEADS = 8
GM_WIDTH = D_MODEL
GM_HEAD_DIM = GM_WIDTH // GM_HEADS
ML_HEADS = 8
ML_QK_DIM = D_MODEL // ML_HEADS // 2
ML_V_DIM = D_MODEL // ML_HEADS
ML_CHUNK = 128
GATE_CAP = 15.0
ML_IN = 2 * ML_HEADS * ML_QK_DIM + 2 * ML_HEADS * ML_V_DIM + 2 * ML_HEADS
D_FF = 4 * D_MODEL
N_A = (DEPTH + 1) // 2
N_B = DEPTH // 2

kernel_name = "hybrid_gmlp_mlstm_sqrelu"


def rmsnorm(x, g):
    xf = x.astype(jnp.float32)
    y = xf * lax.rsqrt(jnp.mean(xf * xf, axis=-1, keepdims=True) + EPS)
    return (y * g.astype(jnp.float32)).astype(x.dtype)


def layernorm(x, g, b):
    xf = x.astype(jnp.float32)
    mu = jnp.mean(xf, axis=-1, keepdims=True)
    xc = xf - mu
    y = xc * lax.rsqrt(jnp.mean(xc * xc, axis=-1, keepdims=True) + EPS)
    return (y * g.astype(jnp.float32) + b.astype(jnp.float32)).astype(x.dtype)


def gmlp_mixer(h, w_in, ln_g, ln_b, w_s, b_s, w_out):
    B, S, _ = h.shape
    nc = S // GM_CHUNK
    z = jax.nn.gelu(h @ w_in, approximate=False)
    u, v = jnp.split(z, 2, axis=-1)
    v = layernorm(v, ln_g, ln_b)
    v = v.reshape(B, nc, GM_CHUNK, GM_HEADS, GM_HEAD_DIM)
    causal = jnp.tril(jnp.ones((GM_CHUNK, GM_CHUNK), dtype=bool))
    w = jnp.where(causal[None], w_s, 0).astype(v.dtype)
    mixed = jnp.einsum('hts,bcshd->bcthd', w, v) + b_s.T[:, :, None].astype(v.dtype)
    gated = u * mixed.reshape(B, S, GM_WIDTH)
    return gated @ w_out


def mlstm_mixer(h, w_in, b_gate, head_g, w_out):
    B, S, _ = h.shape
    H, DK, DV, L = ML_HEADS, ML_QK_DIM, ML_V_DIM, ML_CHUNK
    nc = S // L
    f32 = jnp.float32
    proj = h @ w_in
    q, k, v, o, gates = jnp.split(
        proj, [H * DK, 2 * H * DK, 2 * H * DK + H * DV, 2 * H * DK + 2 * H * DV], axis=-1)
    gates = (gates + b_gate).astype(f32)
    gates = GATE_CAP * jnp.tanh(gates / GATE_CAP)
    i_pre = gates[..., :H]
    log_f = jax.nn.log_sigmoid(gates[..., H:])

    def to_chunks(t, d):
        return t.reshape(B, nc, L, H, d).transpose(1, 0, 3, 2, 4).astype(f32)

    qc = to_chunks(q, DK) * (DK ** -0.5)
    kc = to_chunks(k, DK)
    vc = to_chunks(v, DV)
    ic = i_pre.reshape(B, nc, L, H).transpose(1, 0, 3, 2)
    fc = log_f.reshape(B, nc, L, H).transpose(1, 0, 3, 2)
    causal = jnp.tril(jnp.ones((L, L), dtype=bool))

    def step(carry, xs):
        C, n, m = carry
        q_, k_, v_, i_, lf = xs
        bcum = jnp.cumsum(lf, axis=-1)
        g = bcum[..., -1]
        d_log = bcum[..., :, None] - bcum[..., None, :] + i_[..., None, :]
        d_log = jnp.where(causal, d_log, -jnp.inf)
        inter_log = bcum + m[..., None]
        m_t = jnp.maximum(inter_log, jnp.max(d_log, axis=-1))
        dmat = jnp.exp(d_log - m_t[..., None])
        inter_w = jnp.exp(inter_log - m_t)
        s = jnp.einsum('bhtk,bhsk->bhts', q_, k_) * dmat
        num = (inter_w[..., None] * jnp.einsum('bhtk,bhkv->bhtv', q_, C)
               + jnp.einsum('bhts,bhsv->bhtv', s, v_))
        den = inter_w * jnp.einsum('bhtk,bhk->bht', q_, n) + jnp.sum(s, axis=-1)
        h_out = num / jnp.maximum(jnp.abs(den), jnp.exp(-m_t))[..., None]
        a = g[..., None] - bcum + i_
        m_new = jnp.maximum(g + m, jnp.max(a, axis=-1))
        decay = jnp.exp(g + m - m_new)
        kw = k_ * jnp.exp(a - m_new[..., None])[..., None]
        C_new = decay[..., None, None] * C + jnp.einsum('bhsk,bhsv->bhkv', kw, v_)
        n_new = decay[..., None] * n + jnp.sum(kw, axis=2)
        return (C_new, n_new, m_new), h_out

    init = (jnp.zeros((B, H, DK, DV), f32), jnp.zeros((B, H, DK), f32), jnp.zeros((B, H), f32))
    _, hs = lax.scan(step, init, (qc, kc, vc, ic, fc))
    hs = hs.transpose(1, 0, 3, 2, 4).reshape(B, S, H, DV)
    hs = rmsnorm(hs, head_g).astype(h.dtype)
    hs = hs.reshape(B, S, H * DV) * jax.nn.sigmoid(o)
    return hs @ w_out


def sqrelu_mlp(h, w_up, w_down):
    a = jax.nn.relu(h @ w_up)
    return (a * a) @ w_down


def setup_inputs(seed: int = 0) -> dict:
    key = jax.random.key(seed)
    ks = jax.random.split(key, 20)
    nrm = jax.random.normal
    f32 = jnp.float32
    x = nrm(ks[0], (BATCH, SEQ, D_MODEL), f32)
    norm_mix = 1.0 + 0.05 * nrm(ks[1], (DEPTH, D_MODEL), f32)
    norm_ffn = 1.0 + 0.05 * nrm(ks[2], (DEPTH, D_MODEL), f32)
    gm_w_in = nrm(ks[3], (N_A, D_MODEL, 2 * GM_WIDTH), f32) * D_MODEL ** -0.5
    gm_ln_g = 1.0 + 0.05 * nrm(ks[4], (N_A, GM_WIDTH), f32)
    gm_ln_b = 0.02 * nrm(ks[5], (N_A, GM_WIDTH), f32)
    gm_w_s = nrm(ks[6], (N_A, GM_HEADS, GM_CHUNK, GM_CHUNK), f32) * GM_CHUNK ** -0.5
    gm_b_s = 1.0 + 0.05 * nrm(ks[7], (N_A, GM_HEADS, GM_CHUNK), f32)
    gm_w_out = nrm(ks[8], (N_A, GM_WIDTH, D_MODEL), f32) * GM_WIDTH ** -0.5
    ml_w_in = nrm(ks[9], (N_B, D_MODEL, ML_IN), f32) * D_MODEL ** -0.5
    b_i = 0.1 * nrm(ks[10], (N_B, ML_HEADS), f32)
    b_f = 3.0 + 0.5 * nrm(ks[11], (N_B, ML_HEADS), f32)
    ml_b_gate = jnp.concatenate([b_i, b_f], axis=-1)
    ml_head_g = 1.0 + 0.05 * nrm(ks[12], (N_B, ML_HEADS, ML_V_DIM), f32)
    ml_w_out = nrm(ks[13], (N_B, ML_HEADS * ML_V_DIM, D_MODEL), f32) * (ML_HEADS * ML_V_DIM) ** -0.5
    ffn_w_up = nrm(ks[14], (DEPTH, D_MODEL, D_FF), f32) * D_MODEL ** -0.5
    ffn_w_down = nrm(ks[15], (DEPTH, D_FF, D_MODEL), f32) * (0.5 * D_FF ** -0.5)
    norm_final = 1.0 + 0.05 * nrm(ks[16], (D_MODEL,), f32)
    return {"x": x, "norm_mix": norm_mix, "norm_ffn": norm_ffn,
            "gm_w_in": gm_w_in, "gm_ln_g": gm_ln_g, "gm_ln_b": gm_ln_b,
            "gm_w_s": gm_w_s, "gm_b_s": gm_b_s, "gm_w_out": gm_w_out,
            "ml_w_in": ml_w_in, "ml_b_gate": ml_b_gate, "ml_head_g": ml_head_g,
            "ml_w_out": ml_w_out, "ffn_w_up": ffn_w_up, "ffn_w_down": ffn_w_down,
            "norm_final": norm_final}


def reference(x, norm_mix, norm_ffn, gm_w_in, gm_ln_g, gm_ln_b, gm_w_s, gm_b_s, gm_w_out,
              ml_w_in, ml_b_gate, ml_head_g, ml_w_out, ffn_w_up, ffn_w_down, norm_final):
    h = x
    for i in range(DEPTH):
        hn = rmsnorm(h, norm_mix[i])
        j = i // N_MIXERS
        if i % N_MIXERS == 0:
            mix = gmlp_mixer(hn, gm_w_in[j], gm_ln_g[j], gm_ln_b[j], gm_w_s[j], gm_b_s[j], gm_w_out[j])
        else:
            mix = mlstm_mixer(hn, ml_w_in[j], ml_b_gate[j], ml_head_g[j], ml_w_out[j])
        h = h + mix
        h = h + sqrelu_mlp(rmsnorm(h, norm_ffn[i]), ffn_w_up[i], ffn_w_down[i])
    return rmsnorm(h, norm_final)
```

```python
import os
import numpy as np
from contextlib import ExitStack
import concourse.bass as bass
import concourse.mybir as mybir
from concourse.bass_utils import run_bass_kernel_spmd

F32 = mybir.dt.float32
BF16 = mybir.dt.bfloat16
AF = mybir.ActivationFunctionType
ALU = mybir.AluOpType

D = 4096
KC = 32
NTOK = 2048
EPS = 1e-6
ENGS = ("pe", "act", "dve", "pool", "sp")
NCORES = 8


class Sem:
    def __init__(self, h, name):
        self.h = h
        self.n = 0
        self.name = name


class Prog:
    def __init__(self, num_devices=None):
        self.nc = bass.Bass("TRN2", target_bir_lowering=False, num_devices=num_devices)
        self.es = ExitStack()
        self.sems = {}
        self.stage_idx = 0
        self.barrier = Sem(None, "barrier")
        self.bar = []
        self.dram = {}
        E = self.es.enter_context
        nc = self.nc
        self.scrA = E(nc.sbuf_tensor("scrA", [128, 2], F32))
        self.scrD = E(nc.sbuf_tensor("scrD", [128, 2], F32))
        self.scrP = E(nc.sbuf_tensor("scrP", [128, 2], F32))

    def getsem(self, name):
        if name not in self.sems:
            self.sems[name] = Sem(self.es.enter_context(self.nc.semaphore(name)), name)
        return self.sems[name]

    def dt(self, name, shape, dtype, kind):
        t = self.nc.dram_tensor(name, list(shape), dtype, kind=kind)
        self.dram[name] = t.ap()
        return self.dram[name]


class Stage:
    def __init__(self, P, name):
        self.P = P
        self.nc = P.nc
        self.name = f"s{P.stage_idx}{name}"
        P.stage_idx += 1
        self.es = ExitStack()
        self.ops = {e: [] for e in ENGS}
        self.pending = []
        self.psrel = {}
        self.psnext = 0
        if P.bar:
            for e in ENGS:
                self.ops[e].append(("__wait__", (), {}, tuple(P.bar), None))

    def sb(self, name, shape, dt):
        return self.es.enter_context(self.nc.sbuf_tensor(f"{self.name}_{name}", list(shape), dt))

    def psum(self, name, shape, dt=F32):
        return self.es.enter_context(self.nc.psum_tensor(f"{self.name}_{name}", list(shape), dt))

    def sem(self, name):
        return Sem(None, name)

    def op(self, eng, fname, *args, waits=(), inc=None, **kw):
        waits = tuple(w for w in waits if w is not None)
        if inc is not None:
            sem = self.P.getsem(inc[0].name.split("__")[0] + "__" + eng)
            inc = (sem, inc[1])
        self.ops[eng].append((fname, args, kw, waits, inc))
        if inc is not None:
            inc[0].n += inc[1]
            return (inc[0], inc[0].n)
        return None

    def close(self):
        P = self.P
        bs = P.barrier
        b1 = self.op("act", "activation", out=P.scrA[:, 0:1], in_=P.scrA[:, 1:2], func=AF.Copy,
                     waits=self.pending, inc=(bs, 1))
        b2 = self.op("dve", "memset", P.scrD[:, 0:1], 0.0, inc=(bs, 1))
        b3 = self.op("pool", "memset", P.scrP[:, 0:1], 0.0, inc=(bs, 1))
        P.bar = [b1, b2, b3]
        nc = self.nc
        ops = self.ops

        def mk(engname):
            def f(eng):
                for (fname, args, kw, waits, inc) in ops[engname]:
                    for (sem, val) in waits:
                        eng.wait_ge(sem.h, val)
                    if fname == "__wait__":
                        continue
                    ins = getattr(eng, fname)(*args, **kw)
                    if inc is not None:
                        ins.then_inc(inc[0].h, inc[1])
            return f

        with nc.Block() as block:
            block.tensor(mk("pe"))
            block.scalar(mk("act"))
            block.vector(mk("dve"))
            block.gpsimd(mk("pool"))
            block.sync(mk("sp"))
        self.es.close()


class WStream:
    def __init__(self, S, views, NW=3):
        self.S = S
        self.views = views
        self.NW = NW
        self.bufs = [S.sb(f"wb{i}", [128, KC, 128], BF16) for i in range(NW)]
        self.wsem = [S.sem(f"wsem{i}") for i in range(NW)]
        self.loaded = [None] * len(views)
        self.rel = [None] * NW
        self.next_load = 0
        self.next_use = 0

    def _load(self, j):
        s = j % self.NW
        v, ncols = self.views[j]
        self.loaded[j] = self.S.op("pool", "dma_start", out=self.bufs[s][:, :, 0:ncols], in_=v,
                                   waits=[self.rel[s]], inc=(self.wsem[s], 16))

    def take(self):
        j = self.next_use
        while self.next_load < min(len(self.views), j + self.NW):
            self._load(self.next_load)
            self.next_load += 1
        self.next_use += 1
        return self.bufs[j % self.NW], self.loaded[j], j

    def release(self, j, sv):
        self.rel[j % self.NW] = sv


def wview(W, r0, c0, ncols=128):
    return (W[r0:r0 + D, c0:c0 + ncols].rearrange("(kc p) n -> p kc n", p=128), ncols)


def gemm_ws(S, ws, nblk, rhs, T, epi, ps, first_waits=(), mcols=128):
    pedone = S.sem("pedone")
    nslots = 4096 // T
    ntt = T // 512
    last = None
    for c in range(nblk):
        buf, wwait, j = ws.take()
        slot = S.psnext % nslots
        S.psnext += 1
        waits = [wwait, S.psrel.get(slot)] + (list(first_waits) if c == 0 else [])
        for kc in range(KC):
            for tt in range(ntt):
                lastmm = (kc == KC - 1 and tt == ntt - 1)
                v = S.op("pe", "matmul", out=ps[0:mcols, slot * T + tt * 512: slot * T + (tt + 1) * 512],
                         lhsT=buf[:, kc, 0:mcols], rhs=rhs[:, kc, tt * 512:(tt + 1) * 512],
                         start=(kc == 0), stop=(kc == KC - 1),
                         waits=waits if (kc == 0 and tt == 0) else (),
                         inc=(pedone, 1) if lastmm else None)
        ws.release(j, v)
        S.psrel[slot] = epi(c, ps[0:mcols, slot * T:(slot + 1) * T], v)
        last = S.psrel[slot]
    return last


def consts_basic(S):
    c = {}
    c["ones_bf"] = S.sb("ones_bf", [128, 128], BF16)
    c["eps"] = S.sb("eps_t", [128, 1], F32)
    cs = S.sem("cset")
    S.op("pool", "memset", c["ones_bf"][:, :], 1.0)
    c["rdy"] = S.op("pool", "memset", c["eps"][:, :], EPS, inc=(cs, 1))
    return c


def load_vec(S, dst, src_ap, eng="sp"):
    sm = S.sem("vecld")
    return S.op(eng, "dma_start", out=dst[:, :], in_=src_ap.rearrange("(c p) -> p c", p=128),
                allow_slow_non_contiguous=True, inc=(sm, 16))


def norm_pass(S, cst, hT, t0, T, gvec, gwait, xb, ps, want_sq_only=False):
    xf = [S.sb(f"xf{i}", [128, T], F32) for i in range(2)]
    sq = [S.sb(f"sq{i}", [128, T], BF16) for i in range(2)]
    srt = S.sb("srt", [128, T], F32)
    rstd = S.sb("rstd", [128, T], F32)
    xld = [S.sem(f"xld{i}") for i in range(2)]
    sqrdy = S.sem("sqrdy")
    sqfree_s = S.sem("sqfree")
    nrm = S.sem("nrm")
    slot = S.psnext % (4096 // T)
    S.psnext += 1
    pss = ps[:, slot * T:(slot + 1) * T]
    xfree = [None, None]
    sqfree = [None, None]
    ntt = T // 512
    v_sq = None
    for fc in range(KC):
        i = fc % 2
        v_ld = S.op("sp", "dma_start", out=xf[i][:, :], in_=hT[fc * 128:(fc + 1) * 128, t0:t0 + T],
                    waits=[xfree[i]], inc=(xld[i], 16))
        if not want_sq_only:
            S.op("act", "activation", out=xb[:, fc, :], in_=xf[i][:, :], func=AF.Copy,
                 scale=gvec[:, fc:fc + 1], waits=[v_ld, gwait if fc == 0 else None])
        v_sq = S.op("act", "activation", out=sq[i][:, :], in_=xf[i][:, :], func=AF.Square,
                    waits=[v_ld, sqfree[i]], inc=(sqrdy, 1))
        xfree[i] = v_sq
        for tt in range(ntt):
            w = [v_sq] if tt == 0 else []
            if fc == 0 and tt == 0:
                w += [S.psrel.get(slot), cst["rdy"]]
            v_pe = S.op("pe", "matmul", out=pss[:, tt * 512:(tt + 1) * 512], lhsT=cst["ones_bf"][:, :],
                        rhs=sq[i][:, tt * 512:(tt + 1) * 512], start=(fc == 0), stop=(fc == KC - 1),
                        waits=w, inc=(sqfree_s, 1) if tt == ntt - 1 else None)
        sqfree[i] = v_pe
    v1 = S.op("act", "activation", out=srt[:, :], in_=pss, func=AF.Sqrt, scale=1.0 / D,
              bias=cst["eps"][:, 0:1], waits=[v_pe], inc=(nrm, 1))
    S.psrel[slot] = v1
    v2 = S.op("dve", "reciprocal", out=rstd[:, :], in_=srt[:, :], waits=[v1], inc=(nrm, 1))
    S.norm_scratch = (xf[0], xf[1], srt)
    return rstd, v2, v_sq


def ffn_stage(P, hT, t0, T, w_up, w_dn, gdram):
    S = Stage(P, "ffn")
    cst = consts_basic(S)
    ps = S.psum("ps", [128, 4096], F32)
    xb = S.sb("xb", [128, KC, T], BF16)
    ag = S.sb("ag", [128, KC, T], BF16)
    gv = S.sb("gv", [128, KC], F32)
    gw = load_vec(S, gv, gdram)
    rstd, rrdy, xrdy = norm_pass(S, cst, hT, t0, T, gv, gw, xb, ps)
    NG = 4
    views = []
    for g in range(NG):
        views += [wview(w_up, 0, g * D + c * 128) for c in range(KC)]
        views += [wview(w_dn, g * D, c * 128) for c in range(KC)]
    ws = WStream(S, views)
    r1 = [S.sb(f"r1_{i}", [128, T], F32) for i in range(2)]
    st = [S.sb(f"st{i}", [128, T], F32) for i in range(2)]
    epi_s = S.sem("epi")
    acc_s = [S.sem(f"acc{i}") for i in range(2)]
    state = {"k": 0, "ag_done": None, "st_free": [None, None], "r1_free": [None, None], "dn_read": None}

    def up_epi(c, pv, pev):
        i = state["k"] % 2
        state["k"] += 1
        a = S.op("act", "activation", out=r1[i][:, :], in_=pv, func=AF.Relu,
                 waits=[pev, state["r1_free"][i]], inc=(epi_s, 1))
        b = S.op("dve", "tensor_tensor", out=r1[i][:, :], in0=r1[i][:, :], in1=rstd[:, :], op=ALU.mult,
                 waits=[a, rrdy], inc=(epi_s, 1))
        d = S.op("act", "activation", out=ag[:, c, :], in_=r1[i][:, :], func=AF.Square,
                 waits=[b, state["dn_read"] if c == 0 else None], inc=(epi_s, 1))
        state["r1_free"][i] = d
        state["ag_done"] = d
        return a

    def dn_epi(c, pv, pev):
        i = state["k"] % 2
        state["k"] += 1
        a = S.op("act", "activation", out=st[i][:, :], in_=pv, func=AF.Copy,
                 waits=[pev, state["st_free"][i]], inc=(epi_s, 1))
        v = S.op("pool", "dma_start", out=hT[c * 128:(c + 1) * 128, t0:t0 + T], in_=st[i][:, :],
                 accum_op=ALU.add, waits=[a], inc=(acc_s[i], 16))
        state["st_free"][i] = v
        S.pending.append(v)
        return a

    for g in range(NG):
        gemm_ws(S, ws, KC, xb, T, up_epi, ps, first_waits=[xrdy] if g == 0 else [])
        lastdn = gemm_ws(S, ws, KC, ag, T, dn_epi, ps, first_waits=[state["ag_done"]])
        state["dn_read"] = ws.rel[(ws.next_use - 1) % ws.NW]
    S.close()


def final_stage(P, hT, outT, t0, T, gdram):
    S = Stage(P, "fin")
    cst = consts_basic(S)
    ps = S.psum("ps", [128, 4096], F32)
    gv = S.sb("gv", [128, KC], F32)
    gw = load_vec(S, gv, gdram)
    rstd, rrdy, _ = norm_pass(S, cst, hT, t0, T, gv, gw, None, ps, want_sq_only=True)
    xg = [S.sb(f"xg{i}", [128, T], F32) for i in range(2)]
    ld = [S.sem(f"fld{i}") for i in range(2)]
    stq = [S.sem(f"fst{i}") for i in range(2)]
    fe = S.sem("fe")
    free = [None, None]
    for fc in range(KC):
        i = fc % 2
        v_ld = S.op("sp", "dma_start", out=xg[i][:, :], in_=hT[fc * 128:(fc + 1) * 128, t0:t0 + T],
                    waits=[free[i]], inc=(ld[i], 16))
        a = S.op("dve", "scalar_tensor_tensor", out=xg[i][:, :], in0=xg[i][:, :], scalar=gv[:, fc:fc + 1],
                 in1=rstd[:, :], op0=ALU.mult, op1=ALU.mult, waits=[v_ld, rrdy, gw], inc=(fe, 1))
        v = S.op("act", "dma_start", out=outT[fc * 128:(fc + 1) * 128, t0:t0 + T], in_=xg[i][:, :],
                 waits=[a], inc=(stq[i], 16))
        free[i] = v
        S.pending.append(v)
    S.close()


def gmlp_stage(P, hT, t0, w_in, w_out, gdram, lng_d, lnb_d, wsT_d, bs_d):
    T = 512
    NCH = 4
    S = Stage(P, "gm")
    cst = consts_basic(S)
    nc = S.nc
    ps = S.psum("ps", [128, 4096], F32)
    xb = S.sb("xb", [128, KC, T], BF16)
    uT = S.sb("uT", [128, KC, T], BF16)
    vg = S.sb("vg", [128, KC, T], BF16)
    vtk = S.sb("vtk", [128, NCH, D], BF16)
    gv = S.sb("gv", [128, KC], F32)
    lng = S.sb("lng", [128, KC], F32)
    lnb = S.sb("lnb", [128, KC], F32)
    load_vec(S, gv, gdram)
    load_vec(S, lng, lng_d)
    lw = load_vec(S, lnb, lnb_d)
    gw = lw
    ones_f = S.sb("ones_f", [128, 128], F32)
    ident_b = S.sb("ident_b", [128, 128], BF16)
    wT = S.sb("wT", [128, 8, 128], F32)
    wTb = S.sb("wTb", [128, 8, 128], BF16)
    bsrow = S.sb("bsrow", [1, 1024], F32)
    rwbs = S.sb("rwbs", [128, 2, 1024], F32)
    c2 = S.sb("c2", [128, KC, 128], F32)
    cs = S.sem("gmc")
    cl = S.sem("gmcl")
    w1 = S.op("sp", "dma_start", out=wT[:, :, :], in_=wsT_d.rearrange("h s t -> s h t"), inc=(cl, 16))
    cl2 = S.sem("gmcl2")
    w2 = S.op("sp", "dma_start", out=bsrow[:, :], in_=bs_d.rearrange("h t -> (h t)").unsqueeze(0), inc=(cl2, 16))
    S.op("pool", "memset", ones_f[:, :], 1.0)
    S.op("pool", "affine_select", out=ident_b[:, :], in_=cst["ones_bf"][:, :], pattern=[[1, 128]],
         compare_op=ALU.is_equal, fill=0.0, base=0, channel_multiplier=-1, waits=[cst["rdy"]])
    S.op("pool", "affine_select", out=wT[:, :, :], in_=wT[:, :, :], pattern=[[0, 8], [1, 128]],
         compare_op=ALU.is_ge, fill=0.0, base=0, channel_multiplier=-1, waits=[w1])
    k1 = S.op("pool", "tensor_copy", out=wTb[:, :, :], in_=wT[:, :, :], inc=(cs, 1))
    wTf = wT[:, :, :].rearrange("p h t -> p (h t)")
    for half in range(2):
        S.op("pe", "matmul", out=ps[:, 3072 + half * 512:3072 + (half + 1) * 512], lhsT=ones_f[:, :],
             rhs=wTf[:, half * 512:(half + 1) * 512], start=True, stop=True, waits=[k1] if half == 0 else [])
    for half in range(2):
        k2 = S.op("pe", "matmul", out=ps[:, 2048 + half * 512:2048 + (half + 1) * 512], lhsT=ones_f[0:1, :],
                  rhs=bsrow[0:1, half * 512:(half + 1) * 512], start=True, stop=True,
                  waits=[w2] if half == 0 else [], inc=(cs, 1) if half == 1 else None)
    S.op("act", "activation", out=rwbs[:, 0, :], in_=ps[:, 3072:4096], func=AF.Copy, waits=[k2])
    k3 = S.op("act", "activation", out=rwbs[:, 1, :], in_=ps[:, 2048:3072], func=AF.Copy, inc=(cs, 1))
    for s_ in (4, 5, 6, 7):
        S.psrel[s_] = k3
    for fc in range(KC):
        h = fc // 4
        k4 = S.op("dve", "scalar_tensor_tensor", out=c2[:, fc, :], in0=rwbs[:, 0, h * 128:(h + 1) * 128],
                  scalar=lnb[:, fc:fc + 1], in1=rwbs[:, 1, h * 128:(h + 1) * 128], op0=ALU.mult, op1=ALU.add,
                  waits=[k3, lw] if fc == 0 else [], inc=(cs, 1) if fc == KC - 1 else None)
    c2rdy = k4
    rstd, rrdy, xrdy = norm_pass(S, cst, hT, t0, T, gv, gw, xb, ps)
    views = [wview(w_in, 0, c * 128) for c in range(KC)]
    views += [wview(w_in, 0, D + c * 128) for c in range(KC)]
    views += [wview(w_out, 0, c * 128) for c in range(KC)]
    ws = WStream(S, views, NW=2)
    tmp = [S.sb(f"tmp{i}", [128, T], F32) for i in range(2)]
    epi_s = S.sem("epi")
    state = {"k": 0, "tfree": [None, None], "last": None}

    def gelu_epi(dst):
        def epi(c, pv, pev):
            i = state["k"] % 2
            state["k"] += 1
            a = S.op("dve", "tensor_tensor", out=tmp[i][:, :], in0=pv, in1=rstd[:, :], op=ALU.mult,
                     waits=[pev, rrdy, state["tfree"][i]], inc=(epi_s, 1))
            b = S.op("act", "activation", out=dst[:, c, :], in_=tmp[i][:, :], func=AF.Gelu,
                     waits=[a], inc=(epi_s, 1))
            state["tfree"][i] = b
            state["last"] = b
            return a
        return epi

    gemm_ws(S, ws, KC, xb, T, gelu_epi(uT), ps, first_waits=[xrdy])
    gemm_ws(S, ws, KC, xb, T, gelu_epi(vg), ps)
    vdone = state["last"]
    xb_read = ws.rel[(ws.next_use - 1) % ws.NW]
    sqv = [S.sb(f"sqv{i}", [128, T], BF16) for i in range(2)]
    lns = S.sem("lns")
    lnp = S.sem("lnp")
    slot1 = S.psnext % 8
    S.psnext += 1
    slot2 = S.psnext % 8
    S.psnext += 1
    p1 = ps[:, slot1 * T:(slot1 + 1) * T]
    p2 = ps[:, slot2 * T:(slot2 + 1) * T]
    sfree = [None, None]
    for fc in range(KC):
        i = fc % 2
        a = S.op("dve", "tensor_tensor", out=sqv[i][:, :], in0=vg[:, fc, :], in1=vg[:, fc, :], op=ALU.mult,
                 waits=[sfree[i], vdone if fc == 0 else None], inc=(lns, 1))
        S.op("pe", "matmul", out=p1, lhsT=cst["ones_bf"][:, :], rhs=vg[:, fc, :], start=(fc == 0),
             stop=(fc == KC - 1), waits=[vdone, S.psrel.get(slot1), S.psrel.get(slot2)] if fc == 0 else [])
        b = S.op("pe", "matmul", out=p2, lhsT=cst["ones_bf"][:, :], rhs=sqv[i][:, :], start=(fc == 0),
                 stop=(fc == KC - 1), waits=[a], inc=(lnp, 1))
        sfree[i] = b
    mean, var, lrs = S.norm_scratch
    a = S.op("act", "activation", out=mean[:, :], in_=p1, func=AF.Copy, scale=1.0 / D, waits=[b], inc=(lns, 1))
    a2 = S.op("dve", "tensor_tensor", out=var[:, :], in0=mean[:, :], in1=mean[:, :], op=ALU.mult, waits=[a], inc=(lns, 1))
    a3 = S.op("dve", "scalar_tensor_tensor", out=var[:, :], in0=p2, scalar=1.0 / D, in1=var[:, :],
              op0=ALU.mult, op1=ALU.subtract, waits=[a2], inc=(lns, 1))
    S.psrel[slot1] = a3
    S.psrel[slot2] = a3
    a4 = S.op("act", "activation", out=var[:, :], in_=var[:, :], func=AF.Sqrt, bias=cst["eps"][:, 0:1],
              waits=[a3], inc=(lns, 1))
    a5 = S.op("dve", "reciprocal", out=lrs[:, :], in_=var[:, :], waits=[a4], inc=(lns, 1))
    vn = xb
    trs = S.sem("trs")
    trp = S.sem("trp")
    mixp = S.sem("mixp")
    tcp = S.sem("tcp")
    gts = S.sem("gts")
    pst = S.psum
    psb = ps.bitcast(BF16) if hasattr(ps, "bitcast") else None
    for fc in range(KC):
        a = S.op("dve", "tensor_tensor", out=tmp[0][:, :], in0=vg[:, fc, :], in1=mean[:, :], op=ALU.subtract,
                 waits=[a5, xb_read, state["tfree"][0]] if fc == 0 else [state.get("vnw")], inc=(trs, 1))
        state["vnw"] = S.op("dve", "tensor_tensor", out=vn[:, fc, :], in0=tmp[0][:, :], in1=lrs[:, :], op=ALU.mult,
                            waits=[a], inc=(trs, 1))
    vn_done = state["vnw"]
    pstr = ps[:, :].bitcast(BF16)
    cnt = 0
    for n in range(NCH):
        for gq in range(8):
            slot = S.psnext % 8
            S.psnext += 1
            for q in range(4):
                fc = gq * 4 + q
                v = S.op("pe", "transpose", out=pstr[:, slot * 1024 + q * 128: slot * 1024 + (q + 1) * 128],
                         in_=vn[:, fc, n * 128:(n + 1) * 128], identity=ident_b[:, :],
                         waits=[vn_done, S.psrel.get(slot), k1] if q == 0 else [],
                         inc=(trp, 1) if q == 3 else None)
            eng = "act" if cnt % 2 == 0 else "dve"
            if eng == "act":
                e = S.op("act", "activation", out=vtk[:, n, gq * 512:(gq + 1) * 512],
                         in_=pstr[:, slot * 1024: slot * 1024 + 512], func=AF.Copy, waits=[v], inc=(tcp, 1))
            else:
                e = S.op("dve", "tensor_copy", out=vtk[:, n, gq * 512:(gq + 1) * 512],
                         in_=pstr[:, slot * 1024: slot * 1024 + 512], waits=[v], inc=(tcp, 1))
            S.psrel[slot] = e
            cnt += 1
            state["vtk_a" if eng == "act" else "vtk_d"] = e
    vtk_done = [state["vtk_a"], state["vtk_d"]]
    for n in range(NCH):
        for gq in range(8):
            slot = S.psnext % 8
            S.psnext += 1
            h = gq
            for q in range(4):
                fc = gq * 4 + q
                v = S.op("pe", "matmul", out=ps[:, slot * 512 + q * 128: slot * 512 + (q + 1) * 128],
                         lhsT=vtk[:, n, fc * 128:(fc + 1) * 128], rhs=wTb[:, h, :], start=True, stop=True,
                         waits=(vtk_done + [S.psrel.get(slot)]) if q == 0 else [],
                         inc=(mixp, 1) if q == 3 else None)
            for q in range(4):
                fc = gq * 4 + q
                a = S.op("dve", "scalar_tensor_tensor", out=tmp[1][:, q * 128:(q + 1) * 128],
                         in0=ps[:, slot * 512 + q * 128: slot * 512 + (q + 1) * 128], scalar=lng[:, fc:fc + 1],
                         in1=c2[:, fc, :], op0=ALU.mult, op1=ALU.add,
                         waits=[v, c2rdy, state["tfree"][1], state.get("gt")] if q == 0 else [],
                         inc=(gts, 1) if q == 3 else None)
            S.psrel[slot] = a
            uview = uT[:, gq * 4:(gq + 1) * 4, n * 128:(n + 1) * 128]
            state["gt"] = S.op("dve", "tensor_tensor", out=uview, in0=uview,
                               in1=tmp[1][:, :].rearrange("p (q t) -> p q t", q=4), op=ALU.mult, waits=[a], inc=(gts, 1))
    gated_done = state["gt"]
    st = tmp
    acc_s = [S.sem(f"acc{i}") for i in range(2)]
    st_free = [None, None]

    def out_epi(c, pv, pev):
        i = c % 2
        a = S.op("act", "activation", out=st[i][:, :], in_=pv, func=AF.Copy, waits=[pev, st_free[i]], inc=(epi_s, 1))
        v = S.op("pool", "dma_start", out=hT[c * 128:(c + 1) * 128, t0:t0 + T], in_=st[i][:, :],
                 accum_op=ALU.add, waits=[a], inc=(acc_s[i], 16))
        st_free[i] = v
        S.pending.append(v)
        return a

    gemm_ws(S, ws, KC, uT, T, out_epi, ps, first_waits=[gated_done])
    S.close()


def copy_stage(P, dst, src):
    S = Stage(P, "cp")
    cs = S.sem("cpy")
    for q in range(8):
        v = S.op("sp", "dma_start", out=dst[q * 512:(q + 1) * 512, :], in_=src[q * 512:(q + 1) * 512, :], inc=(cs, 16))
    S.pending.append(v)
    S.close()


def mlproj_stage(P, hT, t0, T, w_in, gdram, bgate_d, projT, gates_tok, kv_only=False):
    S = Stage(P, "mlp")
    cst = consts_basic(S)
    ps = S.psum("ps", [128, 4096], F32)
    xb = S.sb("xb", [128, KC, T], BF16)
    gv = S.sb("gv", [128, KC], F32)
    gw = load_vec(S, gv, gdram)
    ident_f = S.sb("ident_f", [128, 128], F32)
    ones_f = S.sb("ones_f", [128, 128], F32)
    S.op("pool", "memset", ones_f[:, :], 1.0)
    cs = S.sem("cset")
    idr = S.op("pool", "affine_select", out=ident_f[:, :], in_=ones_f[:, :], pattern=[[1, 128]],
               compare_op=ALU.is_equal, fill=0.0, base=0, channel_multiplier=-1, waits=[cst["rdy"]], inc=(cs, 1))
    rstd, rrdy, xrdy = norm_pass(S, cst, hT, t0, T, gv, gw, xb, ps)
    blocks = list(range(16, 64)) if kv_only else list(range(96))
    views = [wview(w_in, 0, c * 128) for c in blocks] + [wview(w_in, 0, 12304 - 128)]
    ws = WStream(S, views)
    st = [S.sb(f"st{i}", [128, T], BF16) for i in range(2)]
    tmp = [S.sb(f"tmp{i}", [128, T], F32) for i in range(2)]
    epi_s = S.sem("epi")
    sts = [S.sem(f"pst{i}") for i in range(2)]
    state = {"k": 0, "stfree": [None, None], "tfree": [None, None]}

    def epi(ci, pv, pev):
        c = blocks[ci]
        i = state["k"] % 2
        state["k"] += 1
        if c < 64:
            sc = 0.0625 if c < 16 else 1.0
            a = S.op("dve", "scalar_tensor_tensor", out=st[i][:, :], in0=pv, scalar=sc, in1=rstd[:, :],
                     op0=ALU.mult, op1=ALU.mult, waits=[pev, rrdy, state["stfree"][i]], inc=(epi_s, 1))
            rel = a
            fin = a
        else:
            a = S.op("dve", "tensor_tensor", out=tmp[i][:, :], in0=pv, in1=rstd[:, :], op=ALU.mult,
                     waits=[pev, rrdy, state["tfree"][i]], inc=(epi_s, 1))
            fin = S.op("act", "activation", out=st[i][:, :], in_=tmp[i][:, :], func=AF.Sigmoid,
                       waits=[a, state["stfree"][i]], inc=(epi_s, 1))
            state["tfree"][i] = fin
            rel = a
        v = S.op("sp", "dma_start", out=projT[c * 128:(c + 1) * 128, t0:t0 + T], in_=st[i][:, :],
                 waits=[fin], inc=(sts[i], 16))
        state["stfree"][i] = v
        S.pending.append(v)
        return rel

    gemm_ws(S, ws, len(blocks), xb, T, epi, ps, first_waits=[xrdy])
    import os
    if os.environ.get("SKIPG"):
        S.close()
        return
    nch = T // 128
    gpe = S.sem("gpe")
    gac = S.sem("gac")
    gdv = S.sem("gdv")
    gpo = S.sem("gpo")
    buf, wwait, j = ws.take()
    slot = S.psnext % (4096 // T)
    S.psnext += 1
    psg = ps[:, slot * T: slot * T + nch * 16]
    for n in range(nch):
        for kc in range(KC):
            v = S.op("pe", "matmul", out=psg[:, n * 16:(n + 1) * 16], lhsT=xb[:, kc, n * 128:(n + 1) * 128],
                     rhs=buf[:, kc, 112:128], start=(kc == 0), stop=(kc == KC - 1),
                     waits=[wwait, S.psrel.get(slot)] if (n == 0 and kc == 0) else [],
                     inc=(gpe, 1) if (n == nch - 1 and kc == KC - 1) else None)
    ws.release(j, v)
    gdone = v
    slot2 = S.psnext % (4096 // T)
    S.psnext += 1
    pst = ps[:, slot2 * T: slot2 * T + nch * 128].rearrange("p (n k) -> p n k", k=128)
    for n in range(nch):
        tv = S.op("pe", "transpose", out=pst[:, n, :], in_=rstd[:, n * 128:(n + 1) * 128], identity=ident_f[:, :],
                  waits=[rrdy, idr, S.psrel.get(slot2)] if n == 0 else [], inc=(gpe, 1) if n == nch - 1 else None)
    rt = S.sb("rt", [128, nch], F32)
    a0 = S.op("act", "activation", out=rt[:, :], in_=pst[:, :, 0], func=AF.Copy, waits=[tv], inc=(gac, 1))
    S.psrel[slot2] = a0
    bgrow = S.sb("bgrow", [1, 16], F32)
    bgl2 = S.sem("bgl2")
    bw = S.op("sp", "dma_start", out=bgrow[:, :], in_=bgate_d.unsqueeze(0), inc=(bgl2, 16))
    slot3 = S.psnext % (4096 // T)
    S.psnext += 1
    pbg = ps[:, slot3 * T: slot3 * T + 16]
    bm = S.op("pe", "matmul", out=pbg, lhsT=ones_f[0:1, :], rhs=bgrow[0:1, :], start=True, stop=True,
              waits=[bw, idr, S.psrel.get(slot3)], inc=(gpe, 1))
    bgbc = S.sb("bgbc", [128, 16], F32)
    b0 = S.op("act", "activation", out=bgbc[:, :], in_=pbg, func=AF.Copy, waits=[bm], inc=(gac, 1))
    S.psrel[slot3] = b0
    if os.environ.get("GSTOP") == "1":
        S.pending.append(b0); S.pending.append(a0)
        S.close()
        return
    gt = S.sb("gt", [128, nch, 16], F32)
    cap = S.sb("cap", [128, nch, 16], F32)
    lp = S.sb("lp", [128, nch, 16], F32)
    one1 = S.sb("one1", [128, 1], F32)
    o1 = S.op("pool", "memset", one1[:, :], 1.0, inc=(gpo, 1))
    a = None
    graw = S.sb("graw", [128, nch, 16], F32)
    ev = S.op("act", "activation", out=graw[:, :, :], in_=psg.rearrange("p (n k) -> p n k", k=16), func=AF.Copy,
              waits=[gdone, a0], inc=(gac, 1))
    S.psrel[slot] = ev
    a = None
    for n in range(nch):
        a = S.op("dve", "scalar_tensor_tensor", out=gt[:, n, :], in0=graw[:, n, :], scalar=rt[:, n:n + 1],
                 in1=bgbc[:, :], op0=ALU.mult, op1=ALU.add, waits=[ev, b0] if n == 0 else [a], inc=(gdv, 1))
    if os.environ.get("GSTOP") == "2":
        S.pending.append(a)
        S.close()
        return
    c1 = S.op("act", "activation", out=cap[:, :, :], in_=gt[:, :, :], func=AF.Tanh, scale=1.0 / 15.0, waits=[a], inc=(gac, 1))
    c2 = S.op("act", "activation", out=lp[:, :, :], in_=cap[:, :, :], func=AF.Exp, scale=-15.0, waits=[c1], inc=(gac, 1))
    c3 = S.op("act", "activation", out=lp[:, :, :], in_=lp[:, :, :], func=AF.Ln, bias=one1[:, 0:1], waits=[c2, o1], inc=(gac, 1))
    gtk = S.sb("gtk", [128, nch, 16], F32)
    d1 = S.op("dve", "tensor_scalar", out=gtk[:, :, 0:8], in0=cap[:, :, 0:8], scalar1=15.0, scalar2=None, op0=ALU.mult,
              waits=[c3], inc=(gdv, 1))
    d2 = S.op("dve", "tensor_scalar", out=gtk[:, :, 8:16], in0=lp[:, :, 8:16], scalar1=-1.0, scalar2=None, op0=ALU.mult,
              waits=[d1], inc=(gdv, 1))
    gl = S.sem("gld")
    v = S.op("sp", "dma_start", out=gates_tok[t0:t0 + T, :].rearrange("(n p) g -> p n g", p=128), in_=gtk[:, :, :],
             waits=[d2], inc=(gl, 16))
    S.pending.append(v)
    S.close()


def mlrec_stage(P, t0, T, full, projT, gates_tok, stC_in, stn_in, stC_out, stn_out,
                headg_d=None, w_out=None, hT=None):
    S = Stage(P, "mlr" if full else "mls")
    cst = consts_basic(S)
    nch = T // 128
    psA = S.psum("psA", [128, 512], F32)
    psNum = S.psum("psNum", [128, 512], F32)
    psQc = S.psum("psQc", [128, 512], F32)
    psC = S.psum("psC", [128, 1024], F32)
    psD = S.psum("psD", [128, 512], F32)
    psT = S.psum("psT", [128, 1024], F32)
    pT = psT[:, :].bitcast(BF16)
    ones_f = S.sb("ones_f", [128, 128], F32)
    ident_f = S.sb("ident_f", [128, 128], F32)
    ident_b = S.sb("ident_b", [128, 128], BF16)
    Utri = S.sb("Utri", [128, 128], F32)
    negm = S.sb("negm", [128, 128], F32)
    zer = S.sb("zer", [128, 128], F32)
    cs = S.sem("cset")
    S.op("pool", "memset", ones_f[:, :], 1.0)
    S.op("pool", "memset", zer[:, :], 0.0)
    S.op("pool", "affine_select", out=ident_f[:, :], in_=ones_f[:, :], pattern=[[1, 128]],
         compare_op=ALU.is_equal, fill=0.0, base=0, channel_multiplier=-1, waits=[cst["rdy"]])
    S.op("pool", "affine_select", out=ident_b[:, :], in_=cst["ones_bf"][:, :], pattern=[[1, 128]],
         compare_op=ALU.is_equal, fill=0.0, base=0, channel_multiplier=-1)
    S.op("pool", "affine_select", out=Utri[:, :], in_=ones_f[:, :], pattern=[[1, 128]],
         compare_op=ALU.is_ge, fill=0.0, base=0, channel_multiplier=-1)
    crdy = S.op("pool", "affine_select", out=negm[:, :], in_=zer[:, :], pattern=[[1, 128]],
                compare_op=ALU.is_ge, fill=-30000.0, base=0, channel_multiplier=-1, inc=(cs, 1))
    gtk = S.sb("gtk", [128, nch, 16], F32)
    lf = S.sb("lf", [128, nch * 8], F32)
    bcum = S.sb("bcum", [128, nch * 8], F32)
    Gbc = S.sb("Gbc", [128, nch * 8], F32)
    bias_s = S.sb("bias_s", [128, nch * 8], F32)
    ea = S.sb("ea", [128, nch * 8], F32)
    etok = S.sb("etok", [128, nch * 8], F32)
    eg = S.sb("eg", [128, nch * 8], F32)
    gl = S.sem("gld")
    gp = S.sem("gpre")
    v = S.op("sp", "dma_start", out=gtk[:, :, :], in_=gates_tok[t0:t0 + T, :].rearrange("(n p) g -> p n g", p=128),
             inc=(gl, 16))
    lf3 = lf[:, :].rearrange("p (n h) -> p n h", h=8)
    a = S.op("dve", "tensor_copy", out=lf3, in_=gtk[:, :, 8:16], waits=[v], inc=(gp, 1))
    NH = nch * 8
    S.op("pe", "matmul", out=psA[:, 256:256 + NH], lhsT=Utri[:, :], rhs=lf[:, :], start=True, stop=True, waits=[a, crdy])
    b = S.op("pe", "matmul", out=psA[:, 320:320 + NH], lhsT=ones_f[:, :], rhs=lf[:, :], start=True, stop=True, inc=(gp, 1))
    c1 = S.op("act", "activation", out=bcum[:, :], in_=psA[:, 256:256 + NH], func=AF.Copy, waits=[b], inc=(gp, 1))
    c2 = S.op("act", "activation", out=Gbc[:, :], in_=psA[:, 320:320 + NH], func=AF.Copy, waits=[c1], inc=(gp, 1))
    d1 = S.op("dve", "tensor_tensor", out=bias_s[:, :].rearrange("p (n h) -> p n h", h=8), in0=gtk[:, :, 0:8],
              in1=bcum[:, :].rearrange("p (n h) -> p n h", h=8), op=ALU.subtract, waits=[c2], inc=(gp, 1))
    d2 = S.op("dve", "tensor_tensor", out=ea[:, :], in0=bias_s[:, :], in1=Gbc[:, :], op=ALU.add, waits=[d1], inc=(gp, 1))
    e1 = S.op("act", "activation", out=ea[:, :], in_=ea[:, :], func=AF.Exp, waits=[d2], inc=(gp, 1))
    e2 = S.op("act", "activation", out=etok[:, :], in_=bcum[:, :], func=AF.Exp, waits=[e1], inc=(gp, 1))
    grdy = S.op("act", "activation", out=eg[:, :], in_=Gbc[:, :], func=AF.Exp, waits=[e2], inc=(gp, 1))
    RSTOP = int(os.environ.get("RSTOP", "0"))
    if RSTOP == 1:
        S.pending.append(grdy)
        S.close()
        return
    ncols = 12 if full else 6
    hb = S.sb("hb", [128, ncols, T], BF16)
    Cst = S.sb("Cst", [128, 1024], F32)
    Cb = S.sb("Cb", [128, 1024], BF16)
    nst = S.sb("nst", [128, 2], F32)
    nb = S.sb("nb", [128, 2], BF16)
    kw = [S.sb(f"kw{i}", [128, 256], BF16) for i in range(2)]
    vtok = [S.sb(f"vtok{i}", [128, 512], BF16) for i in range(2)]
    if full:
        hsT = S.sb("hsT", [128, KC, T], BF16)
        otok = [S.sb(f"otok{i}", [128, 512], BF16) for i in range(2)]
        diagb = [S.sb(f"diagb{i}", [128, 128], F32) for i in range(2)]
        DT = [S.sb(f"DT{i}", [128, 128], F32) for i in range(2)]
        PT = [S.sb(f"PT{i}", [128, 128], BF16) for i in range(2)]
        intra = [S.sb(f"intra{i}", [128, 512], F32) for i in range(2)]
        num = [S.sb(f"num{i}", [128, 512], F32) for i in range(2)]
        junk = S.sb("junk", [128, 512], F32)
        go = [S.sb(f"go{i}", [128, 512], F32) for i in range(2)]
        hs = [S.sb(f"hs{i}", [128, 512], BF16) for i in range(2)]
        sm = [S.sb(f"sm{i}", [128, 8], F32) for i in range(2)]
        hg = S.sb("hg", [128, D], F32)
        hgl = S.sem("hgl")
        hgw = S.op("sp", "dma_start", out=hg[:, :], in_=headg_d[0, :].partition_broadcast(128), inc=(hgl, 16))
    hl = S.sem("hld")
    cl = S.sem("cld")
    cst_s = S.sem("cstore")
    pe_s = S.sem("rpe")
    ac_s = S.sem("ract")
    dv_s = S.sem("rdve")
    po_s = S.sem("rpool")
    prev = {"pe_step": None, "cstore": None, "hb_read": None, "psT_k": None, "psT_v": None, "psT_o": None,
            "psT_h": None, "psA_b": None, "psA_s": None, "psNum": None, "psQc": None, "psC": None, "psD": None,
            "cupd": None, "cb": None, "hsTcopy": None}
    step = 0
    for h in range(int(os.environ.get('RH', 8))):
        lw = []
        if full:
            rows = [(h * 256, 2, 0), (2048 + h * 256, 2, 2), (4096 + h * 512, 4, 4), (8192 + h * 512, 4, 8)]
        else:
            rows = [(2048 + h * 256, 2, 0), (4096 + h * 512, 4, 2)]
        for (r0, nck, dst) in rows:
            lw.append(S.op("sp", "dma_start", out=hb[:, dst:dst + nck, :],
                           in_=projT[r0:r0 + nck * 128, t0:t0 + T].rearrange("(j p) t -> p j t", p=128),
                           waits=[prev["hb_read"]], inc=(hl, 16)))
        hbw = [lw[-1]]
        cw1 = S.op("sp", "dma_start", out=Cst[:, :], in_=stC_in[:, h * 1024:(h + 1) * 1024],
                   waits=[prev["cstore"]], inc=(cl, 16))
        cw = S.op("sp", "dma_start", out=nst[:, :], in_=stn_in[:, h * 2:(h + 1) * 2], inc=(cl, 16))
        if full:
            prev["cb"] = S.op("act", "activation", out=Cb[:, :], in_=Cst[:, :], func=AF.Copy, waits=[cw, prev["pe_step"]], inc=(ac_s, 1))
            prev["cb"] = S.op("act", "activation", out=nb[:, :], in_=nst[:, :], func=AF.Copy, inc=(ac_s, 1))
        if RSTOP == 2:
            S.pending += hbw + [cw]
            S.close()
            return
        if full:
            qo, ko, vo, oo = 0, 2, 4, 8
        else:
            ko, vo = 0, 2
        for n in range(nch):
            par = step % 2
            ch = n * 8 + h
            tk = slice(n * 128, (n + 1) * 128)
            first = (n == 0)
            for j in range(2):
                S.op("pe", "transpose", out=pT[:, j * 128:(j + 1) * 128], in_=hb[:, ko + j, tk], identity=ident_b[:, :],
                     waits=(hbw + [prev["psT_k"], prev["psT_v"], crdy]) if j == 0 else [])
            for j in range(4):
                tv = S.op("pe", "transpose", out=pT[:, 256 + j * 128:256 + (j + 1) * 128], in_=hb[:, vo + j, tk],
                          identity=ident_b[:, :], waits=[prev["psT_v"]] if j == 0 else [],
                          inc=(pe_s, 1) if j == 3 else None)
            if full:
                for j in range(4):
                    to = S.op("pe", "transpose", out=pT[:, 1024 + j * 128:1024 + (j + 1) * 128], in_=hb[:, oo + j, tk],
                              identity=ident_b[:, :], waits=[prev["psT_o"], prev["psT_h"]] if j == 0 else [],
                              inc=(pe_s, 1) if j == 3 else None)
            kwv = S.op("dve", "tensor_scalar", out=kw[par][:, :], in0=pT[:, 0:256], scalar1=ea[:, ch:ch + 1], scalar2=None,
                       op0=ALU.mult, waits=[tv, grdy, prev["pe_step"]], inc=(dv_s, 1))
            prev["psT_k"] = kwv
            vtv = S.op("act", "activation", out=vtok[par][:, :], in_=pT[:, 256:768], func=AF.Copy,
                       waits=[tv, prev["pe_step"], kwv], inc=(ac_s, 1))
            prev["psT_v"] = vtv
            if RSTOP == 31:
                S.ops["act"].pop()
                vtv.__class__
                S.pending += [kwv, cw]
                S.close()
                return
            if RSTOP == 32:
                S.ops["act"].pop()
                S.ops["dve"].pop()
                S.pending += [tv, cw]
                S.close()
                return
            if RSTOP == 3:
                S.pending += [kwv, vtv, cw]
                S.close()
                return
            if full:
                otv = S.op("act", "activation", out=otok[par][:, :], in_=pT[:, 1024:1536], func=AF.Copy,
                           waits=[to], inc=(ac_s, 1))
                prev["psT_o"] = otv
                gov = S.op("pool", "tensor_tensor", out=go[par][:, :], in0=otok[par][:, :], in1=hg[:, h * 512:(h + 1) * 512],
                           op=ALU.mult, waits=[otv, hgw], inc=(po_s, 1))
                dgv = S.op("act", "activation", out=diagb[par][:, :], in_=ident_f[:, :], func=AF.Copy,
                           scale=bcum[:, ch:ch + 1], waits=[grdy], inc=(ac_s, 1))
                S.op("pe", "matmul", out=psA[:, 0:128], lhsT=ones_f[:, :], rhs=diagb[par][:, :], start=True, stop=False,
                     waits=[dgv, prev["psA_b"]])
                pbv = S.op("pe", "matmul", out=psA[:, 0:128], lhsT=ident_f[:, :], rhs=negm[:, :], start=False, stop=True,
                           inc=(pe_s, 1))
                for j in range(2):
                    psv = S.op("pe", "matmul", out=psA[:, 128:256], lhsT=hb[:, ko + j, tk], rhs=hb[:, qo + j, tk],
                               start=(j == 0), stop=(j == 1), waits=[prev["psA_s"]] if j == 0 else [],
                               inc=(pe_s, 1) if j == 1 else None)
                dtv = S.op("act", "activation", out=DT[par][:, :], in_=psA[:, 0:128], func=AF.Exp,
                           bias=bias_s[:, ch:ch + 1], waits=[pbv], inc=(ac_s, 1))
                prev["psA_b"] = dtv
                ptv = S.op("dve", "tensor_tensor", out=PT[par][:, :], in0=psA[:, 128:256], in1=DT[par][:, :], op=ALU.mult,
                           waits=[psv, dtv], inc=(dv_s, 1))
                prev["psA_s"] = ptv
                nmv = S.op("pe", "matmul", out=psNum[:, :], lhsT=PT[par][:, :], rhs=vtok[par][:, :], start=True, stop=True,
                           waits=[ptv, vtv, prev["psNum"]], inc=(pe_s, 1))
                S.op("pe", "matmul", out=psD[:, 0:1], lhsT=PT[par][:, :], rhs=cst["ones_bf"][:, 0:1], start=True, stop=True,
                     waits=[prev["psD"]])
                for j in range(2):
                    S.op("pe", "matmul", out=psQc[:, :], lhsT=hb[:, qo + j, tk], rhs=Cb[:, j * 512:(j + 1) * 512],
                         start=(j == 0), stop=(j == 1), waits=[prev["cb"], prev["psQc"]] if j == 0 else [])
                for j in range(2):
                    qcv = S.op("pe", "matmul", out=psD[:, 1:2], lhsT=hb[:, qo + j, tk], rhs=nb[:, j:j + 1],
                               start=(j == 0), stop=(j == 1), inc=(pe_s, 1) if j == 1 else None)
            for j in range(2):
                S.op("pe", "matmul", out=psC[:, j * 512:(j + 1) * 512], lhsT=kw[par][:, j * 128:(j + 1) * 128],
                     rhs=vtok[par][:, :], start=True, stop=True, waits=[kwv, vtv, prev["psC"]] if j == 0 else [])
            for j in range(2):
                pcv = S.op("pe", "matmul", out=psD[:, 2 + j:3 + j], lhsT=kw[par][:, j * 128:(j + 1) * 128],
                           rhs=cst["ones_bf"][:, 0:1], start=True, stop=True,
                           waits=[prev["psD"]] if (j == 0 and not full) else [],
                           inc=(pe_s, 1) if j == 1 else None)
            pe_last = pcv
            if RSTOP == 4:
                S.pending += [pcv, cw]
                S.close()
                return
            if full:
                inv = S.op("act", "activation", out=intra[par][:, :], in_=psNum[:, :], func=AF.Copy, waits=[nmv], inc=(ac_s, 1))
                prev["psNum"] = inv
                smx = sm[par]
                d0 = S.op("dve", "tensor_copy", out=smx[:, 0:2], in_=psD[:, 0:2], waits=[qcv], inc=(dv_s, 1))
                nuv = S.op("dve", "scalar_tensor_tensor", out=num[par][:, :], in0=psQc[:, :], scalar=etok[:, ch:ch + 1],
                           in1=intra[par][:, :], op0=ALU.mult, op1=ALU.add, waits=[inv, qcv], inc=(dv_s, 1))
                prev["psQc"] = nuv
                x1 = S.op("dve", "scalar_tensor_tensor", out=smx[:, 2:3], in0=smx[:, 1:2], scalar=etok[:, ch:ch + 1],
                          in1=smx[:, 0:1], op0=ALU.mult, op1=ALU.add, waits=[d0], inc=(dv_s, 1))
                x2a = S.op("dve", "scalar_tensor_tensor", out=smx[:, 3:4], in0=smx[:, 2:3], scalar=-1.0,
                           in1=smx[:, 2:3], op0=ALU.mult, op1=ALU.max, waits=[x1], inc=(dv_s, 1))
                x2 = S.op("dve", "tensor_scalar_max", out=smx[:, 3:4], in0=smx[:, 3:4], scalar1=1.0,
                          waits=[x2a], inc=(dv_s, 1))
                x3 = S.op("dve", "reciprocal", out=smx[:, 4:5], in_=smx[:, 3:4], waits=[x2], inc=(dv_s, 1))
                s1 = S.op("act", "activation", out=junk[:, :], in_=num[par][:, :], func=AF.Square, accum_out=smx[:, 5:6],
                          waits=[nuv], inc=(ac_s, 1))
                y1 = S.op("dve", "tensor_tensor", out=smx[:, 6:7], in0=smx[:, 4:5], in1=smx[:, 4:5], op=ALU.mult,
                          waits=[x3], inc=(dv_s, 1))
                y2 = S.op("dve", "scalar_tensor_tensor", out=smx[:, 6:7], in0=smx[:, 5:6], scalar=1.0 / 512.0,
                          in1=smx[:, 6:7], op0=ALU.mult, op1=ALU.mult, waits=[y1, s1], inc=(dv_s, 1))
                y3 = S.op("act", "activation", out=smx[:, 6:7], in_=smx[:, 6:7], func=AF.Sqrt, bias=cst["eps"][:, 0:1],
                          waits=[y2], inc=(ac_s, 1))
                y4 = S.op("dve", "reciprocal", out=smx[:, 7:8], in_=smx[:, 6:7], waits=[y3], inc=(dv_s, 1))
                y5 = S.op("dve", "tensor_tensor", out=smx[:, 7:8], in0=smx[:, 7:8], in1=smx[:, 4:5], op=ALU.mult,
                          waits=[y4], inc=(dv_s, 1))
                hsv = S.op("dve", "scalar_tensor_tensor", out=hs[par][:, :], in0=num[par][:, :], scalar=smx[:, 7:8],
                           in1=go[par][:, :], op0=ALU.mult, op1=ALU.mult, waits=[y5, gov], inc=(dv_s, 1))
                for j in range(4):
                    thv = S.op("pe", "transpose", out=pT[:, 1536 + j * 128:1536 + (j + 1) * 128],
                               in_=hs[par][:, j * 128:(j + 1) * 128], identity=ident_b[:, :],
                               waits=[hsv, prev["psT_h"]] if j == 0 else [], inc=(pe_s, 1) if j == 3 else None)
                pe_last = thv
                hcv = S.op("act", "activation", out=hsT[:, h * 4:(h + 1) * 4, tk],
                           in_=pT[:, 1536:2048].rearrange("p (j t) -> p j t", j=4), func=AF.Copy, waits=[thv], inc=(ac_s, 1))
                prev["psT_h"] = hcv
                prev["hsTcopy"] = hcv
            cu = S.op("dve", "scalar_tensor_tensor", out=Cst[:, :], in0=Cst[:, :], scalar=eg[:, ch:ch + 1], in1=psC[:, :],
                      op0=ALU.mult, op1=ALU.add, waits=[pcv, cw if first else None, prev["cb"] if full else None],
                      inc=(dv_s, 1))
            prev["psC"] = cu
            nu = S.op("dve", "scalar_tensor_tensor", out=nst[:, :], in0=nst[:, :], scalar=eg[:, ch:ch + 1], in1=psD[:, 2:4],
                      op0=ALU.mult, op1=ALU.add, waits=[cu], inc=(dv_s, 1))
            prev["psD"] = nu
            prev["cupd"] = nu
            if RSTOP == 5:
                S.pending += [nu]
                S.close()
                return
            if full:
                S.op("act", "activation", out=Cb[:, :], in_=Cst[:, :], func=AF.Copy, waits=[nu], inc=(ac_s, 1))
                prev["cb"] = S.op("act", "activation", out=nb[:, :], in_=nst[:, :], func=AF.Copy, inc=(ac_s, 1))
            prev["pe_step"] = pe_last
            step += 1
        prev["hb_read"] = prev["pe_step"]
        S.op("sp", "dma_start", out=stC_out[:, h * 1024:(h + 1) * 1024], in_=Cst[:, :], waits=[prev["cupd"]], inc=(cst_s, 16))
        prev["cstore"] = S.op("sp", "dma_start", out=stn_out[:, h * 2:(h + 1) * 2], in_=nst[:, :], inc=(cst_s, 16))
    S.pending.append(prev["cstore"])
    if full:
        views = [wview(w_out, 0, c * 128) for c in range(KC)]
        ws = WStream(S, views)
        st = [S.sb(f"st{i}", [128, T], F32) for i in range(2)]
        acc_s = [S.sem(f"acc{i}") for i in range(2)]
        epi_s = S.sem("epi")
        st_free = [None, None]
        pedone = S.sem("pedone")
        banks = [psC, psT]
        brel = [prev["psC"], prev["hsTcopy"]]
        for c in range(KC):
            buf, wwait, j = ws.take()
            bi = c % 2
            pb = banks[bi]
            for kc in range(KC):
                for tt in range(T // 512):
                    lastmm = (kc == KC - 1 and tt == T // 512 - 1)
                    v = S.op("pe", "matmul", out=pb[:, tt * 512:(tt + 1) * 512], lhsT=buf[:, kc, :],
                             rhs=hsT[:, kc, tt * 512:(tt + 1) * 512], start=(kc == 0), stop=(kc == KC - 1),
                             waits=[wwait, brel[bi], prev["hsTcopy"]] if (kc == 0 and tt == 0) else (),
                             inc=(pedone, 1) if lastmm else None)
            ws.release(j, v)
            i = c % 2
            a = S.op("act", "activation", out=st[i][:, :], in_=pb[:, 0:T], func=AF.Copy, waits=[v, st_free[i]], inc=(epi_s, 1))
            brel[bi] = a
            dv = S.op("pool", "dma_start", out=hT[c * 128:(c + 1) * 128, t0:t0 + T], in_=st[i][:, :],
                      accum_op=ALU.add, waits=[a], inc=(acc_s[i], 16))
            st_free[i] = dv
            S.pending.append(dv)
    S.close()


def select_stage(P, gath, sel_d, stC, stn):
    S = Stage(P, "sel")
    W = 8192 + 16
    acc = S.sb("acc", [128, W], F32)
    gb = [S.sb(f"gb{i}", [128, W], F32) for i in range(2)]
    sel = S.sb("sel", [128, 8], F32)
    ls = S.sem("sld")
    gs = [S.sem(f"sg{i}") for i in range(2)]
    ds = S.sem("sdv")
    ss = S.sem("sst")
    sw = S.op("sp", "dma_start", out=sel[:, :], in_=sel_d[:, :], inc=(ls, 16))
    free = [None, None]
    a = None
    for r in range(8):
        i = r % 2
        v = S.op("sp", "dma_start", out=gb[i][:, :], in_=gath[r, :, :], waits=[free[i]], inc=(gs[i], 16))
        if r == 0:
            a = S.op("dve", "tensor_scalar", out=acc[:, :], in0=gb[i][:, :], scalar1=sel[:, 0:1], scalar2=None,
                     op0=ALU.mult, waits=[v, sw], inc=(ds, 1))
        else:
            a = S.op("dve", "scalar_tensor_tensor", out=acc[:, :], in0=gb[i][:, :], scalar=sel[:, r:r + 1], in1=acc[:, :],
                     op0=ALU.mult, op1=ALU.add, waits=[v, a], inc=(ds, 1))
        free[i] = a
    S.op("sp", "dma_start", out=stC[:, :], in_=acc[:, 0:8192], waits=[a], inc=(ss, 16))
    v = S.op("sp", "dma_start", out=stn[:, :], in_=acc[:, 8192:W], inc=(ss, 16))
    S.pending.append(v)
    S.close()


def scale_stage(P, src, sel_d, dst):
    S = Stage(P, "scl")
    buf = S.sb("buf", [128, SW], F32)
    sel = S.sb("sel", [128, 1], F32)
    l1 = S.sem("scl1")
    l2 = S.sem("scl2")
    dv = S.sem("scdv")
    st = S.sem("scst")
    a = S.op("sp", "dma_start", out=sel[:, :], in_=sel_d[:, :], inc=(l1, 16))
    b = S.op("sp", "dma_start", out=buf[:, :], in_=src[:, :], inc=(l2, 16))
    c = S.op("dve", "tensor_scalar", out=buf[:, :], in0=buf[:, :], scalar1=sel[:, 0:1], scalar2=None, op0=ALU.mult,
             waits=[a, b], inc=(dv, 1))
    v = S.op("sp", "dma_start", out=dst[:, :], in_=buf[:, :], waits=[c], inc=(st, 16))
    S.pending.append(v)
    S.close()


SW = 8192 + 16


def build(mode):
    P = Prog()
    EI, EO, IN = "ExternalInput", "ExternalOutput", "Internal"
    d = P.dt
    p1 = mode in ("fused", "p1")
    p2 = mode in ("fused", "p2")
    nm = d("norm_mix", [2, D], F32, EI)
    nf = d("norm_ffn", [2, D], F32, EI)
    if p1:
        xT = d("xT", [D, NTOK], F32, EI)
        gm_w_in = d("gm_w_in", [D, 2 * D], F32, EI)
        gm_w_out = d("gm_w_out", [D, D], F32, EI)
        lng = d("gm_ln_g", [D], F32, EI)
        lnb = d("gm_ln_b", [D], F32, EI)
        wsT = d("gm_w_sT", [8, 128, 128], F32, EI)
        bs = d("gm_b_s", [8, 128], F32, EI)
        up0 = d("ffn_up0", [D, 4 * D], F32, EI)
        dn0 = d("ffn_dn0", [4 * D, D], F32, EI)
        ml_w_in = d("ml_w_in", [D, 12304], F32, EI)
        bgate = d("ml_b_gate", [16], F32, EI)
        zst = d("zst", [128, SW], F32, EI)
    if p2:
        headg = d("ml_head_g", [1, D], F32, EI)
        ml_w_out = d("ml_w_out", [D, D], F32, EI)
        up1 = d("ffn_up1", [D, 4 * D], F32, EI)
        dn1 = d("ffn_dn1", [4 * D, D], F32, EI)
        nfin = d("norm_final", [D], F32, EI)
        outT = d("outT", [D, NTOK], F32, EO)
    if mode == "fused":
        hT = d("hT", [D, NTOK], F32, IN)
        projT = d("projT", [12288, NTOK], BF16, IN)
        gtok = d("gtok", [NTOK, 16], F32, IN)
        stA = d("stA", [128, SW], F32, IN)
        stB = d("stB", [128, SW], F32, IN)
        stI = d("stI", [128, SW], F32, IN)
        xpT = d("xpT", [D, NTOK], F32, EI)
        hpT = d("hpT", [D, NTOK], F32, IN)
        projpT = d("projpT", [12288, NTOK], BF16, IN)
        gtokp = d("gtokp", [NTOK, 16], F32, IN)
        stAp = d("stAp", [128, SW], F32, IN)
        stBp = d("stBp", [128, SW], F32, IN)
        sel = d("sel", [128, 1], F32, EI)
    elif mode == "p1":
        hT = d("hT", [D, NTOK], F32, EO)
        projT = d("projT", [12288, NTOK], BF16, EO)
        gtok = d("gtok", [NTOK, 16], F32, EO)
        stA = d("stA", [128, SW], F32, IN)
        stB = d("stB", [128, SW], F32, EO)
    else:
        hin = d("hin", [D, NTOK], F32, EI)
        hT = d("hT", [D, NTOK], F32, IN)
        projT = d("projT", [12288, NTOK], BF16, EI)
        gtok = d("gtok", [NTOK, 16], F32, EI)
        stI = d("stI", [128, SW], F32, EI)
        stA = d("stA", [128, SW], F32, IN)
        stB = d("stB", [128, SW], F32, IN)

    def CN(t):
        return t[:, 0:8192], t[:, 8192:SW]

    if p1:
        copy_stage(P, hT, xT)
        for t in range(4):
            gmlp_stage(P, hT, t * 512, gm_w_in, gm_w_out, nm[0, :], lng, lnb, wsT, bs)
        for t in range(2):
            ffn_stage(P, hT, t * 1024, 1024, up0, dn0, nf[0, :])
        for t in range(2):
            mlproj_stage(P, hT, t * 1024, 1024, ml_w_in, nm[1, :], bgate, projT, gtok)
    if mode == "p1":
        mlrec_stage(P, 0, 1024, False, projT, gtok, *CN(zst), *CN(stA))
        mlrec_stage(P, 1024, 1024, False, projT, gtok, *CN(stA), *CN(stB))
    if mode == "fused":
        copy_stage(P, hpT, xpT)
        for t in range(4):
            gmlp_stage(P, hpT, t * 512, gm_w_in, gm_w_out, nm[0, :], lng, lnb, wsT, bs)
        for t in range(2):
            ffn_stage(P, hpT, t * 1024, 1024, up0, dn0, nf[0, :])
        for t in range(2):
            mlproj_stage(P, hpT, t * 1024, 1024, ml_w_in, nm[1, :], bgate, projpT, gtokp, kv_only=True)
        mlrec_stage(P, 0, 1024, False, projpT, gtokp, *CN(zst), *CN(stAp))
        mlrec_stage(P, 1024, 1024, False, projpT, gtokp, *CN(stAp), *CN(stBp))
        scale_stage(P, stBp, sel, stI)
    if mode == "p2":
        copy_stage(P, hT, hin)
    if p2:
        mlrec_stage(P, 0, 1024, True, projT, gtok, *CN(stI), *CN(stA), headg_d=headg, w_out=ml_w_out, hT=hT)
        mlrec_stage(P, 1024, 1024, True, projT, gtok, *CN(stA), *CN(stB), headg_d=headg, w_out=ml_w_out, hT=hT)
        for t in range(2):
            ffn_stage(P, hT, t * 1024, 1024, up1, dn1, nf[1, :])
        for t in range(2):
            final_stage(P, hT, outT, t * 1024, 1024, nfin)
    return P


def _c(a):
    return np.ascontiguousarray(a, dtype=np.float32)


MODE = "fused"


def kernel(x, norm_mix, norm_ffn, gm_w_in, gm_ln_g, gm_ln_b, gm_w_s, gm_b_s, gm_w_out,
           ml_w_in, ml_b_gate, ml_head_g, ml_w_out, ffn_w_up, ffn_w_down, norm_final):
    x = np.asarray(x, dtype=np.float32)
    xs = x.reshape(NCORES, NTOK, D)
    xT = [_c(xs[c].T) for c in range(NCORES)]
    com = {"norm_mix": _c(norm_mix), "norm_ffn": _c(norm_ffn)}
    in1 = {"gm_w_in": _c(gm_w_in[0]), "gm_w_out": _c(gm_w_out[0]), "gm_ln_g": _c(gm_ln_g[0]),
           "gm_ln_b": _c(gm_ln_b[0]), "gm_w_sT": _c(np.transpose(np.asarray(gm_w_s[0]), (0, 2, 1))),
           "gm_b_s": _c(gm_b_s[0]), "ffn_up0": _c(ffn_w_up[0]), "ffn_dn0": _c(ffn_w_down[0]),
           "ml_w_in": _c(ml_w_in[0]), "ml_b_gate": _c(ml_b_gate[0]),
           "zst": np.zeros((128, SW), np.float32)}
    in2 = {"ml_head_g": _c(np.asarray(ml_head_g[0]).reshape(1, D)), "ml_w_out": _c(ml_w_out[0]),
           "ffn_up1": _c(ffn_w_up[1]), "ffn_dn1": _c(ffn_w_down[1]), "norm_final": _c(norm_final)}
    if MODE == "fused":
        P = build("fused")
        maps = []
        for c in range(NCORES):
            sel = np.full((128, 1), float(c % 2), np.float32)
            m = dict(com); m.update(in1); m.update(in2); m["xT"] = xT[c]; m["sel"] = sel
            m["xpT"] = xT[c - (c % 2)]
            maps.append(m)
        res = run_bass_kernel_spmd(P.nc, maps, core_ids=list(range(NCORES)))
        outs = [r["outT"] for r in res.results]
    else:
        P1 = build("p1")
        maps = []
        for c in range(NCORES):
            m = dict(com); m.update(in1); m["xT"] = xT[c]
            maps.append(m)
        r1 = run_bass_kernel_spmd(P1.nc, maps, core_ids=list(range(NCORES))).results
        P2 = build("p2")
        maps = []
        for c in range(NCORES):
            m = dict(com); m.update(in2)
            m["hin"] = r1[c]["hT"]; m["projT"] = r1[c]["projT"]; m["gtok"] = r1[c]["gtok"]
            m["stI"] = r1[c - 1]["stB"] if c % 2 == 1 else np.zeros((128, SW), np.float32)
            maps.append(m)
        r2 = run_bass_kernel_spmd(P2.nc, maps, core_ids=list(range(NCORES))).results
        outs = [r["outT"] for r in r2]
    out = np.stack([np.asarray(o, dtype=np.float32).T for o in outs], axis=0)
    return np.ascontiguousarray(out.reshape(4, 4096, D))
```

```python
import os
import numpy as np
from contextlib import ExitStack
import concourse.bass as bass
import concourse.mybir as mybir
from concourse.bass_utils import run_bass_kernel_spmd

F32 = mybir.dt.float32
BF16 = mybir.dt.bfloat16
AF = mybir.ActivationFunctionType
ALU = mybir.AluOpType

D = 4096
KC = 32
NTOK = 2048
EPS = 1e-6
ENGS = ("pe", "act", "dve", "pool", "sp")
WC_BLOCKS = {"gmi": 64, "gmo": 32, "up0": 128, "dn0": 128, "up1": 128, "dn1": 128, "mli": 97, "mlo": 32}
NCORES = 8


class Sem:
    def __init__(self, h, name):
        self.h = h
        self.n = 0
        self.name = name


class Prog:
    def __init__(self, num_devices=None):
        self.nc = bass.Bass("TRN2", target_bir_lowering=False, num_devices=num_devices)
        self.es = ExitStack()
        self.sems = {}
        self.stage_idx = 0
        self.barrier = Sem(None, "barrier")
        self.bar = []
        self.wc = {}
        self.wc_filled = set()
        self.use_cache = True
        self.dram = {}
        E = self.es.enter_context
        nc = self.nc
        self.scrA = E(nc.sbuf_tensor("scrA", [128, 2], F32))
        self.scrD = E(nc.sbuf_tensor("scrD", [128, 2], F32))
        self.scrP = E(nc.sbuf_tensor("scrP", [128, 2], F32))

    def getsem(self, name):
        if name not in self.sems:
            self.sems[name] = Sem(self.es.enter_context(self.nc.semaphore(name)), name)
        return self.sems[name]

    def cache_ap(self, key, nblk_hint=None):
        name, blk = key
        if name not in self.wc:
            t = self.nc.dram_tensor("wc_" + name, [WC_BLOCKS[name], 128, 4096], BF16, kind="Internal")
            self.wc[name] = t.ap()
        return self.wc[name][blk, :, :].rearrange("p (k n) -> p k n", n=128)

    def dt(self, name, shape, dtype, kind):
        t = self.nc.dram_tensor(name, list(shape), dtype, kind=kind)
        self.dram[name] = t.ap()
        return self.dram[name]


class Stage:
    def __init__(self, P, name):
        self.P = P
        self.nc = P.nc
        self.name = f"s{P.stage_idx}{name}"
        P.stage_idx += 1
        self.es = ExitStack()
        self.ops = {e: [] for e in ENGS}
        self.pending = []
        self.psrel = {}
        self.psnext = 0
        if P.bar:
            for e in ENGS:
                self.ops[e].append(("__wait__", (), {}, tuple(P.bar), None))

    def sb(self, name, shape, dt):
        return self.es.enter_context(self.nc.sbuf_tensor(f"{self.name}_{name}", list(shape), dt))

    def psum(self, name, shape, dt=F32):
        return self.es.enter_context(self.nc.psum_tensor(f"{self.name}_{name}", list(shape), dt))

    def sem(self, name):
        return Sem(None, name)

    def op(self, eng, fname, *args, waits=(), inc=None, **kw):
        waits = tuple(w for w in waits if w is not None)
        if inc is not None:
            sem = self.P.getsem(inc[0].name.split("__")[0] + "__" + eng)
            inc = (sem, inc[1])
        self.ops[eng].append((fname, args, kw, waits, inc))
        if inc is not None:
            inc[0].n += inc[1]
            return (inc[0], inc[0].n)
        return None

    def close(self):
        P = self.P
        bs = P.barrier
        b1 = self.op("act", "activation", out=P.scrA[:, 0:1], in_=P.scrA[:, 1:2], func=AF.Copy,
                     waits=self.pending, inc=(bs, 1))
        b2 = self.op("dve", "memset", P.scrD[:, 0:1], 0.0, inc=(bs, 1))
        b3 = self.op("pool", "memset", P.scrP[:, 0:1], 0.0, inc=(bs, 1))
        P.bar = [b1, b2, b3]
        nc = self.nc
        ops = self.ops

        def mk(engname):
            def f(eng):
                for (fname, args, kw, waits, inc) in ops[engname]:
                    for (sem, val) in waits:
                        eng.wait_ge(sem.h, val)
                    if fname == "__wait__":
                        continue
                    ins = getattr(eng, fname)(*args, **kw)
                    if inc is not None:
                        ins.then_inc(inc[0].h, inc[1])
            return f

        with nc.Block() as block:
            block.tensor(mk("pe"))
            block.scalar(mk("act"))
            block.vector(mk("dve"))
            block.gpsimd(mk("pool"))
            block.sync(mk("sp"))
        self.es.close()


class WStream:
    def __init__(self, S, views, NW=3):
        self.S = S
        self.views = views
        self.NW = NW
        self.bufs = [S.sb(f"wb{i}", [128, KC, 128], BF16) for i in range(NW)]
        self.wsem = [S.sem(f"wsem{i}") for i in range(NW)]
        self.loaded = [None] * len(views)
        self.wst = [S.sem(f"wst{i}") for i in range(NW)]
        self.stored = [None] * NW
        self.rel = [None] * NW
        self.next_load = 0
        self.next_use = 0

    def _load(self, j):
        s = j % self.NW
        S = self.S
        P = S.P
        v, ncols, key = self.views[j]
        waits = [self.rel[s], self.stored[s]]
        if key is not None and P.use_cache and key in P.wc_filled:
            self.loaded[j] = S.op("sp", "dma_start", out=self.bufs[s][:, :, :], in_=P.cache_ap(key),
                                  waits=waits, inc=(self.wsem[s], 16))
            self.stored[s] = None
            return
        ld = S.op("pool", "dma_start", out=self.bufs[s][:, :, 0:ncols], in_=v, waits=waits, inc=(self.wsem[s], 16))
        self.loaded[j] = ld
        self.stored[s] = None
        if key is not None and P.use_cache and ncols == 128:
            st = S.op("sp", "dma_start", out=P.cache_ap(key), in_=self.bufs[s][:, :, :], waits=[ld],
                      inc=(self.wst[s], 16))
            self.stored[s] = st
            S.pending.append(st)
            P.wc_filled.add(key)

    def take(self):
        j = self.next_use
        while self.next_load < min(len(self.views), j + self.NW):
            self._load(self.next_load)
            self.next_load += 1
        self.next_use += 1
        return self.bufs[j % self.NW], self.loaded[j], j

    def release(self, j, sv):
        self.rel[j % self.NW] = sv


def wview(W, r0, c0, ncols=128, key=None):
    return (W[r0:r0 + D, c0:c0 + ncols].rearrange("(kc p) n -> p kc n", p=128), ncols, key)


def gemm_ws(S, ws, nblk, rhs, T, epi, ps, first_waits=(), mcols=128):
    pedone = S.sem("pedone")
    nslots = 4096 // T
    ntt = T // 512
    last = None
    for c in range(nblk):
        buf, wwait, j = ws.take()
        slot = S.psnext % nslots
        S.psnext += 1
        waits = [wwait, S.psrel.get(slot)] + (list(first_waits) if c == 0 else [])
        for kc in range(KC):
            for tt in range(ntt):
                lastmm = (kc == KC - 1 and tt == ntt - 1)
                v = S.op("pe", "matmul", out=ps[0:mcols, slot * T + tt * 512: slot * T + (tt + 1) * 512],
                         lhsT=buf[:, kc, 0:mcols], rhs=rhs[:, kc, tt * 512:(tt + 1) * 512],
                         start=(kc == 0), stop=(kc == KC - 1),
                         waits=waits if (kc == 0 and tt == 0) else (),
                         inc=(pedone, 1) if lastmm else None)
        ws.release(j, v)
        S.psrel[slot] = epi(c, ps[0:mcols, slot * T:(slot + 1) * T], v)
        last = S.psrel[slot]
    return last


def consts_basic(S):
    c = {}
    c["ones_bf"] = S.sb("ones_bf", [128, 128], BF16)
    c["eps"] = S.sb("eps_t", [128, 1], F32)
    cs = S.sem("cset")
    S.op("pool", "memset", c["ones_bf"][:, :], 1.0)
    c["rdy"] = S.op("pool", "memset", c["eps"][:, :], EPS, inc=(cs, 1))
    return c


def load_vec(S, dst, src_ap, eng="sp"):
    sm = S.sem("vecld")
    return S.op(eng, "dma_start", out=dst[:, :], in_=src_ap.rearrange("(c p) -> p c", p=128),
                allow_slow_non_contiguous=True, inc=(sm, 16))


def norm_pass(S, cst, hT, t0, T, gvec, gwait, xb, ps, want_sq_only=False):
    xf = [S.sb(f"xf{i}", [128, T], F32) for i in range(2)]
    sq = [S.sb(f"sq{i}", [128, T], BF16) for i in range(2)]
    srt = S.sb("srt", [128, T], F32)
    rstd = S.sb("rstd", [128, T], F32)
    xld = [S.sem(f"xld{i}") for i in range(2)]
    sqrdy = S.sem("sqrdy")
    sqfree_s = S.sem("sqfree")
    nrm = S.sem("nrm")
    slot = S.psnext % (4096 // T)
    S.psnext += 1
    pss = ps[:, slot * T:(slot + 1) * T]
    xfree = [None, None]
    sqfree = [None, None]
    ntt = T // 512
    v_sq = None
    for fc in range(KC):
        i = fc % 2
        v_ld = S.op("sp", "dma_start", out=xf[i][:, :], in_=hT[fc * 128:(fc + 1) * 128, t0:t0 + T],
                    waits=[xfree[i]], inc=(xld[i], 16))
        if not want_sq_only:
            S.op("act", "activation", out=xb[:, fc, :], in_=xf[i][:, :], func=AF.Copy,
                 scale=gvec[:, fc:fc + 1], waits=[v_ld, gwait if fc == 0 else None])
        v_sq = S.op("act", "activation", out=sq[i][:, :], in_=xf[i][:, :], func=AF.Square,
                    waits=[v_ld, sqfree[i]], inc=(sqrdy, 1))
        xfree[i] = v_sq
        for tt in range(ntt):
            w = [v_sq] if tt == 0 else []
            if fc == 0 and tt == 0:
                w += [S.psrel.get(slot), cst["rdy"]]
            v_pe = S.op("pe", "matmul", out=pss[:, tt * 512:(tt + 1) * 512], lhsT=cst["ones_bf"][:, :],
                        rhs=sq[i][:, tt * 512:(tt + 1) * 512], start=(fc == 0), stop=(fc == KC - 1),
                        waits=w, inc=(sqfree_s, 1) if tt == ntt - 1 else None)
        sqfree[i] = v_pe
    v1 = S.op("act", "activation", out=srt[:, :], in_=pss, func=AF.Sqrt, scale=1.0 / D,
              bias=cst["eps"][:, 0:1], waits=[v_pe], inc=(nrm, 1))
    S.psrel[slot] = v1
    v2 = S.op("dve", "reciprocal", out=rstd[:, :], in_=srt[:, :], waits=[v1], inc=(nrm, 1))
    S.norm_scratch = (xf[0], xf[1], srt)
    return rstd, v2, v_sq


def ffn_stage(P, hT, t0, T, w_up, w_dn, gdram, lname='0'):
    S = Stage(P, "ffn")
    cst = consts_basic(S)
    ps = S.psum("ps", [128, 4096], F32)
    xb = S.sb("xb", [128, KC, T], BF16)
    ag = S.sb("ag", [128, KC, T], BF16)
    gv = S.sb("gv", [128, KC], F32)
    gw = load_vec(S, gv, gdram)
    rstd, rrdy, xrdy = norm_pass(S, cst, hT, t0, T, gv, gw, xb, ps)
    NG = 4
    views = []
    for g in range(NG):
        views += [wview(w_up, 0, g * D + c * 128, key=("up" + lname, g * KC + c)) for c in range(KC)]
        views += [wview(w_dn, g * D, c * 128, key=("dn" + lname, g * KC + c)) for c in range(KC)]
    ws = WStream(S, views)
    r1 = [S.sb(f"r1_{i}", [128, T], F32) for i in range(2)]
    st = [S.sb(f"st{i}", [128, T], F32) for i in range(2)]
    epi_s = S.sem("epi")
    acc_s = [S.sem(f"acc{i}") for i in range(2)]
    state = {"k": 0, "ag_done": None, "st_free": [None, None], "r1_free": [None, None], "dn_read": None}

    def up_epi(c, pv, pev):
        i = state["k"] % 2
        state["k"] += 1
        a = S.op("act", "activation", out=r1[i][:, :], in_=pv, func=AF.Relu,
                 waits=[pev, state["r1_free"][i]], inc=(epi_s, 1))
        b = S.op("dve", "tensor_tensor", out=r1[i][:, :], in0=r1[i][:, :], in1=rstd[:, :], op=ALU.mult,
                 waits=[a, rrdy], inc=(epi_s, 1))
        d = S.op("act", "activation", out=ag[:, c, :], in_=r1[i][:, :], func=AF.Square,
                 waits=[b, state["dn_read"] if c == 0 else None], inc=(epi_s, 1))
        state["r1_free"][i] = d
        state["ag_done"] = d
        return a

    def dn_epi(c, pv, pev):
        i = state["k"] % 2
        state["k"] += 1
        a = S.op("act", "activation", out=st[i][:, :], in_=pv, func=AF.Copy,
                 waits=[pev, state["st_free"][i]], inc=(epi_s, 1))
        v = S.op("pool", "dma_start", out=hT[c * 128:(c + 1) * 128, t0:t0 + T], in_=st[i][:, :],
                 accum_op=ALU.add, waits=[a], inc=(acc_s[i], 16))
        state["st_free"][i] = v
        S.pending.append(v)
        return a

    for g in range(NG):
        gemm_ws(S, ws, KC, xb, T, up_epi, ps, first_waits=[xrdy] if g == 0 else [])
        lastdn = gemm_ws(S, ws, KC, ag, T, dn_epi, ps, first_waits=[state["ag_done"]])
        state["dn_read"] = ws.rel[(ws.next_use - 1) % ws.NW]
    S.close()


def final_stage(P, hT, outT, t0, T, gdram):
    S = Stage(P, "fin")
    cst = consts_basic(S)
    ps = S.psum("ps", [128, 4096], F32)
    gv = S.sb("gv", [128, KC], F32)
    gw = load_vec(S, gv, gdram)
    rstd, rrdy, _ = norm_pass(S, cst, hT, t0, T, gv, gw, None, ps, want_sq_only=True)
    xg = [S.sb(f"xg{i}", [128, T], F32) for i in range(2)]
    ld = [S.sem(f"fld{i}") for i in range(2)]
    stq = [S.sem(f"fst{i}") for i in range(2)]
    fe = S.sem("fe")
    free = [None, None]
    for fc in range(KC):
        i = fc % 2
        v_ld = S.op("sp", "dma_start", out=xg[i][:, :], in_=hT[fc * 128:(fc + 1) * 128, t0:t0 + T],
                    waits=[free[i]], inc=(ld[i], 16))
        a = S.op("dve", "scalar_tensor_tensor", out=xg[i][:, :], in0=xg[i][:, :], scalar=gv[:, fc:fc + 1],
                 in1=rstd[:, :], op0=ALU.mult, op1=ALU.mult, waits=[v_ld, rrdy, gw], inc=(fe, 1))
        v = S.op("act", "dma_start", out=outT[fc * 128:(fc + 1) * 128, t0:t0 + T], in_=xg[i][:, :],
                 waits=[a], inc=(stq[i], 16))
        free[i] = v
        S.pending.append(v)
    S.close()


def gmlp_stage(P, hT, t0, w_in, w_out, gdram, lng_d, lnb_d, wsT_d, bs_d):
    T = 512
    NCH = 4
    S = Stage(P, "gm")
    cst = consts_basic(S)
    nc = S.nc
    ps = S.psum("ps", [128, 4096], F32)
    xb = S.sb("xb", [128, KC, T], BF16)
    uT = S.sb("uT", [128, KC, T], BF16)
    vg = S.sb("vg", [128, KC, T], BF16)
    vtk = S.sb("vtk", [128, NCH, D], BF16)
    gv = S.sb("gv", [128, KC], F32)
    lng = S.sb("lng", [128, KC], F32)
    lnb = S.sb("lnb", [128, KC], F32)
    load_vec(S, gv, gdram)
    load_vec(S, lng, lng_d)
    lw = load_vec(S, lnb, lnb_d)
    gw = lw
    ones_f = S.sb("ones_f", [128, 128], F32)
    ident_b = S.sb("ident_b", [128, 128], BF16)
    wT = S.sb("wT", [128, 8, 128], F32)
    wTb = S.sb("wTb", [128, 8, 128], BF16)
    bsrow = S.sb("bsrow", [1, 1024], F32)
    rwbs = S.sb("rwbs", [128, 2, 1024], F32)
    c2 = S.sb("c2", [128, KC, 128], F32)
    cs = S.sem("gmc")
    cl = S.sem("gmcl")
    w1 = S.op("sp", "dma_start", out=wT[:, :, :], in_=wsT_d.rearrange("h s t -> s h t"), inc=(cl, 16))
    cl2 = S.sem("gmcl2")
    w2 = S.op("sp", "dma_start", out=bsrow[:, :], in_=bs_d.rearrange("h t -> (h t)").unsqueeze(0), inc=(cl2, 16))
    S.op("pool", "memset", ones_f[:, :], 1.0)
    S.op("pool", "affine_select", out=ident_b[:, :], in_=cst["ones_bf"][:, :], pattern=[[1, 128]],
         compare_op=ALU.is_equal, fill=0.0, base=0, channel_multiplier=-1, waits=[cst["rdy"]])
    S.op("pool", "affine_select", out=wT[:, :, :], in_=wT[:, :, :], pattern=[[0, 8], [1, 128]],
         compare_op=ALU.is_ge, fill=0.0, base=0, channel_multiplier=-1, waits=[w1])
    k1 = S.op("pool", "tensor_copy", out=wTb[:, :, :], in_=wT[:, :, :], inc=(cs, 1))
    wTf = wT[:, :, :].rearrange("p h t -> p (h t)")
    for half in range(2):
        S.op("pe", "matmul", out=ps[:, 3072 + half * 512:3072 + (half + 1) * 512], lhsT=ones_f[:, :],
             rhs=wTf[:, half * 512:(half + 1) * 512], start=True, stop=True, waits=[k1] if half == 0 else [])
    for half in range(2):
        k2 = S.op("pe", "matmul", out=ps[:, 2048 + half * 512:2048 + (half + 1) * 512], lhsT=ones_f[0:1, :],
                  rhs=bsrow[0:1, half * 512:(half + 1) * 512], start=True, stop=True,
                  waits=[w2] if half == 0 else [], inc=(cs, 1) if half == 1 else None)
    S.op("act", "activation", out=rwbs[:, 0, :], in_=ps[:, 3072:4096], func=AF.Copy, waits=[k2])
    k3 = S.op("act", "activation", out=rwbs[:, 1, :], in_=ps[:, 2048:3072], func=AF.Copy, inc=(cs, 1))
    for s_ in (4, 5, 6, 7):
        S.psrel[s_] = k3
    for fc in range(KC):
        h = fc // 4
        k4 = S.op("dve", "scalar_tensor_tensor", out=c2[:, fc, :], in0=rwbs[:, 0, h * 128:(h + 1) * 128],
                  scalar=lnb[:, fc:fc + 1], in1=rwbs[:, 1, h * 128:(h + 1) * 128], op0=ALU.mult, op1=ALU.add,
                  waits=[k3, lw] if fc == 0 else [], inc=(cs, 1) if fc == KC - 1 else None)
    c2rdy = k4
    rstd, rrdy, xrdy = norm_pass(S, cst, hT, t0, T, gv, gw, xb, ps)
    views = [wview(w_in, 0, c * 128, key=("gmi", c)) for c in range(KC)]
    views += [wview(w_in, 0, D + c * 128, key=("gmi", KC + c)) for c in range(KC)]
    views += [wview(w_out, 0, c * 128, key=("gmo", c)) for c in range(KC)]
    ws = WStream(S, views, NW=2)
    tmp = [S.sb(f"tmp{i}", [128, T], F32) for i in range(2)]
    epi_s = S.sem("epi")
    state = {"k": 0, "tfree": [None, None], "last": None}

    def gelu_epi(dst):
        def epi(c, pv, pev):
            i = state["k"] % 2
            state["k"] += 1
            a = S.op("dve", "tensor_tensor", out=tmp[i][:, :], in0=pv, in1=rstd[:, :], op=ALU.mult,
                     waits=[pev, rrdy, state["tfree"][i]], inc=(epi_s, 1))
            b = S.op("act", "activation", out=dst[:, c, :], in_=tmp[i][:, :], func=AF.Gelu,
                     waits=[a], inc=(epi_s, 1))
            state["tfree"][i] = b
            state["last"] = b
            return a
        return epi

    gemm_ws(S, ws, KC, xb, T, gelu_epi(uT), ps, first_waits=[xrdy])
    gemm_ws(S, ws, KC, xb, T, gelu_epi(vg), ps)
    vdone = state["last"]
    xb_read = ws.rel[(ws.next_use - 1) % ws.NW]
    sqv = [S.sb(f"sqv{i}", [128, T], BF16) for i in range(2)]
    lns = S.sem("lns")
    lnp = S.sem("lnp")
    slot1 = S.psnext % 8
    S.psnext += 1
    slot2 = S.psnext % 8
    S.psnext += 1
    p1 = ps[:, slot1 * T:(slot1 + 1) * T]
    p2 = ps[:, slot2 * T:(slot2 + 1) * T]
    sfree = [None, None]
    for fc in range(KC):
        i = fc % 2
        a = S.op("dve", "tensor_tensor", out=sqv[i][:, :], in0=vg[:, fc, :], in1=vg[:, fc, :], op=ALU.mult,
                 waits=[sfree[i], vdone if fc == 0 else None], inc=(lns, 1))
        S.op("pe", "matmul", out=p1, lhsT=cst["ones_bf"][:, :], rhs=vg[:, fc, :], start=(fc == 0),
             stop=(fc == KC - 1), waits=[vdone, S.psrel.get(slot1), S.psrel.get(slot2)] if fc == 0 else [])
        b = S.op("pe", "matmul", out=p2, lhsT=cst["ones_bf"][:, :], rhs=sqv[i][:, :], start=(fc == 0),
                 stop=(fc == KC - 1), waits=[a], inc=(lnp, 1))
        sfree[i] = b
    mean, var, lrs = S.norm_scratch
    a = S.op("act", "activation", out=mean[:, :], in_=p1, func=AF.Copy, scale=1.0 / D, waits=[b], inc=(lns, 1))
    a2 = S.op("dve", "tensor_tensor", out=var[:, :], in0=mean[:, :], in1=mean[:, :], op=ALU.mult, waits=[a], inc=(lns, 1))
    a3 = S.op("dve", "scalar_tensor_tensor", out=var[:, :], in0=p2, scalar=1.0 / D, in1=var[:, :],
              op0=ALU.mult, op1=ALU.subtract, waits=[a2], inc=(lns, 1))
    S.psrel[slot1] = a3
    S.psrel[slot2] = a3
    a4 = S.op("act", "activation", out=var[:, :], in_=var[:, :], func=AF.Sqrt, bias=cst["eps"][:, 0:1],
              waits=[a3], inc=(lns, 1))
    a5 = S.op("dve", "reciprocal", out=lrs[:, :], in_=var[:, :], waits=[a4], inc=(lns, 1))
    vn = xb
    trs = S.sem("trs")
    trp = S.sem("trp")
    mixp = S.sem("mixp")
    tcp = S.sem("tcp")
    gts = S.sem("gts")
    pst = S.psum
    psb = ps.bitcast(BF16) if hasattr(ps, "bitcast") else None
    for fc in range(KC):
        a = S.op("dve", "tensor_tensor", out=tmp[0][:, :], in0=vg[:, fc, :], in1=mean[:, :], op=ALU.subtract,
                 waits=[a5, xb_read, state["tfree"][0]] if fc == 0 else [state.get("vnw")], inc=(trs, 1))
        state["vnw"] = S.op("dve", "tensor_tensor", out=vn[:, fc, :], in0=tmp[0][:, :], in1=lrs[:, :], op=ALU.mult,
                            waits=[a], inc=(trs, 1))
    vn_done = state["vnw"]
    pstr = ps[:, :].bitcast(BF16)
    cnt = 0
    for n in range(NCH):
        for gq in range(8):
            slot = S.psnext % 8
            S.psnext += 1
            for q in range(4):
                fc = gq * 4 + q
                v = S.op("pe", "transpose", out=pstr[:, slot * 1024 + q * 128: slot * 1024 + (q + 1) * 128],
                         in_=vn[:, fc, n * 128:(n + 1) * 128], identity=ident_b[:, :],
                         waits=[vn_done, S.psrel.get(slot), k1] if q == 0 else [],
                         inc=(trp, 1) if q == 3 else None)
            eng = "act" if cnt % 2 == 0 else "dve"
            if eng == "act":
                e = S.op("act", "activation", out=vtk[:, n, gq * 512:(gq + 1) * 512],
                         in_=pstr[:, slot * 1024: slot * 1024 + 512], func=AF.Copy, waits=[v], inc=(tcp, 1))
            else:
                e = S.op("dve", "tensor_copy", out=vtk[:, n, gq * 512:(gq + 1) * 512],
                         in_=pstr[:, slot * 1024: slot * 1024 + 512], waits=[v], inc=(tcp, 1))
            S.psrel[slot] = e
            cnt += 1
            state["vtk_a" if eng == "act" else "vtk_d"] = e
    vtk_done = [state["vtk_a"], state["vtk_d"]]
    for n in range(NCH):
        for gq in range(8):
            slot = S.psnext % 8
            S.psnext += 1
            h = gq
            for q in range(4):
                fc = gq * 4 + q
                v = S.op("pe", "matmul", out=ps[:, slot * 512 + q * 128: slot * 512 + (q + 1) * 128],
                         lhsT=vtk[:, n, fc * 128:(fc + 1) * 128], rhs=wTb[:, h, :], start=True, stop=True,
                         waits=(vtk_done + [S.psrel.get(slot)]) if q == 0 else [],
                         inc=(mixp, 1) if q == 3 else None)
            for q in range(4):
                fc = gq * 4 + q
                a = S.op("dve", "scalar_tensor_tensor", out=tmp[1][:, q * 128:(q + 1) * 128],
                         in0=ps[:, slot * 512 + q * 128: slot * 512 + (q + 1) * 128], scalar=lng[:, fc:fc + 1],
                         in1=c2[:, fc, :], op0=ALU.mult, op1=ALU.add,
                         waits=[v, c2rdy, state["tfree"][1], state.get("gt")] if q == 0 else [],
                         inc=(gts, 1) if q == 3 else None)
            S.psrel[slot] = a
            uview = uT[:, gq * 4:(gq + 1) * 4, n * 128:(n + 1) * 128]
            state["gt"] = S.op("dve", "tensor_tensor", out=uview, in0=uview,
                               in1=tmp[1][:, :].rearrange("p (q t) -> p q t", q=4), op=ALU.mult, waits=[a], inc=(gts, 1))
    gated_done = state["gt"]
    st = tmp
    acc_s = [S.sem(f"acc{i}") for i in range(2)]
    st_free = [None, None]

    def out_epi(c, pv, pev):
        i = c % 2
        a = S.op("act", "activation", out=st[i][:, :], in_=pv, func=AF.Copy, waits=[pev, st_free[i]], inc=(epi_s, 1))
        v = S.op("pool", "dma_start", out=hT[c * 128:(c + 1) * 128, t0:t0 + T], in_=st[i][:, :],
                 accum_op=ALU.add, waits=[a], inc=(acc_s[i], 16))
        st_free[i] = v
        S.pending.append(v)
        return a

    gemm_ws(S, ws, KC, uT, T, out_epi, ps, first_waits=[gated_done])
    S.close()


def copy_stage(P, dst, src):
    S = Stage(P, "cp")
    cs = S.sem("cpy")
    for q in range(8):
        v = S.op("sp", "dma_start", out=dst[q * 512:(q + 1) * 512, :], in_=src[q * 512:(q + 1) * 512, :], inc=(cs, 16))
    S.pending.append(v)
    S.close()


def mlproj_stage(P, hT, t0, T, w_in, gdram, bgate_d, projT, gates_tok, kv_only=False):
    S = Stage(P, "mlp")
    cst = consts_basic(S)
    ps = S.psum("ps", [128, 4096], F32)
    xb = S.sb("xb", [128, KC, T], BF16)
    gv = S.sb("gv", [128, KC], F32)
    gw = load_vec(S, gv, gdram)
    ident_f = S.sb("ident_f", [128, 128], F32)
    ones_f = S.sb("ones_f", [128, 128], F32)
    S.op("pool", "memset", ones_f[:, :], 1.0)
    cs = S.sem("cset")
    idr = S.op("pool", "affine_select", out=ident_f[:, :], in_=ones_f[:, :], pattern=[[1, 128]],
               compare_op=ALU.is_equal, fill=0.0, base=0, channel_multiplier=-1, waits=[cst["rdy"]], inc=(cs, 1))
    rstd, rrdy, xrdy = norm_pass(S, cst, hT, t0, T, gv, gw, xb, ps)
    blocks = list(range(16, 64)) if kv_only else list(range(96))
    views = [wview(w_in, 0, c * 128, key=("mli", c)) for c in blocks] + [wview(w_in, 0, 12304 - 128, key=("mli", 96))]
    ws = WStream(S, views)
    st = [S.sb(f"st{i}", [128, T], BF16) for i in range(2)]
    tmp = [S.sb(f"tmp{i}", [128, T], F32) for i in range(2)]
    epi_s = S.sem("epi")
    sts = [S.sem(f"pst{i}") for i in range(2)]
    state = {"k": 0, "stfree": [None, None], "tfree": [None, None]}

    def epi(ci, pv, pev):
        c = blocks[ci]
        i = state["k"] % 2
        state["k"] += 1
        if c < 64:
            sc = 0.0625 if c < 16 else 1.0
            a = S.op("dve", "scalar_tensor_tensor", out=st[i][:, :], in0=pv, scalar=sc, in1=rstd[:, :],
                     op0=ALU.mult, op1=ALU.mult, waits=[pev, rrdy, state["stfree"][i]], inc=(epi_s, 1))
            rel = a
            fin = a
        else:
            a = S.op("dve", "tensor_tensor", out=tmp[i][:, :], in0=pv, in1=rstd[:, :], op=ALU.mult,
                     waits=[pev, rrdy, state["tfree"][i]], inc=(epi_s, 1))
            fin = S.op("act", "activation", out=st[i][:, :], in_=tmp[i][:, :], func=AF.Sigmoid,
                       waits=[a, state["stfree"][i]], inc=(epi_s, 1))
            state["tfree"][i] = fin
            rel = a
        v = S.op("sp", "dma_start", out=projT[c * 128:(c + 1) * 128, t0:t0 + T], in_=st[i][:, :],
                 waits=[fin], inc=(sts[i], 16))
        state["stfree"][i] = v
        S.pending.append(v)
        return rel

    gemm_ws(S, ws, len(blocks), xb, T, epi, ps, first_waits=[xrdy])
    import os
    if os.environ.get("SKIPG"):
        S.close()
        return
    nch = T // 128
    gpe = S.sem("gpe")
    gac = S.sem("gac")
    gdv = S.sem("gdv")
    gpo = S.sem("gpo")
    buf, wwait, j = ws.take()
    slot = S.psnext % (4096 // T)
    S.psnext += 1
    psg = ps[:, slot * T: slot * T + nch * 16]
    for n in range(nch):
        for kc in range(KC):
            v = S.op("pe", "matmul", out=psg[:, n * 16:(n + 1) * 16], lhsT=xb[:, kc, n * 128:(n + 1) * 128],
                     rhs=buf[:, kc, 112:128], start=(kc == 0), stop=(kc == KC - 1),
                     waits=[wwait, S.psrel.get(slot)] if (n == 0 and kc == 0) else [],
                     inc=(gpe, 1) if (n == nch - 1 and kc == KC - 1) else None)
    ws.release(j, v)
    gdone = v
    slot2 = S.psnext % (4096 // T)
    S.psnext += 1
    pst = ps[:, slot2 * T: slot2 * T + nch * 128].rearrange("p (n k) -> p n k", k=128)
    for n in range(nch):
        tv = S.op("pe", "transpose", out=pst[:, n, :], in_=rstd[:, n * 128:(n + 1) * 128], identity=ident_f[:, :],
                  waits=[rrdy, idr, S.psrel.get(slot2)] if n == 0 else [], inc=(gpe, 1) if n == nch - 1 else None)
    rt = S.sb("rt", [128, nch], F32)
    a0 = S.op("act", "activation", out=rt[:, :], in_=pst[:, :, 0], func=AF.Copy, waits=[tv], inc=(gac, 1))
    S.psrel[slot2] = a0
    bgrow = S.sb("bgrow", [1, 16], F32)
    bgl2 = S.sem("bgl2")
    bw = S.op("sp", "dma_start", out=bgrow[:, :], in_=bgate_d.unsqueeze(0), inc=(bgl2, 16))
    slot3 = S.psnext % (4096 // T)
    S.psnext += 1
    pbg = ps[:, slot3 * T: slot3 * T + 16]
    bm = S.op("pe", "matmul", out=pbg, lhsT=ones_f[0:1, :], rhs=bgrow[0:1, :], start=True, stop=True,
              waits=[bw, idr, S.psrel.get(slot3)], inc=(gpe, 1))
    bgbc = S.sb("bgbc", [128, 16], F32)
    b0 = S.op("act", "activation", out=bgbc[:, :], in_=pbg, func=AF.Copy, waits=[bm], inc=(gac, 1))
    S.psrel[slot3] = b0
    if os.environ.get("GSTOP") == "1":
        S.pending.append(b0); S.pending.append(a0)
        S.close()
        return
    gt = S.sb("gt", [128, nch, 16], F32)
    cap = S.sb("cap", [128, nch, 16], F32)
    lp = S.sb("lp", [128, nch, 16], F32)
    one1 = S.sb("one1", [128, 1], F32)
    o1 = S.op("pool", "memset", one1[:, :], 1.0, inc=(gpo, 1))
    a = None
    graw = S.sb("graw", [128, nch, 16], F32)
    ev = S.op("act", "activation", out=graw[:, :, :], in_=psg.rearrange("p (n k) -> p n k", k=16), func=AF.Copy,
              waits=[gdone, a0], inc=(gac, 1))
    S.psrel[slot] = ev
    a = None
    for n in range(nch):
        a = S.op("dve", "scalar_tensor_tensor", out=gt[:, n, :], in0=graw[:, n, :], scalar=rt[:, n:n + 1],
                 in1=bgbc[:, :], op0=ALU.mult, op1=ALU.add, waits=[ev, b0] if n == 0 else [a], inc=(gdv, 1))
    if os.environ.get("GSTOP") == "2":
        S.pending.append(a)
        S.close()
        return
    c1 = S.op("act", "activation", out=cap[:, :, :], in_=gt[:, :, :], func=AF.Tanh, scale=1.0 / 15.0, waits=[a], inc=(gac, 1))
    c2 = S.op("act", "activation", out=lp[:, :, :], in_=cap[:, :, :], func=AF.Exp, scale=-15.0, waits=[c1], inc=(gac, 1))
    c3 = S.op("act", "activation", out=lp[:, :, :], in_=lp[:, :, :], func=AF.Ln, bias=one1[:, 0:1], waits=[c2, o1], inc=(gac, 1))
    gtk = S.sb("gtk", [128, nch, 16], F32)
    d1 = S.op("dve", "tensor_scalar", out=gtk[:, :, 0:8], in0=cap[:, :, 0:8], scalar1=15.0, scalar2=None, op0=ALU.mult,
              waits=[c3], inc=(gdv, 1))
    d2 = S.op("dve", "tensor_scalar", out=gtk[:, :, 8:16], in0=lp[:, :, 8:16], scalar1=-1.0, scalar2=None, op0=ALU.mult,
              waits=[d1], inc=(gdv, 1))
    gl = S.sem("gld")
    v = S.op("sp", "dma_start", out=gates_tok[t0:t0 + T, :].rearrange("(n p) g -> p n g", p=128), in_=gtk[:, :, :],
             waits=[d2], inc=(gl, 16))
    S.pending.append(v)
    S.close()


def mlrec_stage(P, t0, T, full, projT, gates_tok, stC_in, stn_in, stC_out, stn_out,
                headg_d=None, w_out=None, hT=None):
    S = Stage(P, "mlr" if full else "mls")
    cst = consts_basic(S)
    nch = T // 128
    psA = S.psum("psA", [128, 512], F32)
    psNum = S.psum("psNum", [128, 512], F32)
    psQc = S.psum("psQc", [128, 512], F32)
    psC = S.psum("psC", [128, 1024], F32)
    psD = S.psum("psD", [128, 512], F32)
    psT = S.psum("psT", [128, 1024], F32)
    pT = psT[:, :].bitcast(BF16)
    ones_f = S.sb("ones_f", [128, 128], F32)
    ident_f = S.sb("ident_f", [128, 128], F32)
    ident_b = S.sb("ident_b", [128, 128], BF16)
    Utri = S.sb("Utri", [128, 128], F32)
    negm = S.sb("negm", [128, 128], F32)
    zer = S.sb("zer", [128, 128], F32)
    cs = S.sem("cset")
    S.op("pool", "memset", ones_f[:, :], 1.0)
    S.op("pool", "memset", zer[:, :], 0.0)
    S.op("pool", "affine_select", out=ident_f[:, :], in_=ones_f[:, :], pattern=[[1, 128]],
         compare_op=ALU.is_equal, fill=0.0, base=0, channel_multiplier=-1, waits=[cst["rdy"]])
    S.op("pool", "affine_select", out=ident_b[:, :], in_=cst["ones_bf"][:, :], pattern=[[1, 128]],
         compare_op=ALU.is_equal, fill=0.0, base=0, channel_multiplier=-1)
    S.op("pool", "affine_select", out=Utri[:, :], in_=ones_f[:, :], pattern=[[1, 128]],
         compare_op=ALU.is_ge, fill=0.0, base=0, channel_multiplier=-1)
    crdy = S.op("pool", "affine_select", out=negm[:, :], in_=zer[:, :], pattern=[[1, 128]],
                compare_op=ALU.is_ge, fill=-30000.0, base=0, channel_multiplier=-1, inc=(cs, 1))
    gtk = S.sb("gtk", [128, nch, 16], F32)
    lf = S.sb("lf", [128, nch * 8], F32)
    bcum = S.sb("bcum", [128, nch * 8], F32)
    Gbc = S.sb("Gbc", [128, nch * 8], F32)
    bias_s = S.sb("bias_s", [128, nch * 8], F32)
    ea = S.sb("ea", [128, nch * 8], F32)
    etok = S.sb("etok", [128, nch * 8], F32)
    eg = S.sb("eg", [128, nch * 8], F32)
    gl = S.sem("gld")
    gp = S.sem("gpre")
    v = S.op("sp", "dma_start", out=gtk[:, :, :], in_=gates_tok[t0:t0 + T, :].rearrange("(n p) g -> p n g", p=128),
             inc=(gl, 16))
    lf3 = lf[:, :].rearrange("p (n h) -> p n h", h=8)
    a = S.op("dve", "tensor_copy", out=lf3, in_=gtk[:, :, 8:16], waits=[v], inc=(gp, 1))
    NH = nch * 8
    S.op("pe", "matmul", out=psA[:, 256:256 + NH], lhsT=Utri[:, :], rhs=lf[:, :], start=True, stop=True, waits=[a, crdy])
    b = S.op("pe", "matmul", out=psA[:, 320:320 + NH], lhsT=ones_f[:, :], rhs=lf[:, :], start=True, stop=True, inc=(gp, 1))
    c1 = S.op("act", "activation", out=bcum[:, :], in_=psA[:, 256:256 + NH], func=AF.Copy, waits=[b], inc=(gp, 1))
    c2 = S.op("act", "activation", out=Gbc[:, :], in_=psA[:, 320:320 + NH], func=AF.Copy, waits=[c1], inc=(gp, 1))
    d1 = S.op("dve", "tensor_tensor", out=bias_s[:, :].rearrange("p (n h) -> p n h", h=8), in0=gtk[:, :, 0:8],
              in1=bcum[:, :].rearrange("p (n h) -> p n h", h=8), op=ALU.subtract, waits=[c2], inc=(gp, 1))
    d2 = S.op("dve", "tensor_tensor", out=ea[:, :], in0=bias_s[:, :], in1=Gbc[:, :], op=ALU.add, waits=[d1], inc=(gp, 1))
    e1 = S.op("act", "activation", out=ea[:, :], in_=ea[:, :], func=AF.Exp, waits=[d2], inc=(gp, 1))
    e2 = S.op("act", "activation", out=etok[:, :], in_=bcum[:, :], func=AF.Exp, waits=[e1], inc=(gp, 1))
    grdy = S.op("act", "activation", out=eg[:, :], in_=Gbc[:, :], func=AF.Exp, waits=[e2], inc=(gp, 1))
    RSTOP = int(os.environ.get("RSTOP", "0"))
    if RSTOP == 1:
        S.pending.append(grdy)
        S.close()
        return
    ncols = 12 if full else 6
    hb = S.sb("hb", [128, ncols, T], BF16)
    Cst = S.sb("Cst", [128, 1024], F32)
    Cb = S.sb("Cb", [128, 1024], BF16)
    nst = S.sb("nst", [128, 2], F32)
    nb = S.sb("nb", [128, 2], BF16)
    kw = [S.sb(f"kw{i}", [128, 256], BF16) for i in range(2)]
    vtok = [S.sb(f"vtok{i}", [128, 512], BF16) for i in range(2)]
    if full:
        hsT = S.sb("hsT", [128, KC, T], BF16)
        otok = [S.sb(f"otok{i}", [128, 512], BF16) for i in range(2)]
        diagb = [S.sb(f"diagb{i}", [128, 128], F32) for i in range(2)]
        DT = [S.sb(f"DT{i}", [128, 128], F32) for i in range(2)]
        PT = [S.sb(f"PT{i}", [128, 128], BF16) for i in range(2)]
        intra = [S.sb(f"intra{i}", [128, 512], F32) for i in range(2)]
        num = [S.sb(f"num{i}", [128, 512], F32) for i in range(2)]
        junk = S.sb("junk", [128, 512], F32)
        go = [S.sb(f"go{i}", [128, 512], F32) for i in range(2)]
        hs = [S.sb(f"hs{i}", [128, 512], BF16) for i in range(2)]
        sm = [S.sb(f"sm{i}", [128, 8], F32) for i in range(2)]
        hg = S.sb("hg", [128, D], F32)
        hgl = S.sem("hgl")
        hgw = S.op("sp", "dma_start", out=hg[:, :], in_=headg_d[0, :].partition_broadcast(128), inc=(hgl, 16))
    hl = S.sem("hld")
    cl = S.sem("cld")
    cst_s = S.sem("cstore")
    pe_s = S.sem("rpe")
    ac_s = S.sem("ract")
    dv_s = S.sem("rdve")
    po_s = S.sem("rpool")
    prev = {"pe_step": None, "cstore": None, "hb_read": None, "psT_k": None, "psT_v": None, "psT_o": None,
            "psT_h": None, "psA_b": None, "psA_s": None, "psNum": None, "psQc": None, "psC": None, "psD": None,
            "cupd": None, "cb": None, "hsTcopy": None}
    step = 0
    for h in range(int(os.environ.get('RH', 8))):
        lw = []
        if full:
            rows = [(h * 256, 2, 0), (2048 + h * 256, 2, 2), (4096 + h * 512, 4, 4), (8192 + h * 512, 4, 8)]
        else:
            rows = [(2048 + h * 256, 2, 0), (4096 + h * 512, 4, 2)]
        for (r0, nck, dst) in rows:
            lw.append(S.op("sp", "dma_start", out=hb[:, dst:dst + nck, :],
                           in_=projT[r0:r0 + nck * 128, t0:t0 + T].rearrange("(j p) t -> p j t", p=128),
                           waits=[prev["hb_read"]], inc=(hl, 16)))
        hbw = [lw[-1]]
        cw1 = S.op("sp", "dma_start", out=Cst[:, :], in_=stC_in[:, h * 1024:(h + 1) * 1024],
                   waits=[prev["cstore"]], inc=(cl, 16))
        cw = S.op("sp", "dma_start", out=nst[:, :], in_=stn_in[:, h * 2:(h + 1) * 2], inc=(cl, 16))
        if full:
            prev["cb"] = S.op("act", "activation", out=Cb[:, :], in_=Cst[:, :], func=AF.Copy, waits=[cw, prev["pe_step"]], inc=(ac_s, 1))
            prev["cb"] = S.op("act", "activation", out=nb[:, :], in_=nst[:, :], func=AF.Copy, inc=(ac_s, 1))
        if RSTOP == 2:
            S.pending += hbw + [cw]
            S.close()
            return
        if full:
            qo, ko, vo, oo = 0, 2, 4, 8
        else:
            ko, vo = 0, 2
        for n in range(nch):
            par = step % 2
            ch = n * 8 + h
            tk = slice(n * 128, (n + 1) * 128)
            first = (n == 0)
            for j in range(2):
                S.op("pe", "transpose", out=pT[:, j * 128:(j + 1) * 128], in_=hb[:, ko + j, tk], identity=ident_b[:, :],
                     waits=(hbw + [prev["psT_k"], prev["psT_v"], crdy]) if j == 0 else [])
            for j in range(4):
                tv = S.op("pe", "transpose", out=pT[:, 256 + j * 128:256 + (j + 1) * 128], in_=hb[:, vo + j, tk],
                          identity=ident_b[:, :], waits=[prev["psT_v"]] if j == 0 else [],
                          inc=(pe_s, 1) if j == 3 else None)
            if full:
                for j in range(4):
                    to = S.op("pe", "transpose", out=pT[:, 1024 + j * 128:1024 + (j + 1) * 128], in_=hb[:, oo + j, tk],
                              identity=ident_b[:, :], waits=[prev["psT_o"], prev["psT_h"]] if j == 0 else [],
                              inc=(pe_s, 1) if j == 3 else None)
            kwv = S.op("dve", "tensor_scalar", out=kw[par][:, :], in0=pT[:, 0:256], scalar1=ea[:, ch:ch + 1], scalar2=None,
                       op0=ALU.mult, waits=[tv, grdy, prev["pe_step"]], inc=(dv_s, 1))
            prev["psT_k"] = kwv
            vtv = S.op("act", "activation", out=vtok[par][:, :], in_=pT[:, 256:768], func=AF.Copy,
                       waits=[tv, prev["pe_step"], kwv], inc=(ac_s, 1))
            prev["psT_v"] = vtv
            if RSTOP == 31:
                S.ops["act"].pop()
                vtv.__class__
                S.pending += [kwv, cw]
                S.close()
                return
            if RSTOP == 32:
                S.ops["act"].pop()
                S.ops["dve"].pop()
                S.pending += [tv, cw]
                S.close()
                return
            if RSTOP == 3:
                S.pending += [kwv, vtv, cw]
                S.close()
                return
            if full:
                otv = S.op("act", "activation", out=otok[par][:, :], in_=pT[:, 1024:1536], func=AF.Copy,
                           waits=[to], inc=(ac_s, 1))
                prev["psT_o"] = otv
                gov = S.op("pool", "tensor_tensor", out=go[par][:, :], in0=otok[par][:, :], in1=hg[:, h * 512:(h + 1) * 512],
                           op=ALU.mult, waits=[otv, hgw], inc=(po_s, 1))
                dgv = S.op("act", "activation", out=diagb[par][:, :], in_=ident_f[:, :], func=AF.Copy,
                           scale=bcum[:, ch:ch + 1], waits=[grdy], inc=(ac_s, 1))
                S.op("pe", "matmul", out=psA[:, 0:128], lhsT=ones_f[:, :], rhs=diagb[par][:, :], start=True, stop=False,
                     waits=[dgv, prev["psA_b"]])
                pbv = S.op("pe", "matmul", out=psA[:, 0:128], lhsT=ident_f[:, :], rhs=negm[:, :], start=False, stop=True,
                           inc=(pe_s, 1))
                for j in range(2):
                    psv = S.op("pe", "matmul", out=psA[:, 128:256], lhsT=hb[:, ko + j, tk], rhs=hb[:, qo + j, tk],
                               start=(j == 0), stop=(j == 1), waits=[prev["psA_s"]] if j == 0 else [],
                               inc=(pe_s, 1) if j == 1 else None)
                dtv = S.op("act", "activation", out=DT[par][:, :], in_=psA[:, 0:128], func=AF.Exp,
                           bias=bias_s[:, ch:ch + 1], waits=[pbv], inc=(ac_s, 1))
                prev["psA_b"] = dtv
                ptv = S.op("dve", "tensor_tensor", out=PT[par][:, :], in0=psA[:, 128:256], in1=DT[par][:, :], op=ALU.mult,
                           waits=[psv, dtv], inc=(dv_s, 1))
                prev["psA_s"] = ptv
                nmv = S.op("pe", "matmul", out=psNum[:, :], lhsT=PT[par][:, :], rhs=vtok[par][:, :], start=True, stop=True,
                           waits=[ptv, vtv, prev["psNum"]], inc=(pe_s, 1))
                S.op("pe", "matmul", out=psD[:, 0:1], lhsT=PT[par][:, :], rhs=cst["ones_bf"][:, 0:1], start=True, stop=True,
                     waits=[prev["psD"]])
                for j in range(2):
                    S.op("pe", "matmul", out=psQc[:, :], lhsT=hb[:, qo + j, tk], rhs=Cb[:, j * 512:(j + 1) * 512],
                         start=(j == 0), stop=(j == 1), waits=[prev["cb"], prev["psQc"]] if j == 0 else [])
                for j in range(2):
                    qcv = S.op("pe", "matmul", out=psD[:, 1:2], lhsT=hb[:, qo + j, tk], rhs=nb[:, j:j + 1],
                               start=(j == 0), stop=(j == 1), inc=(pe_s, 1) if j == 1 else None)
            for j in range(2):
                S.op("pe", "matmul", out=psC[:, j * 512:(j + 1) * 512], lhsT=kw[par][:, j * 128:(j + 1) * 128],
                     rhs=vtok[par][:, :], start=True, stop=True, waits=[kwv, vtv, prev["psC"]] if j == 0 else [])
            for j in range(2):
                pcv = S.op("pe", "matmul", out=psD[:, 2 + j:3 + j], lhsT=kw[par][:, j * 128:(j + 1) * 128],
                           rhs=cst["ones_bf"][:, 0:1], start=True, stop=True,
                           waits=[prev["psD"]] if (j == 0 and not full) else [],
                           inc=(pe_s, 1) if j == 1 else None)
            pe_last = pcv
            if RSTOP == 4:
                S.pending += [pcv, cw]
                S.close()
                return
            if full:
                inv = S.op("act", "activation", out=intra[par][:, :], in_=psNum[:, :], func=AF.Copy, waits=[nmv], inc=(ac_s, 1))
                prev["psNum"] = inv
                smx = sm[par]
                d0 = S.op("dve", "tensor_copy", out=smx[:, 0:2], in_=psD[:, 0:2], waits=[qcv], inc=(dv_s, 1))
                nuv = S.op("dve", "scalar_tensor_tensor", out=num[par][:, :], in0=psQc[:, :], scalar=etok[:, ch:ch + 1],
                           in1=intra[par][:, :], op0=ALU.mult, op1=ALU.add, waits=[inv, qcv], inc=(dv_s, 1))
                prev["psQc"] = nuv
                x1 = S.op("dve", "scalar_tensor_tensor", out=smx[:, 2:3], in0=smx[:, 1:2], scalar=etok[:, ch:ch + 1],
                          in1=smx[:, 0:1], op0=ALU.mult, op1=ALU.add, waits=[d0], inc=(dv_s, 1))
                x2a = S.op("dve", "scalar_tensor_tensor", out=smx[:, 3:4], in0=smx[:, 2:3], scalar=-1.0,
                           in1=smx[:, 2:3], op0=ALU.mult, op1=ALU.max, waits=[x1], inc=(dv_s, 1))
                x2 = S.op("dve", "tensor_scalar_max", out=smx[:, 3:4], in0=smx[:, 3:4], scalar1=1.0,
                          waits=[x2a], inc=(dv_s, 1))
                x3 = S.op("dve", "reciprocal", out=smx[:, 4:5], in_=smx[:, 3:4], waits=[x2], inc=(dv_s, 1))
                s1 = S.op("act", "activation", out=junk[:, :], in_=num[par][:, :], func=AF.Square, accum_out=smx[:, 5:6],
                          waits=[nuv], inc=(ac_s, 1))
                y1 = S.op("dve", "tensor_tensor", out=smx[:, 6:7], in0=smx[:, 4:5], in1=smx[:, 4:5], op=ALU.mult,
                          waits=[x3], inc=(dv_s, 1))
                y2 = S.op("dve", "scalar_tensor_tensor", out=smx[:, 6:7], in0=smx[:, 5:6], scalar=1.0 / 512.0,
                          in1=smx[:, 6:7], op0=ALU.mult, op1=ALU.mult, waits=[y1, s1], inc=(dv_s, 1))
                y3 = S.op("act", "activation", out=smx[:, 6:7], in_=smx[:, 6:7], func=AF.Sqrt, bias=cst["eps"][:, 0:1],
                          waits=[y2], inc=(ac_s, 1))
                y4 = S.op("dve", "reciprocal", out=smx[:, 7:8], in_=smx[:, 6:7], waits=[y3], inc=(dv_s, 1))
                y5 = S.op("dve", "tensor_tensor", out=smx[:, 7:8], in0=smx[:, 7:8], in1=smx[:, 4:5], op=ALU.mult,
                          waits=[y4], inc=(dv_s, 1))
                hsv = S.op("dve", "scalar_tensor_tensor", out=hs[par][:, :], in0=num[par][:, :], scalar=smx[:, 7:8],
                           in1=go[par][:, :], op0=ALU.mult, op1=ALU.mult, waits=[y5, gov], inc=(dv_s, 1))
                for j in range(4):
                    thv = S.op("pe", "transpose", out=pT[:, 1536 + j * 128:1536 + (j + 1) * 128],
                               in_=hs[par][:, j * 128:(j + 1) * 128], identity=ident_b[:, :],
                               waits=[hsv, prev["psT_h"]] if j == 0 else [], inc=(pe_s, 1) if j == 3 else None)
                pe_last = thv
                hcv = S.op("act", "activation", out=hsT[:, h * 4:(h + 1) * 4, tk],
                           in_=pT[:, 1536:2048].rearrange("p (j t) -> p j t", j=4), func=AF.Copy, waits=[thv], inc=(ac_s, 1))
                prev["psT_h"] = hcv
                prev["hsTcopy"] = hcv
            cu = S.op("dve", "scalar_tensor_tensor", out=Cst[:, :], in0=Cst[:, :], scalar=eg[:, ch:ch + 1], in1=psC[:, :],
                      op0=ALU.mult, op1=ALU.add, waits=[pcv, cw if first else None, prev["cb"] if full else None],
                      inc=(dv_s, 1))
            prev["psC"] = cu
            nu = S.op("dve", "scalar_tensor_tensor", out=nst[:, :], in0=nst[:, :], scalar=eg[:, ch:ch + 1], in1=psD[:, 2:4],
                      op0=ALU.mult, op1=ALU.add, waits=[cu], inc=(dv_s, 1))
            prev["psD"] = nu
            prev["cupd"] = nu
            if RSTOP == 5:
                S.pending += [nu]
                S.close()
                return
            if full:
                S.op("act", "activation", out=Cb[:, :], in_=Cst[:, :], func=AF.Copy, waits=[nu], inc=(ac_s, 1))
                prev["cb"] = S.op("act", "activation", out=nb[:, :], in_=nst[:, :], func=AF.Copy, inc=(ac_s, 1))
            prev["pe_step"] = pe_last
            step += 1
        prev["hb_read"] = prev["pe_step"]
        S.op("sp", "dma_start", out=stC_out[:, h * 1024:(h + 1) * 1024], in_=Cst[:, :], waits=[prev["cupd"]], inc=(cst_s, 16))
        prev["cstore"] = S.op("sp", "dma_start", out=stn_out[:, h * 2:(h + 1) * 2], in_=nst[:, :], inc=(cst_s, 16))
    S.pending.append(prev["cstore"])
    if full:
        views = [wview(w_out, 0, c * 128, key=("mlo", c)) for c in range(KC)]
        ws = WStream(S, views)
        st = [S.sb(f"st{i}", [128, T], F32) for i in range(2)]
        acc_s = [S.sem(f"acc{i}") for i in range(2)]
        epi_s = S.sem("epi")
        st_free = [None, None]
        pedone = S.sem("pedone")
        banks = [psC, psT]
        brel = [prev["psC"], prev["hsTcopy"]]
        for c in range(KC):
            buf, wwait, j = ws.take()
            bi = c % 2
            pb = banks[bi]
            for kc in range(KC):
                for tt in range(T // 512):
                    lastmm = (kc == KC - 1 and tt == T // 512 - 1)
                    v = S.op("pe", "matmul", out=pb[:, tt * 512:(tt + 1) * 512], lhsT=buf[:, kc, :],
                             rhs=hsT[:, kc, tt * 512:(tt + 1) * 512], start=(kc == 0), stop=(kc == KC - 1),
                             waits=[wwait, brel[bi], prev["hsTcopy"]] if (kc == 0 and tt == 0) else (),
                             inc=(pedone, 1) if lastmm else None)
            ws.release(j, v)
            i = c % 2
            a = S.op("act", "activation", out=st[i][:, :], in_=pb[:, 0:T], func=AF.Copy, waits=[v, st_free[i]], inc=(epi_s, 1))
            brel[bi] = a
            dv = S.op("pool", "dma_start", out=hT[c * 128:(c + 1) * 128, t0:t0 + T], in_=st[i][:, :],
                      accum_op=ALU.add, waits=[a], inc=(acc_s[i], 16))
            st_free[i] = dv
            S.pending.append(dv)
    S.close()


def select_stage(P, gath, sel_d, stC, stn):
    S = Stage(P, "sel")
    W = 8192 + 16
    acc = S.sb("acc", [128, W], F32)
    gb = [S.sb(f"gb{i}", [128, W], F32) for i in range(2)]
    sel = S.sb("sel", [128, 8], F32)
    ls = S.sem("sld")
    gs = [S.sem(f"sg{i}") for i in range(2)]
    ds = S.sem("sdv")
    ss = S.sem("sst")
    sw = S.op("sp", "dma_start", out=sel[:, :], in_=sel_d[:, :], inc=(ls, 16))
    free = [None, None]
    a = None
    for r in range(8):
        i = r % 2
        v = S.op("sp", "dma_start", out=gb[i][:, :], in_=gath[r, :, :], waits=[free[i]], inc=(gs[i], 16))
        if r == 0:
            a = S.op("dve", "tensor_scalar", out=acc[:, :], in0=gb[i][:, :], scalar1=sel[:, 0:1], scalar2=None,
                     op0=ALU.mult, waits=[v, sw], inc=(ds, 1))
        else:
            a = S.op("dve", "scalar_tensor_tensor", out=acc[:, :], in0=gb[i][:, :], scalar=sel[:, r:r + 1], in1=acc[:, :],
                     op0=ALU.mult, op1=ALU.add, waits=[v, a], inc=(ds, 1))
        free[i] = a
    S.op("sp", "dma_start", out=stC[:, :], in_=acc[:, 0:8192], waits=[a], inc=(ss, 16))
    v = S.op("sp", "dma_start", out=stn[:, :], in_=acc[:, 8192:W], inc=(ss, 16))
    S.pending.append(v)
    S.close()


def scale_stage(P, src, sel_d, dst):
    S = Stage(P, "scl")
    buf = S.sb("buf", [128, SW], F32)
    sel = S.sb("sel", [128, 1], F32)
    l1 = S.sem("scl1")
    l2 = S.sem("scl2")
    dv = S.sem("scdv")
    st = S.sem("scst")
    a = S.op("sp", "dma_start", out=sel[:, :], in_=sel_d[:, :], inc=(l1, 16))
    b = S.op("sp", "dma_start", out=buf[:, :], in_=src[:, :], inc=(l2, 16))
    c = S.op("dve", "tensor_scalar", out=buf[:, :], in0=buf[:, :], scalar1=sel[:, 0:1], scalar2=None, op0=ALU.mult,
             waits=[a, b], inc=(dv, 1))
    v = S.op("sp", "dma_start", out=dst[:, :], in_=buf[:, :], waits=[c], inc=(st, 16))
    S.pending.append(v)
    S.close()


SW = 8192 + 16


def build(mode):
    P = Prog()
    EI, EO, IN = "ExternalInput", "ExternalOutput", "Internal"
    d = P.dt
    p1 = mode in ("fused", "p1")
    p2 = mode in ("fused", "p2")
    nm = d("norm_mix", [2, D], F32, EI)
    nf = d("norm_ffn", [2, D], F32, EI)
    if p1:
        xT = d("xT", [D, NTOK], F32, EI)
        gm_w_in = d("gm_w_in", [D, 2 * D], F32, EI)
        gm_w_out = d("gm_w_out", [D, D], F32, EI)
        lng = d("gm_ln_g", [D], F32, EI)
        lnb = d("gm_ln_b", [D], F32, EI)
        wsT = d("gm_w_sT", [8, 128, 128], F32, EI)
        bs = d("gm_b_s", [8, 128], F32, EI)
        up0 = d("ffn_up0", [D, 4 * D], F32, EI)
        dn0 = d("ffn_dn0", [4 * D, D], F32, EI)
        ml_w_in = d("ml_w_in", [D, 12304], F32, EI)
        bgate = d("ml_b_gate", [16], F32, EI)
        zst = d("zst", [128, SW], F32, EI)
    if p2:
        headg = d("ml_head_g", [1, D], F32, EI)
        ml_w_out = d("ml_w_out", [D, D], F32, EI)
        up1 = d("ffn_up1", [D, 4 * D], F32, EI)
        dn1 = d("ffn_dn1", [4 * D, D], F32, EI)
        nfin = d("norm_final", [D], F32, EI)
        outT = d("outT", [D, NTOK], F32, EO)
    if mode == "fused":
        hT = d("hT", [D, NTOK], F32, IN)
        projT = d("projT", [12288, NTOK], BF16, IN)
        gtok = d("gtok", [NTOK, 16], F32, IN)
        stA = d("stA", [128, SW], F32, IN)
        stB = d("stB", [128, SW], F32, IN)
        stI = d("stI", [128, SW], F32, IN)
        xpT = d("xpT", [D, NTOK], F32, EI)
        hpT = d("hpT", [D, NTOK], F32, IN)
        projpT = d("projpT", [12288, NTOK], BF16, IN)
        gtokp = d("gtokp", [NTOK, 16], F32, IN)
        stAp = d("stAp", [128, SW], F32, IN)
        stBp = d("stBp", [128, SW], F32, IN)
        sel = d("sel", [128, 1], F32, EI)
    elif mode == "p1":
        hT = d("hT", [D, NTOK], F32, EO)
        projT = d("projT", [12288, NTOK], BF16, EO)
        gtok = d("gtok", [NTOK, 16], F32, EO)
        stA = d("stA", [128, SW], F32, IN)
        stB = d("stB", [128, SW], F32, EO)
    else:
        hin = d("hin", [D, NTOK], F32, EI)
        hT = d("hT", [D, NTOK], F32, IN)
        projT = d("projT", [12288, NTOK], BF16, EI)
        gtok = d("gtok", [NTOK, 16], F32, EI)
        stI = d("stI", [128, SW], F32, EI)
        stA = d("stA", [128, SW], F32, IN)
        stB = d("stB", [128, SW], F32, IN)

    def CN(t):
        return t[:, 0:8192], t[:, 8192:SW]

    if p1:
        copy_stage(P, hT, xT)
        for t in range(4):
            gmlp_stage(P, hT, t * 512, gm_w_in, gm_w_out, nm[0, :], lng, lnb, wsT, bs)
        for t in range(2):
            ffn_stage(P, hT, t * 1024, 1024, up0, dn0, nf[0, :], lname='0')
        for t in range(2):
            mlproj_stage(P, hT, t * 1024, 1024, ml_w_in, nm[1, :], bgate, projT, gtok)
    if mode == "p1":
        mlrec_stage(P, 0, 1024, False, projT, gtok, *CN(zst), *CN(stA))
        mlrec_stage(P, 1024, 1024, False, projT, gtok, *CN(stA), *CN(stB))
    if mode == "fused":
        copy_stage(P, hpT, xpT)
        for t in range(4):
            gmlp_stage(P, hpT, t * 512, gm_w_in, gm_w_out, nm[0, :], lng, lnb, wsT, bs)
        for t in range(2):
            ffn_stage(P, hpT, t * 1024, 1024, up0, dn0, nf[0, :], lname='0')
        for t in range(2):
            mlproj_stage(P, hpT, t * 1024, 1024, ml_w_in, nm[1, :], bgate, projpT, gtokp, kv_only=True)
        mlrec_stage(P, 0, 1024, False, projpT, gtokp, *CN(zst), *CN(stAp))
        mlrec_stage(P, 1024, 1024, False, projpT, gtokp, *CN(stAp), *CN(stBp))
        scale_stage(P, stBp, sel, stI)
    if mode == "p2":
        copy_stage(P, hT, hin)
    if p2:
        mlrec_stage(P, 0, 1024, True, projT, gtok, *CN(stI), *CN(stA), headg_d=headg, w_out=ml_w_out, hT=hT)
        mlrec_stage(P, 1024, 1024, True, projT, gtok, *CN(stA), *CN(stB), headg_d=headg, w_out=ml_w_out, hT=hT)
        for t in range(2):
            ffn_stage(P, hT, t * 1024, 1024, up1, dn1, nf[1, :], lname='1')
        for t in range(2):
            final_stage(P, hT, outT, t * 1024, 1024, nfin)
    return P


def _c(a):
    return np.ascontiguousarray(a, dtype=np.float32)


MODE = "fused"


def kernel(x, norm_mix, norm_ffn, gm_w_in, gm_ln_g, gm_ln_b, gm_w_s, gm_b_s, gm_w_out,
           ml_w_in, ml_b_gate, ml_head_g, ml_w_out, ffn_w_up, ffn_w_down, norm_final):
    x = np.asarray(x, dtype=np.float32)
    xs = x.reshape(NCORES, NTOK, D)
    xT = [_c(xs[c].T) for c in range(NCORES)]
    com = {"norm_mix": _c(norm_mix), "norm_ffn": _c(norm_ffn)}
    in1 = {"gm_w_in": _c(gm_w_in[0]), "gm_w_out": _c(gm_w_out[0]), "gm_ln_g": _c(gm_ln_g[0]),
           "gm_ln_b": _c(gm_ln_b[0]), "gm_w_sT": _c(np.transpose(np.asarray(gm_w_s[0]), (0, 2, 1))),
           "gm_b_s": _c(gm_b_s[0]), "ffn_up0": _c(ffn_w_up[0]), "ffn_dn0": _c(ffn_w_down[0]),
           "ml_w_in": _c(ml_w_in[0]), "ml_b_gate": _c(ml_b_gate[0]),
           "zst": np.zeros((128, SW), np.float32)}
    in2 = {"ml_head_g": _c(np.asarray(ml_head_g[0]).reshape(1, D)), "ml_w_out": _c(ml_w_out[0]),
           "ffn_up1": _c(ffn_w_up[1]), "ffn_dn1": _c(ffn_w_down[1]), "norm_final": _c(norm_final)}
    if MODE == "fused":
        P = build("fused")
        maps = []
        for c in range(NCORES):
            sel = np.full((128, 1), float(c % 2), np.float32)
            m = dict(com); m.update(in1); m.update(in2); m["xT"] = xT[c]; m["sel"] = sel
            m["xpT"] = xT[c - (c % 2)]
            maps.append(m)
        res = run_bass_kernel_spmd(P.nc, maps, core_ids=list(range(NCORES)))
        outs = [r["outT"] for r in res.results]
    else:
        P1 = build("p1")
        maps = []
        for c in range(NCORES):
            m = dict(com); m.update(in1); m["xT"] = xT[c]
            maps.append(m)
        r1 = run_bass_kernel_spmd(P1.nc, maps, core_ids=list(range(NCORES))).results
        P2 = build("p2")
        maps = []
        for c in range(NCORES):
            m = dict(com); m.update(in2)
            m["hin"] = r1[c]["hT"]; m["projT"] = r1[c]["projT"]; m["gtok"] = r1[c]["gtok"]
            m["stI"] = r1[c - 1]["stB"] if c % 2 == 1 else np.zeros((128, SW), np.float32)
            maps.append(m)
        r2 = run_bass_kernel_spmd(P2.nc, maps, core_ids=list(range(NCORES))).results
        outs = [r["outT"] for r in r2]
    out = np.stack([np.asarray(o, dtype=np.float32).T for o in outs], axis=0)
    return np.ascontiguousarray(out.reshape(4, 4096, D))
```

```python
import numpy as np
from contextlib import ExitStack
import concourse.bass as bass
import concourse.mybir as mybir
from concourse.bass_utils import run_bass_kernel_spmd

F32 = mybir.dt.float32
BF16 = mybir.dt.bfloat16
AF = mybir.ActivationFunctionType
ALU = mybir.AluOpType

D = 4096
KC = 32
NTOK = 2048
EPS = 1e-6
ENGS = ("pe", "act", "dve", "pool", "sp")
WC_BLOCKS = {"gmi": 64, "gmo": 32, "up0": 128, "dn0": 128, "up1": 128, "dn1": 128, "mli": 97, "mlo": 32}
NCORES = 8


class Sem:
    def __init__(self, h, name):
        self.h = h
        self.n = 0
        self.name = name


class Prog:
    def __init__(self, num_devices=None):
        self.nc = bass.Bass("TRN2", target_bir_lowering=False, num_devices=num_devices)
        self.es = ExitStack()
        self.sems = {}
        self.stage_idx = 0
        self.barrier = Sem(None, "barrier")
        self.bar = []
        self.wc = {}
        self.wc_filled = set()
        self.use_cache = True
        self.dram = {}
        E = self.es.enter_context
        nc = self.nc
        self.scrA = E(nc.sbuf_tensor("scrA", [128, 2], F32))
        self.scrD = E(nc.sbuf_tensor("scrD", [128, 2], F32))
        self.scrP = E(nc.sbuf_tensor("scrP", [128, 2], F32))

    def getsem(self, name):
        if name not in self.sems:
            self.sems[name] = Sem(self.es.enter_context(self.nc.semaphore(name)), name)
        return self.sems[name]

    def cache_ap(self, key, nblk_hint=None):
        name, blk = key
        if name not in self.wc:
            t = self.nc.dram_tensor("wc_" + name, [WC_BLOCKS[name], 128, 4096], BF16, kind="Internal")
            self.wc[name] = t.ap()
        return self.wc[name][blk, :, :].rearrange("p (k n) -> p k n", n=128)

    def dt(self, name, shape, dtype, kind):
        t = self.nc.dram_tensor(name, list(shape), dtype, kind=kind)
        self.dram[name] = t.ap()
        return self.dram[name]


class Stage:
    def __init__(self, P, name):
        self.P = P
        self.nc = P.nc
        self.name = f"s{P.stage_idx}{name}"
        P.stage_idx += 1
        self.es = ExitStack()
        self.ops = {e: [] for e in ENGS}
        self.pending = []
        self.psrel = {}
        self.psnext = 0
        if P.bar:
            for e in ENGS:
                self.ops[e].append(("__wait__", (), {}, tuple(P.bar), None))

    def sb(self, name, shape, dt):
        return self.es.enter_context(self.nc.sbuf_tensor(f"{self.name}_{name}", list(shape), dt))

    def psum(self, name, shape, dt=F32):
        return self.es.enter_context(self.nc.psum_tensor(f"{self.name}_{name}", list(shape), dt))

    def sem(self, name):
        return Sem(None, name)

    def op(self, eng, fname, *args, waits=(), inc=None, **kw):
        waits = tuple(w for w in waits if w is not None)
        if inc is not None:
            sem = self.P.getsem(inc[0].name.split("__")[0] + "__" + eng)
            inc = (sem, inc[1])
        self.ops[eng].append((fname, args, kw, waits, inc))
        if inc is not None:
            inc[0].n += inc[1]
            return (inc[0], inc[0].n)
        return None

    def close(self):
        P = self.P
        bs = P.barrier
        b1 = self.op("act", "activation", out=P.scrA[:, 0:1], in_=P.scrA[:, 1:2], func=AF.Copy,
                     waits=self.pending, inc=(bs, 1))
        b2 = self.op("dve", "memset", P.scrD[:, 0:1], 0.0, inc=(bs, 1))
        b3 = self.op("pool", "memset", P.scrP[:, 0:1], 0.0, inc=(bs, 1))
        P.bar = [b1, b2, b3]
        nc = self.nc
        ops = self.ops

        def mk(engname):
            def f(eng):
                for (fname, args, kw, waits, inc) in ops[engname]:
                    for (sem, val) in waits:
                        eng.wait_ge(sem.h, val)
                    if fname == "__wait__":
                        continue
                    ins = getattr(eng, fname)(*args, **kw)
                    if inc is not None:
                        ins.then_inc(inc[0].h, inc[1])
            return f

        with nc.Block() as block:
            block.tensor(mk("pe"))
            block.scalar(mk("act"))
            block.vector(mk("dve"))
            block.gpsimd(mk("pool"))
            block.sync(mk("sp"))
        self.es.close()


class WStream:
    def __init__(self, S, views, NW=3):
        self.S = S
        self.views = views
        self.NW = NW
        self.bufs = [S.sb(f"wb{i}", [128, KC, 128], BF16) for i in range(NW)]
        self.wsem = [S.sem(f"wsem{i}") for i in range(NW)]
        self.loaded = [None] * len(views)
        self.wst = [S.sem(f"wst{i}") for i in range(NW)]
        self.stored = [None] * NW
        self.rel = [None] * NW
        self.next_load = 0
        self.next_use = 0

    def _load(self, j):
        s = j % self.NW
        S = self.S
        P = S.P
        v, ncols, key = self.views[j]
        waits = [self.rel[s], self.stored[s]]
        if key is not None and P.use_cache and key in P.wc_filled:
            self.loaded[j] = S.op("sp", "dma_start", out=self.bufs[s][:, :, :], in_=P.cache_ap(key),
                                  waits=waits, inc=(self.wsem[s], 16))
            self.stored[s] = None
            return
        ld = S.op("pool", "dma_start", out=self.bufs[s][:, :, 0:ncols], in_=v, waits=waits, inc=(self.wsem[s], 16))
        self.loaded[j] = ld
        self.stored[s] = None
        if key is not None and P.use_cache and ncols == 128:
            st = S.op("sp", "dma_start", out=P.cache_ap(key), in_=self.bufs[s][:, :, :], waits=[ld],
                      inc=(self.wst[s], 16))
            self.stored[s] = st
            S.pending.append(st)
            P.wc_filled.add(key)

    def take(self):
        j = self.next_use
        while self.next_load < min(len(self.views), j + self.NW):
            self._load(self.next_load)
            self.next_load += 1
        self.next_use += 1
        return self.bufs[j % self.NW], self.loaded[j], j

    def release(self, j, sv):
        self.rel[j % self.NW] = sv


def wview(W, r0, c0, ncols=128, key=None):
    return (W[r0:r0 + D, c0:c0 + ncols].rearrange("(kc p) n -> p kc n", p=128), ncols, key)


def gemm_ws(S, ws, nblk, rhs, T, epi, ps, first_waits=(), mcols=128):
    pedone = S.sem("pedone")
    nslots = 4096 // T
    ntt = T // 512
    last = None
    for c in range(nblk):
        buf, wwait, j = ws.take()
        slot = S.psnext % nslots
        S.psnext += 1
        waits = [wwait, S.psrel.get(slot)] + (list(first_waits) if c == 0 else [])
        for kc in range(KC):
            for tt in range(ntt):
                lastmm = (kc == KC - 1 and tt == ntt - 1)
                v = S.op("pe", "matmul", out=ps[0:mcols, slot * T + tt * 512: slot * T + (tt + 1) * 512],
                         lhsT=buf[:, kc, 0:mcols], rhs=rhs[:, kc, tt * 512:(tt + 1) * 512],
                         start=(kc == 0), stop=(kc == KC - 1),
                         waits=waits if (kc == 0 and tt == 0) else (),
                         inc=(pedone, 1) if lastmm else None)
        ws.release(j, v)
        S.psrel[slot] = epi(c, ps[0:mcols, slot * T:(slot + 1) * T], v)
        last = S.psrel[slot]
    return last


def consts_basic(S):
    c = {}
    c["ones_bf"] = S.sb("ones_bf", [128, 128], BF16)
    c["eps"] = S.sb("eps_t", [128, 1], F32)
    cs = S.sem("cset")
    S.op("pool", "memset", c["ones_bf"][:, :], 1.0)
    c["rdy"] = S.op("pool", "memset", c["eps"][:, :], EPS, inc=(cs, 1))
    return c


def load_vec(S, dst, src_ap, eng="sp"):
    sm = S.sem("vecld")
    return S.op(eng, "dma_start", out=dst[:, :], in_=src_ap.rearrange("(c p) -> p c", p=128),
                allow_slow_non_contiguous=True, inc=(sm, 16))


def norm_pass(S, cst, hT, t0, T, gvec, gwait, xb, ps, want_sq_only=False):
    xf = [S.sb(f"xf{i}", [128, T], F32) for i in range(2)]
    sq = [S.sb(f"sq{i}", [128, T], BF16) for i in range(2)]
    srt = S.sb("srt", [128, T], F32)
    rstd = S.sb("rstd", [128, T], F32)
    xld = [S.sem(f"xld{i}") for i in range(2)]
    sqrdy = S.sem("sqrdy")
    sqfree_s = S.sem("sqfree")
    nrm = S.sem("nrm")
    slot = S.psnext % (4096 // T)
    S.psnext += 1
    pss = ps[:, slot * T:(slot + 1) * T]
    xfree = [None, None]
    sqfree = [None, None]
    ntt = T // 512
    v_sq = None
    for fc in range(KC):
        i = fc % 2
        v_ld = S.op("sp", "dma_start", out=xf[i][:, :], in_=hT[fc * 128:(fc + 1) * 128, t0:t0 + T],
                    waits=[xfree[i]], inc=(xld[i], 16))
        if not want_sq_only:
            S.op("act", "activation", out=xb[:, fc, :], in_=xf[i][:, :], func=AF.Copy,
                 scale=gvec[:, fc:fc + 1], waits=[v_ld, gwait if fc == 0 else None])
        v_sq = S.op("act", "activation", out=sq[i][:, :], in_=xf[i][:, :], func=AF.Square,
                    waits=[v_ld, sqfree[i]], inc=(sqrdy, 1))
        xfree[i] = v_sq
        for tt in range(ntt):
            w = [v_sq] if tt == 0 else []
            if fc == 0 and tt == 0:
                w += [S.psrel.get(slot), cst["rdy"]]
            v_pe = S.op("pe", "matmul", out=pss[:, tt * 512:(tt + 1) * 512], lhsT=cst["ones_bf"][:, :],
                        rhs=sq[i][:, tt * 512:(tt + 1) * 512], start=(fc == 0), stop=(fc == KC - 1),
                        waits=w, inc=(sqfree_s, 1) if tt == ntt - 1 else None)
        sqfree[i] = v_pe
    v1 = S.op("act", "activation", out=srt[:, :], in_=pss, func=AF.Sqrt, scale=1.0 / D,
              bias=cst["eps"][:, 0:1], waits=[v_pe], inc=(nrm, 1))
    S.psrel[slot] = v1
    v2 = S.op("dve", "reciprocal", out=rstd[:, :], in_=srt[:, :], waits=[v1], inc=(nrm, 1))
    S.norm_scratch = (xf[0], xf[1], srt)
    return rstd, v2, v_sq


def ffn_stage(P, hT, t0, T, w_up, w_dn, gdram, lname='0'):
    S = Stage(P, "ffn")
    cst = consts_basic(S)
    ps = S.psum("ps", [128, 4096], F32)
    xb = S.sb("xb", [128, KC, T], BF16)
    ag = S.sb("ag", [128, KC, T], BF16)
    gv = S.sb("gv", [128, KC], F32)
    gw = load_vec(S, gv, gdram)
    rstd, rrdy, xrdy = norm_pass(S, cst, hT, t0, T, gv, gw, xb, ps)
    NG = 4
    views = []
    for g in range(NG):
        views += [wview(w_up, 0, g * D + c * 128, key=("up" + lname, g * KC + c)) for c in range(KC)]
        views += [wview(w_dn, g * D, c * 128, key=("dn" + lname, g * KC + c)) for c in range(KC)]
    ws = WStream(S, views)
    r1 = [S.sb(f"r1_{i}", [128, T], F32) for i in range(2)]
    st = [S.sb(f"st{i}", [128, T], F32) for i in range(2)]
    epi_s = S.sem("epi")
    acc_s = [S.sem(f"acc{i}") for i in range(2)]
    state = {"k": 0, "ag_done": None, "st_free": [None, None], "r1_free": [None, None], "dn_read": None}

    def up_epi(c, pv, pev):
        i = state["k"] % 2
        state["k"] += 1
        a = S.op("act", "activation", out=r1[i][:, :], in_=pv, func=AF.Relu,
                 waits=[pev, state["r1_free"][i]], inc=(epi_s, 1))
        b = S.op("dve", "tensor_tensor", out=r1[i][:, :], in0=r1[i][:, :], in1=rstd[:, :], op=ALU.mult,
                 waits=[a, rrdy], inc=(epi_s, 1))
        d = S.op("act", "activation", out=ag[:, c, :], in_=r1[i][:, :], func=AF.Square,
                 waits=[b, state["dn_read"] if c == 0 else None], inc=(epi_s, 1))
        state["r1_free"][i] = d
        state["ag_done"] = d
        return a

    def dn_epi(c, pv, pev):
        i = state["k"] % 2
        state["k"] += 1
        a = S.op("act", "activation", out=st[i][:, :], in_=pv, func=AF.Copy,
                 waits=[pev, state["st_free"][i]], inc=(epi_s, 1))
        v = S.op("pool", "dma_start", out=hT[c * 128:(c + 1) * 128, t0:t0 + T], in_=st[i][:, :],
                 accum_op=ALU.add, waits=[a], inc=(acc_s[i], 16))
        state["st_free"][i] = v
        S.pending.append(v)
        return a

    for g in range(NG):
        gemm_ws(S, ws, KC, xb, T, up_epi, ps, first_waits=[xrdy] if g == 0 else [])
        lastdn = gemm_ws(S, ws, KC, ag, T, dn_epi, ps, first_waits=[state["ag_done"]])
        state["dn_read"] = ws.rel[(ws.next_use - 1) % ws.NW]
    S.close()


def final_stage(P, hT, outT, t0, T, gdram):
    S = Stage(P, "fin")
    cst = consts_basic(S)
    ps = S.psum("ps", [128, 4096], F32)
    gv = S.sb("gv", [128, KC], F32)
    gw = load_vec(S, gv, gdram)
    rstd, rrdy, _ = norm_pass(S, cst, hT, t0, T, gv, gw, None, ps, want_sq_only=True)
    xg = [S.sb(f"xg{i}", [128, T], F32) for i in range(2)]
    ld = [S.sem(f"fld{i}") for i in range(2)]
    stq = [S.sem(f"fst{i}") for i in range(2)]
    fe = S.sem("fe")
    free = [None, None]
    for fc in range(KC):
        i = fc % 2
        v_ld = S.op("sp", "dma_start", out=xg[i][:, :], in_=hT[fc * 128:(fc + 1) * 128, t0:t0 + T],
                    waits=[free[i]], inc=(ld[i], 16))
        a = S.op("dve", "scalar_tensor_tensor", out=xg[i][:, :], in0=xg[i][:, :], scalar=gv[:, fc:fc + 1],
                 in1=rstd[:, :], op0=ALU.mult, op1=ALU.mult, waits=[v_ld, rrdy, gw], inc=(fe, 1))
        v = S.op("act", "dma_start", out=outT[fc * 128:(fc + 1) * 128, t0:t0 + T], in_=xg[i][:, :],
                 waits=[a], inc=(stq[i], 16))
        free[i] = v
        S.pending.append(v)
    S.close()


def gmlp_stage(P, hT, t0, w_in, w_out, gdram, lng_d, lnb_d, wsT_d, bs_d):
    T = 512
    NCH = 4
    S = Stage(P, "gm")
    cst = consts_basic(S)
    nc = S.nc
    ps = S.psum("ps", [128, 4096], F32)
    xb = S.sb("xb", [128, KC, T], BF16)
    uT = S.sb("uT", [128, KC, T], BF16)
    vg = S.sb("vg", [128, KC, T], BF16)
    vtk = S.sb("vtk", [128, NCH, D], BF16)
    gv = S.sb("gv", [128, KC], F32)
    lng = S.sb("lng", [128, KC], F32)
    lnb = S.sb("lnb", [128, KC], F32)
    load_vec(S, gv, gdram)
    load_vec(S, lng, lng_d)
    lw = load_vec(S, lnb, lnb_d)
    gw = lw
    ones_f = S.sb("ones_f", [128, 128], F32)
    ident_b = S.sb("ident_b", [128, 128], BF16)
    wT = S.sb("wT", [128, 8, 128], F32)
    wTb = S.sb("wTb", [128, 8, 128], BF16)
    bsrow = S.sb("bsrow", [1, 1024], F32)
    rwbs = S.sb("rwbs", [128, 2, 1024], F32)
    c2 = S.sb("c2", [128, KC, 128], F32)
    cs = S.sem("gmc")
    cl = S.sem("gmcl")
    w1 = S.op("sp", "dma_start", out=wT[:, :, :], in_=wsT_d.rearrange("h s t -> s h t"), inc=(cl, 16))
    cl2 = S.sem("gmcl2")
    w2 = S.op("sp", "dma_start", out=bsrow[:, :], in_=bs_d.rearrange("h t -> (h t)").unsqueeze(0), inc=(cl2, 16))
    S.op("pool", "memset", ones_f[:, :], 1.0)
    S.op("pool", "affine_select", out=ident_b[:, :], in_=cst["ones_bf"][:, :], pattern=[[1, 128]],
         compare_op=ALU.is_equal, fill=0.0, base=0, channel_multiplier=-1, waits=[cst["rdy"]])
    S.op("pool", "affine_select", out=wT[:, :, :], in_=wT[:, :, :], pattern=[[0, 8], [1, 128]],
         compare_op=ALU.is_ge, fill=0.0, base=0, channel_multiplier=-1, waits=[w1])
    k1 = S.op("pool", "tensor_copy", out=wTb[:, :, :], in_=wT[:, :, :], inc=(cs, 1))
    wTf = wT[:, :, :].rearrange("p h t -> p (h t)")
    for half in range(2):
        S.op("pe", "matmul", out=ps[:, 3072 + half * 512:3072 + (half + 1) * 512], lhsT=ones_f[:, :],
             rhs=wTf[:, half * 512:(half + 1) * 512], start=True, stop=True, waits=[k1] if half == 0 else [])
    for half in range(2):
        k2 = S.op("pe", "matmul", out=ps[:, 2048 + half * 512:2048 + (half + 1) * 512], lhsT=ones_f[0:1, :],
                  rhs=bsrow[0:1, half * 512:(half + 1) * 512], start=True, stop=True,
                  waits=[w2] if half == 0 else [], inc=(cs, 1) if half == 1 else None)
    S.op("act", "activation", out=rwbs[:, 0, :], in_=ps[:, 3072:4096], func=AF.Copy, waits=[k2])
    k3 = S.op("act", "activation", out=rwbs[:, 1, :], in_=ps[:, 2048:3072], func=AF.Copy, inc=(cs, 1))
    for s_ in (4, 5, 6, 7):
        S.psrel[s_] = k3
    for fc in range(KC):
        h = fc // 4
        k4 = S.op("dve", "scalar_tensor_tensor", out=c2[:, fc, :], in0=rwbs[:, 0, h * 128:(h + 1) * 128],
                  scalar=lnb[:, fc:fc + 1], in1=rwbs[:, 1, h * 128:(h + 1) * 128], op0=ALU.mult, op1=ALU.add,
                  waits=[k3, lw] if fc == 0 else [], inc=(cs, 1) if fc == KC - 1 else None)
    c2rdy = k4
    rstd, rrdy, xrdy = norm_pass(S, cst, hT, t0, T, gv, gw, xb, ps)
    views = [wview(w_in, 0, c * 128, key=("gmi", c)) for c in range(KC)]
    views += [wview(w_in, 0, D + c * 128, key=("gmi", KC + c)) for c in range(KC)]
    views += [wview(w_out, 0, c * 128, key=("gmo", c)) for c in range(KC)]
    ws = WStream(S, views, NW=2)
    tmp = [S.sb(f"tmp{i}", [128, T], F32) for i in range(2)]
    epi_s = S.sem("epi")
    state = {"k": 0, "tfree": [None, None], "last": None}

    def gelu_epi(dst):
        def epi(c, pv, pev):
            i = state["k"] % 2
            state["k"] += 1
            a = S.op("dve", "tensor_tensor", out=tmp[i][:, :], in0=pv, in1=rstd[:, :], op=ALU.mult,
                     waits=[pev, rrdy, state["tfree"][i]], inc=(epi_s, 1))
            b = S.op("act", "activation", out=dst[:, c, :], in_=tmp[i][:, :], func=AF.Gelu,
                     waits=[a], inc=(epi_s, 1))
            state["tfree"][i] = b
            state["last"] = b
            return a
        return epi

    gemm_ws(S, ws, KC, xb, T, gelu_epi(uT), ps, first_waits=[xrdy])
    gemm_ws(S, ws, KC, xb, T, gelu_epi(vg), ps)
    vdone = state["last"]
    xb_read = ws.rel[(ws.next_use - 1) % ws.NW]
    sqv = [S.sb(f"sqv{i}", [128, T], BF16) for i in range(2)]
    lns = S.sem("lns")
    lnp = S.sem("lnp")
    slot1 = S.psnext % 8
    S.psnext += 1
    slot2 = S.psnext % 8
    S.psnext += 1
    p1 = ps[:, slot1 * T:(slot1 + 1) * T]
    p2 = ps[:, slot2 * T:(slot2 + 1) * T]
    sfree = [None, None]
    for fc in range(KC):
        i = fc % 2
        a = S.op("dve", "tensor_tensor", out=sqv[i][:, :], in0=vg[:, fc, :], in1=vg[:, fc, :], op=ALU.mult,
                 waits=[sfree[i], vdone if fc == 0 else None], inc=(lns, 1))
        S.op("pe", "matmul", out=p1, lhsT=cst["ones_bf"][:, :], rhs=vg[:, fc, :], start=(fc == 0),
             stop=(fc == KC - 1), waits=[vdone, S.psrel.get(slot1), S.psrel.get(slot2)] if fc == 0 else [])
        b = S.op("pe", "matmul", out=p2, lhsT=cst["ones_bf"][:, :], rhs=sqv[i][:, :], start=(fc == 0),
                 stop=(fc == KC - 1), waits=[a], inc=(lnp, 1))
        sfree[i] = b
    mean, var, lrs = S.norm_scratch
    a = S.op("act", "activation", out=mean[:, :], in_=p1, func=AF.Copy, scale=1.0 / D, waits=[b], inc=(lns, 1))
    a2 = S.op("dve", "tensor_tensor", out=var[:, :], in0=mean[:, :], in1=mean[:, :], op=ALU.mult, waits=[a], inc=(lns, 1))
    a3 = S.op("dve", "scalar_tensor_tensor", out=var[:, :], in0=p2, scalar=1.0 / D, in1=var[:, :],
              op0=ALU.mult, op1=ALU.subtract, waits=[a2], inc=(lns, 1))
    S.psrel[slot1] = a3
    S.psrel[slot2] = a3
    a4 = S.op("act", "activation", out=var[:, :], in_=var[:, :], func=AF.Sqrt, bias=cst["eps"][:, 0:1],
              waits=[a3], inc=(lns, 1))
    a5 = S.op("dve", "reciprocal", out=lrs[:, :], in_=var[:, :], waits=[a4], inc=(lns, 1))
    vn = xb
    trs = S.sem("trs")
    trp = S.sem("trp")
    mixp = S.sem("mixp")
    tcp = S.sem("tcp")
    gts = S.sem("gts")
    pst = S.psum
    psb = ps.bitcast(BF16) if hasattr(ps, "bitcast") else None
    for fc in range(KC):
        a = S.op("dve", "tensor_tensor", out=tmp[0][:, :], in0=vg[:, fc, :], in1=mean[:, :], op=ALU.subtract,
                 waits=[a5, xb_read, state["tfree"][0]] if fc == 0 else [state.get("vnw")], inc=(trs, 1))
        state["vnw"] = S.op("dve", "tensor_tensor", out=vn[:, fc, :], in0=tmp[0][:, :], in1=lrs[:, :], op=ALU.mult,
                            waits=[a], inc=(trs, 1))
    vn_done = state["vnw"]
    pstr = ps[:, :].bitcast(BF16)
    cnt = 0
    for n in range(NCH):
        for gq in range(8):
            slot = S.psnext % 8
            S.psnext += 1
            for q in range(4):
                fc = gq * 4 + q
                v = S.op("pe", "transpose", out=pstr[:, slot * 1024 + q * 128: slot * 1024 + (q + 1) * 128],
                         in_=vn[:, fc, n * 128:(n + 1) * 128], identity=ident_b[:, :],
                         waits=[vn_done, S.psrel.get(slot), k1] if q == 0 else [],
                         inc=(trp, 1) if q == 3 else None)
            eng = "act" if cnt % 2 == 0 else "dve"
            if eng == "act":
                e = S.op("act", "activation", out=vtk[:, n, gq * 512:(gq + 1) * 512],
                         in_=pstr[:, slot * 1024: slot * 1024 + 512], func=AF.Copy, waits=[v], inc=(tcp, 1))
            else:
                e = S.op("dve", "tensor_copy", out=vtk[:, n, gq * 512:(gq + 1) * 512],
                         in_=pstr[:, slot * 1024: slot * 1024 + 512], waits=[v], inc=(tcp, 1))
            S.psrel[slot] = e
            cnt += 1
            state["vtk_a" if eng == "act" else "vtk_d"] = e
    vtk_done = [state["vtk_a"], state["vtk_d"]]
    for n in range(NCH):
        for gq in range(8):
            slot = S.psnext % 8
            S.psnext += 1
            h = gq
            for q in range(4):
                fc = gq * 4 + q
                v = S.op("pe", "matmul", out=ps[:, slot * 512 + q * 128: slot * 512 + (q + 1) * 128],
                         lhsT=vtk[:, n, fc * 128:(fc + 1) * 128], rhs=wTb[:, h, :], start=True, stop=True,
                         waits=(vtk_done + [S.psrel.get(slot)]) if q == 0 else [],
                         inc=(mixp, 1) if q == 3 else None)
            for q in range(4):
                fc = gq * 4 + q
                a = S.op("dve", "scalar_tensor_tensor", out=tmp[1][:, q * 128:(q + 1) * 128],
                         in0=ps[:, slot * 512 + q * 128: slot * 512 + (q + 1) * 128], scalar=lng[:, fc:fc + 1],
                         in1=c2[:, fc, :], op0=ALU.mult, op1=ALU.add,
                         waits=[v, c2rdy, state["tfree"][1], state.get("gt")] if q == 0 else [],
                         inc=(gts, 1) if q == 3 else None)
            S.psrel[slot] = a
            uview = uT[:, gq * 4:(gq + 1) * 4, n * 128:(n + 1) * 128]
            state["gt"] = S.op("dve", "tensor_tensor", out=uview, in0=uview,
                               in1=tmp[1][:, :].rearrange("p (q t) -> p q t", q=4), op=ALU.mult, waits=[a], inc=(gts, 1))
    gated_done = state["gt"]
    st = tmp
    acc_s = [S.sem(f"acc{i}") for i in range(2)]
    st_free = [None, None]

    def out_epi(c, pv, pev):
        i = c % 2
        a = S.op("act", "activation", out=st[i][:, :], in_=pv, func=AF.Copy, waits=[pev, st_free[i]], inc=(epi_s, 1))
        v = S.op("pool", "dma_start", out=hT[c * 128:(c + 1) * 128, t0:t0 + T], in_=st[i][:, :],
                 accum_op=ALU.add, waits=[a], inc=(acc_s[i], 16))
        st_free[i] = v
        S.pending.append(v)
        return a

    gemm_ws(S, ws, KC, uT, T, out_epi, ps, first_waits=[gated_done])
    S.close()


def copy_stage(P, dst, src):
    S = Stage(P, "cp")
    cs = S.sem("cpy")
    for q in range(8):
        v = S.op("sp", "dma_start", out=dst[q * 512:(q + 1) * 512, :], in_=src[q * 512:(q + 1) * 512, :], inc=(cs, 16))
    S.pending.append(v)
    S.close()


def mlproj_stage(P, hT, t0, T, w_in, gdram, bgate_d, projT, gates_tok, kv_only=False):
    S = Stage(P, "mlp")
    cst = consts_basic(S)
    ps = S.psum("ps", [128, 4096], F32)
    xb = S.sb("xb", [128, KC, T], BF16)
    gv = S.sb("gv", [128, KC], F32)
    gw = load_vec(S, gv, gdram)
    ident_f = S.sb("ident_f", [128, 128], F32)
    ones_f = S.sb("ones_f", [128, 128], F32)
    S.op("pool", "memset", ones_f[:, :], 1.0)
    cs = S.sem("cset")
    idr = S.op("pool", "affine_select", out=ident_f[:, :], in_=ones_f[:, :], pattern=[[1, 128]],
               compare_op=ALU.is_equal, fill=0.0, base=0, channel_multiplier=-1, waits=[cst["rdy"]], inc=(cs, 1))
    rstd, rrdy, xrdy = norm_pass(S, cst, hT, t0, T, gv, gw, xb, ps)
    blocks = list(range(16, 64)) if kv_only else list(range(96))
    views = [wview(w_in, 0, c * 128, key=("mli", c)) for c in blocks] + [wview(w_in, 0, 12304 - 128, key=("mli", 96))]
    ws = WStream(S, views)
    st = [S.sb(f"st{i}", [128, T], BF16) for i in range(2)]
    tmp = [S.sb(f"tmp{i}", [128, T], F32) for i in range(2)]
    epi_s = S.sem("epi")
    sts = [S.sem(f"pst{i}") for i in range(2)]
    state = {"k": 0, "stfree": [None, None], "tfree": [None, None]}

    def epi(ci, pv, pev):
        c = blocks[ci]
        i = state["k"] % 2
        state["k"] += 1
        if c < 64:
            sc = 0.0625 if c < 16 else 1.0
            a = S.op("dve", "scalar_tensor_tensor", out=st[i][:, :], in0=pv, scalar=sc, in1=rstd[:, :],
                     op0=ALU.mult, op1=ALU.mult, waits=[pev, rrdy, state["stfree"][i]], inc=(epi_s, 1))
            rel = a
            fin = a
        else:
            a = S.op("dve", "tensor_tensor", out=tmp[i][:, :], in0=pv, in1=rstd[:, :], op=ALU.mult,
                     waits=[pev, rrdy, state["tfree"][i]], inc=(epi_s, 1))
            fin = S.op("act", "activation", out=st[i][:, :], in_=tmp[i][:, :], func=AF.Sigmoid,
                       waits=[a, state["stfree"][i]], inc=(epi_s, 1))
            state["tfree"][i] = fin
            rel = a
        v = S.op("sp", "dma_start", out=projT[c * 128:(c + 1) * 128, t0:t0 + T], in_=st[i][:, :],
                 waits=[fin], inc=(sts[i], 16))
        state["stfree"][i] = v
        S.pending.append(v)
        return rel

    gemm_ws(S, ws, len(blocks), xb, T, epi, ps, first_waits=[xrdy])
    nch = T // 128
    gpe = S.sem("gpe")
    gac = S.sem("gac")
    gdv = S.sem("gdv")
    gpo = S.sem("gpo")
    buf, wwait, j = ws.take()
    slot = S.psnext % (4096 // T)
    S.psnext += 1
    psg = ps[:, slot * T: slot * T + nch * 16]
    for n in range(nch):
        for kc in range(KC):
            v = S.op("pe", "matmul", out=psg[:, n * 16:(n + 1) * 16], lhsT=xb[:, kc, n * 128:(n + 1) * 128],
                     rhs=buf[:, kc, 112:128], start=(kc == 0), stop=(kc == KC - 1),
                     waits=[wwait, S.psrel.get(slot)] if (n == 0 and kc == 0) else [],
                     inc=(gpe, 1) if (n == nch - 1 and kc == KC - 1) else None)
    ws.release(j, v)
    gdone = v
    slot2 = S.psnext % (4096 // T)
    S.psnext += 1
    pst = ps[:, slot2 * T: slot2 * T + nch * 128].rearrange("p (n k) -> p n k", k=128)
    for n in range(nch):
        tv = S.op("pe", "transpose", out=pst[:, n, :], in_=rstd[:, n * 128:(n + 1) * 128], identity=ident_f[:, :],
                  waits=[rrdy, idr, S.psrel.get(slot2)] if n == 0 else [], inc=(gpe, 1) if n == nch - 1 else None)
    rt = S.sb("rt", [128, nch], F32)
    a0 = S.op("act", "activation", out=rt[:, :], in_=pst[:, :, 0], func=AF.Copy, waits=[tv], inc=(gac, 1))
    S.psrel[slot2] = a0
    bgrow = S.sb("bgrow", [1, 16], F32)
    bgl2 = S.sem("bgl2")
    bw = S.op("sp", "dma_start", out=bgrow[:, :], in_=bgate_d.unsqueeze(0), inc=(bgl2, 16))
    slot3 = S.psnext % (4096 // T)
    S.psnext += 1
    pbg = ps[:, slot3 * T: slot3 * T + 16]
    bm = S.op("pe", "matmul", out=pbg, lhsT=ones_f[0:1, :], rhs=bgrow[0:1, :], start=True, stop=True,
              waits=[bw, idr, S.psrel.get(slot3)], inc=(gpe, 1))
    bgbc = S.sb("bgbc", [128, 16], F32)
    b0 = S.op("act", "activation", out=bgbc[:, :], in_=pbg, func=AF.Copy, waits=[bm], inc=(gac, 1))
    S.psrel[slot3] = b0
    gt = S.sb("gt", [128, nch, 16], F32)
    cap = S.sb("cap", [128, nch, 16], F32)
    lp = S.sb("lp", [128, nch, 16], F32)
    one1 = S.sb("one1", [128, 1], F32)
    o1 = S.op("pool", "memset", one1[:, :], 1.0, inc=(gpo, 1))
    a = None
    graw = S.sb("graw", [128, nch, 16], F32)
    ev = S.op("act", "activation", out=graw[:, :, :], in_=psg.rearrange("p (n k) -> p n k", k=16), func=AF.Copy,
              waits=[gdone, a0], inc=(gac, 1))
    S.psrel[slot] = ev
    a = None
    for n in range(nch):
        a = S.op("dve", "scalar_tensor_tensor", out=gt[:, n, :], in0=graw[:, n, :], scalar=rt[:, n:n + 1],
                 in1=bgbc[:, :], op0=ALU.mult, op1=ALU.add, waits=[ev, b0] if n == 0 else [a], inc=(gdv, 1))
    c1 = S.op("act", "activation", out=cap[:, :, :], in_=gt[:, :, :], func=AF.Tanh, scale=1.0 / 15.0, waits=[a], inc=(gac, 1))
    c2 = S.op("act", "activation", out=lp[:, :, :], in_=cap[:, :, :], func=AF.Exp, scale=-15.0, waits=[c1], inc=(gac, 1))
    c3 = S.op("act", "activation", out=lp[:, :, :], in_=lp[:, :, :], func=AF.Ln, bias=one1[:, 0:1], waits=[c2, o1], inc=(gac, 1))
    gtk = S.sb("gtk", [128, nch, 16], F32)
    d1 = S.op("dve", "tensor_scalar", out=gtk[:, :, 0:8], in0=cap[:, :, 0:8], scalar1=15.0, scalar2=None, op0=ALU.mult,
              waits=[c3], inc=(gdv, 1))
    d2 = S.op("dve", "tensor_scalar", out=gtk[:, :, 8:16], in0=lp[:, :, 8:16], scalar1=-1.0, scalar2=None, op0=ALU.mult,
              waits=[d1], inc=(gdv, 1))
    gl = S.sem("gld")
    v = S.op("sp", "dma_start", out=gates_tok[t0:t0 + T, :].rearrange("(n p) g -> p n g", p=128), in_=gtk[:, :, :],
             waits=[d2], inc=(gl, 16))
    S.pending.append(v)
    S.close()


def mlrec_stage(P, t0, T, full, projT, gates_tok, stC_in, stn_in, stC_out, stn_out,
                headg_d=None, w_out=None, hT=None):
    S = Stage(P, "mlr" if full else "mls")
    cst = consts_basic(S)
    nch = T // 128
    psA = S.psum("psA", [128, 512], F32)
    psNum = S.psum("psNum", [128, 512], F32)
    psQc = S.psum("psQc", [128, 512], F32)
    psC = S.psum("psC", [128, 1024], F32)
    psD = S.psum("psD", [128, 512], F32)
    psT = S.psum("psT", [128, 1024], F32)
    pT = psT[:, :].bitcast(BF16)
    ones_f = S.sb("ones_f", [128, 128], F32)
    ident_f = S.sb("ident_f", [128, 128], F32)
    ident_b = S.sb("ident_b", [128, 128], BF16)
    Utri = S.sb("Utri", [128, 128], F32)
    negm = S.sb("negm", [128, 128], F32)
    zer = S.sb("zer", [128, 128], F32)
    cs = S.sem("cset")
    S.op("pool", "memset", ones_f[:, :], 1.0)
    S.op("pool", "memset", zer[:, :], 0.0)
    S.op("pool", "affine_select", out=ident_f[:, :], in_=ones_f[:, :], pattern=[[1, 128]],
         compare_op=ALU.is_equal, fill=0.0, base=0, channel_multiplier=-1, waits=[cst["rdy"]])
    S.op("pool", "affine_select", out=ident_b[:, :], in_=cst["ones_bf"][:, :], pattern=[[1, 128]],
         compare_op=ALU.is_equal, fill=0.0, base=0, channel_multiplier=-1)
    S.op("pool", "affine_select", out=Utri[:, :], in_=ones_f[:, :], pattern=[[1, 128]],
         compare_op=ALU.is_ge, fill=0.0, base=0, channel_multiplier=-1)
    crdy = S.op("pool", "affine_select", out=negm[:, :], in_=zer[:, :], pattern=[[1, 128]],
                compare_op=ALU.is_ge, fill=-30000.0, base=0, channel_multiplier=-1, inc=(cs, 1))
    gtk = S.sb("gtk", [128, nch, 16], F32)
    lf = S.sb("lf", [128, nch * 8], F32)
    bcum = S.sb("bcum", [128, nch * 8], F32)
    Gbc = S.sb("Gbc", [128, nch * 8], F32)
    bias_s = S.sb("bias_s", [128, nch * 8], F32)
    ea = S.sb("ea", [128, nch * 8], F32)
    etok = S.sb("etok", [128, nch * 8], F32)
    eg = S.sb("eg", [128, nch * 8], F32)
    gl = S.sem("gld")
    gp = S.sem("gpre")
    v = S.op("sp", "dma_start", out=gtk[:, :, :], in_=gates_tok[t0:t0 + T, :].rearrange("(n p) g -> p n g", p=128),
             inc=(gl, 16))
    lf3 = lf[:, :].rearrange("p (n h) -> p n h", h=8)
    a = S.op("dve", "tensor_copy", out=lf3, in_=gtk[:, :, 8:16], waits=[v], inc=(gp, 1))
    NH = nch * 8
    S.op("pe", "matmul", out=psA[:, 256:256 + NH], lhsT=Utri[:, :], rhs=lf[:, :], start=True, stop=True, waits=[a, crdy])
    b = S.op("pe", "matmul", out=psA[:, 320:320 + NH], lhsT=ones_f[:, :], rhs=lf[:, :], start=True, stop=True, inc=(gp, 1))
    c1 = S.op("act", "activation", out=bcum[:, :], in_=psA[:, 256:256 + NH], func=AF.Copy, waits=[b], inc=(gp, 1))
    c2 = S.op("act", "activation", out=Gbc[:, :], in_=psA[:, 320:320 + NH], func=AF.Copy, waits=[c1], inc=(gp, 1))
    d1 = S.op("dve", "tensor_tensor", out=bias_s[:, :].rearrange("p (n h) -> p n h", h=8), in0=gtk[:, :, 0:8],
              in1=bcum[:, :].rearrange("p (n h) -> p n h", h=8), op=ALU.subtract, waits=[c2], inc=(gp, 1))
    d2 = S.op("dve", "tensor_tensor", out=ea[:, :], in0=bias_s[:, :], in1=Gbc[:, :], op=ALU.add, waits=[d1], inc=(gp, 1))
    e1 = S.op("act", "activation", out=ea[:, :], in_=ea[:, :], func=AF.Exp, waits=[d2], inc=(gp, 1))
    e2 = S.op("act", "activation", out=etok[:, :], in_=bcum[:, :], func=AF.Exp, waits=[e1], inc=(gp, 1))
    grdy = S.op("act", "activation", out=eg[:, :], in_=Gbc[:, :], func=AF.Exp, waits=[e2], inc=(gp, 1))
    ncols = 12 if full else 6
    hb = S.sb("hb", [128, ncols, T], BF16)
    Cst = S.sb("Cst", [128, 1024], F32)
    Cb = S.sb("Cb", [128, 1024], BF16)
    nst = S.sb("nst", [128, 2], F32)
    nb = S.sb("nb", [128, 2], BF16)
    kw = [S.sb(f"kw{i}", [128, 256], BF16) for i in range(2)]
    vtok = [S.sb(f"vtok{i}", [128, 512], BF16) for i in range(2)]
    if full:
        hsT = S.sb("hsT", [128, KC, T], BF16)
        otok = [S.sb(f"otok{i}", [128, 512], BF16) for i in range(2)]
        diagb = [S.sb(f"diagb{i}", [128, 128], F32) for i in range(2)]
        DT = [S.sb(f"DT{i}", [128, 128], F32) for i in range(2)]
        PT = [S.sb(f"PT{i}", [128, 128], BF16) for i in range(2)]
        intra = [S.sb(f"intra{i}", [128, 512], F32) for i in range(2)]
        num = [S.sb(f"num{i}", [128, 512], F32) for i in range(2)]
        junk = S.sb("junk", [128, 512], F32)
        go = [S.sb(f"go{i}", [128, 512], F32) for i in range(2)]
        hs = [S.sb(f"hs{i}", [128, 512], BF16) for i in range(2)]
        sm = [S.sb(f"sm{i}", [128, 8], F32) for i in range(2)]
        hg = S.sb("hg", [128, D], F32)
        hgl = S.sem("hgl")
        hgw = S.op("sp", "dma_start", out=hg[:, :], in_=headg_d[0, :].partition_broadcast(128), inc=(hgl, 16))
    hl = S.sem("hld")
    cl = S.sem("cld")
    cst_s = S.sem("cstore")
    pe_s = S.sem("rpe")
    ac_s = S.sem("ract")
    dv_s = S.sem("rdve")
    po_s = S.sem("rpool")
    prev = {"pe_step": None, "cstore": None, "hb_read": None, "psT_k": None, "psT_v": None, "psT_o": None,
            "psT_h": None, "psA_b": None, "psA_s": None, "psNum": None, "psQc": None, "psC": None, "psD": None,
            "cupd": None, "cb": None, "hsTcopy": None}
    step = 0
    for h in range(8):
        lw = []
        if full:
            rows = [(h * 256, 2, 0), (2048 + h * 256, 2, 2), (4096 + h * 512, 4, 4), (8192 + h * 512, 4, 8)]
        else:
            rows = [(2048 + h * 256, 2, 0), (4096 + h * 512, 4, 2)]
        for (r0, nck, dst) in rows:
            lw.append(S.op("sp", "dma_start", out=hb[:, dst:dst + nck, :],
                           in_=projT[r0:r0 + nck * 128, t0:t0 + T].rearrange("(j p) t -> p j t", p=128),
                           waits=[prev["hb_read"]], inc=(hl, 16)))
        hbw = [lw[-1]]
        cw1 = S.op("sp", "dma_start", out=Cst[:, :], in_=stC_in[:, h * 1024:(h + 1) * 1024],
                   waits=[prev["cstore"]], inc=(cl, 16))
        cw = S.op("sp", "dma_start", out=nst[:, :], in_=stn_in[:, h * 2:(h + 1) * 2], inc=(cl, 16))
        if full:
            prev["cb"] = S.op("act", "activation", out=Cb[:, :], in_=Cst[:, :], func=AF.Copy, waits=[cw, prev["pe_step"]], inc=(ac_s, 1))
            prev["cb"] = S.op("act", "activation", out=nb[:, :], in_=nst[:, :], func=AF.Copy, inc=(ac_s, 1))
        if full:
            qo, ko, vo, oo = 0, 2, 4, 8
        else:
            ko, vo = 0, 2
        for n in range(nch):
            par = step % 2
            ch = n * 8 + h
            tk = slice(n * 128, (n + 1) * 128)
            first = (n == 0)
            for j in range(2):
                S.op("pe", "transpose", out=pT[:, j * 128:(j + 1) * 128], in_=hb[:, ko + j, tk], identity=ident_b[:, :],
                     waits=(hbw + [prev["psT_k"], prev["psT_v"], crdy]) if j == 0 else [])
            for j in range(4):
                tv = S.op("pe", "transpose", out=pT[:, 256 + j * 128:256 + (j + 1) * 128], in_=hb[:, vo + j, tk],
                          identity=ident_b[:, :], waits=[prev["psT_v"]] if j == 0 else [],
                          inc=(pe_s, 1) if j == 3 else None)
            if full:
                for j in range(4):
                    to = S.op("pe", "transpose", out=pT[:, 1024 + j * 128:1024 + (j + 1) * 128], in_=hb[:, oo + j, tk],
                              identity=ident_b[:, :], waits=[prev["psT_o"], prev["psT_h"]] if j == 0 else [],
                              inc=(pe_s, 1) if j == 3 else None)
            kwv = S.op("dve", "tensor_scalar", out=kw[par][:, :], in0=pT[:, 0:256], scalar1=ea[:, ch:ch + 1], scalar2=None,
                       op0=ALU.mult, waits=[tv, grdy, prev["pe_step"]], inc=(dv_s, 1))
            prev["psT_k"] = kwv
            vtv = S.op("act", "activation", out=vtok[par][:, :], in_=pT[:, 256:768], func=AF.Copy,
                       waits=[tv, prev["pe_step"], kwv], inc=(ac_s, 1))
            prev["psT_v"] = vtv
            if full:
                otv = S.op("act", "activation", out=otok[par][:, :], in_=pT[:, 1024:1536], func=AF.Copy,
                           waits=[to], inc=(ac_s, 1))
                prev["psT_o"] = otv
                gov = S.op("pool", "tensor_tensor", out=go[par][:, :], in0=otok[par][:, :], in1=hg[:, h * 512:(h + 1) * 512],
                           op=ALU.mult, waits=[otv, hgw], inc=(po_s, 1))
                dgv = S.op("act", "activation", out=diagb[par][:, :], in_=ident_f[:, :], func=AF.Copy,
                           scale=bcum[:, ch:ch + 1], waits=[grdy], inc=(ac_s, 1))
                S.op("pe", "matmul", out=psA[:, 0:128], lhsT=ones_f[:, :], rhs=diagb[par][:, :], start=True, stop=False,
                     waits=[dgv, prev["psA_b"]])
                pbv = S.op("pe", "matmul", out=psA[:, 0:128], lhsT=ident_f[:, :], rhs=negm[:, :], start=False, stop=True,
                           inc=(pe_s, 1))
                for j in range(2):
                    psv = S.op("pe", "matmul", out=psA[:, 128:256], lhsT=hb[:, ko + j, tk], rhs=hb[:, qo + j, tk],
                               start=(j == 0), stop=(j == 1), waits=[prev["psA_s"]] if j == 0 else [],
                               inc=(pe_s, 1) if j == 1 else None)
                dtv = S.op("act", "activation", out=DT[par][:, :], in_=psA[:, 0:128], func=AF.Exp,
                           bias=bias_s[:, ch:ch + 1], waits=[pbv], inc=(ac_s, 1))
                prev["psA_b"] = dtv
                ptv = S.op("dve", "tensor_tensor", out=PT[par][:, :], in0=psA[:, 128:256], in1=DT[par][:, :], op=ALU.mult,
                           waits=[psv, dtv], inc=(dv_s, 1))
                prev["psA_s"] = ptv
                nmv = S.op("pe", "matmul", out=psNum[:, :], lhsT=PT[par][:, :], rhs=vtok[par][:, :], start=True, stop=True,
                           waits=[ptv, vtv, prev["psNum"]], inc=(pe_s, 1))
                S.op("pe", "matmul", out=psD[:, 0:1], lhsT=PT[par][:, :], rhs=cst["ones_bf"][:, 0:1], start=True, stop=True,
                     waits=[prev["psD"]])
                for j in range(2):
                    S.op("pe", "matmul", out=psQc[:, :], lhsT=hb[:, qo + j, tk], rhs=Cb[:, j * 512:(j + 1) * 512],
                         start=(j == 0), stop=(j == 1), waits=[prev["cb"], prev["psQc"]] if j == 0 else [])
                for j in range(2):
                    qcv = S.op("pe", "matmul", out=psD[:, 1:2], lhsT=hb[:, qo + j, tk], rhs=nb[:, j:j + 1],
                               start=(j == 0), stop=(j == 1), inc=(pe_s, 1) if j == 1 else None)
            for j in range(2):
                S.op("pe", "matmul", out=psC[:, j * 512:(j + 1) * 512], lhsT=kw[par][:, j * 128:(j + 1) * 128],
                     rhs=vtok[par][:, :], start=True, stop=True, waits=[kwv, vtv, prev["psC"]] if j == 0 else [])
            for j in range(2):
                pcv = S.op("pe", "matmul", out=psD[:, 2 + j:3 + j], lhsT=kw[par][:, j * 128:(j + 1) * 128],
                           rhs=cst["ones_bf"][:, 0:1], start=True, stop=True,
                           waits=[prev["psD"]] if (j == 0 and not full) else [],
                           inc=(pe_s, 1) if j == 1 else None)
            pe_last = pcv
            if full:
                inv = S.op("act", "activation", out=intra[par][:, :], in_=psNum[:, :], func=AF.Copy, waits=[nmv], inc=(ac_s, 1))
                prev["psNum"] = inv
                smx = sm[par]
                d0 = S.op("dve", "tensor_copy", out=smx[:, 0:2], in_=psD[:, 0:2], waits=[qcv], inc=(dv_s, 1))
                nuv = S.op("dve", "scalar_tensor_tensor", out=num[par][:, :], in0=psQc[:, :], scalar=etok[:, ch:ch + 1],
                           in1=intra[par][:, :], op0=ALU.mult, op1=ALU.add, waits=[inv, qcv], inc=(dv_s, 1))
                prev["psQc"] = nuv
                x1 = S.op("dve", "scalar_tensor_tensor", out=smx[:, 2:3], in0=smx[:, 1:2], scalar=etok[:, ch:ch + 1],
                          in1=smx[:, 0:1], op0=ALU.mult, op1=ALU.add, waits=[d0], inc=(dv_s, 1))
                x2a = S.op("dve", "scalar_tensor_tensor", out=smx[:, 3:4], in0=smx[:, 2:3], scalar=-1.0,
                           in1=smx[:, 2:3], op0=ALU.mult, op1=ALU.max, waits=[x1], inc=(dv_s, 1))
                x2 = S.op("dve", "tensor_scalar_max", out=smx[:, 3:4], in0=smx[:, 3:4], scalar1=1.0,
                          waits=[x2a], inc=(dv_s, 1))
                x3 = S.op("dve", "reciprocal", out=smx[:, 4:5], in_=smx[:, 3:4], waits=[x2], inc=(dv_s, 1))
                s1 = S.op("act", "activation", out=junk[:, :], in_=num[par][:, :], func=AF.Square, accum_out=smx[:, 5:6],
                          waits=[nuv], inc=(ac_s, 1))
                y1 = S.op("dve", "tensor_tensor", out=smx[:, 6:7], in0=smx[:, 4:5], in1=smx[:, 4:5], op=ALU.mult,
                          waits=[x3], inc=(dv_s, 1))
                y2 = S.op("dve", "scalar_tensor_tensor", out=smx[:, 6:7], in0=smx[:, 5:6], scalar=1.0 / 512.0,
                          in1=smx[:, 6:7], op0=ALU.mult, op1=ALU.mult, waits=[y1, s1], inc=(dv_s, 1))
                y3 = S.op("act", "activation", out=smx[:, 6:7], in_=smx[:, 6:7], func=AF.Sqrt, bias=cst["eps"][:, 0:1],
                          waits=[y2], inc=(ac_s, 1))
                y4 = S.op("dve", "reciprocal", out=smx[:, 7:8], in_=smx[:, 6:7], waits=[y3], inc=(dv_s, 1))
                y5 = S.op("dve", "tensor_tensor", out=smx[:, 7:8], in0=smx[:, 7:8], in1=smx[:, 4:5], op=ALU.mult,
                          waits=[y4], inc=(dv_s, 1))
                hsv = S.op("dve", "scalar_tensor_tensor", out=hs[par][:, :], in0=num[par][:, :], scalar=smx[:, 7:8],
                           in1=go[par][:, :], op0=ALU.mult, op1=ALU.mult, waits=[y5, gov], inc=(dv_s, 1))
                for j in range(4):
                    thv = S.op("pe", "transpose", out=pT[:, 1536 + j * 128:1536 + (j + 1) * 128],
                               in_=hs[par][:, j * 128:(j + 1) * 128], identity=ident_b[:, :],
                               waits=[hsv, prev["psT_h"]] if j == 0 else [], inc=(pe_s, 1) if j == 3 else None)
                pe_last = thv
                hcv = S.op("act", "activation", out=hsT[:, h * 4:(h + 1) * 4, tk],
                           in_=pT[:, 1536:2048].rearrange("p (j t) -> p j t", j=4), func=AF.Copy, waits=[thv], inc=(ac_s, 1))
                prev["psT_h"] = hcv
                prev["hsTcopy"] = hcv
            cu = S.op("dve", "scalar_tensor_tensor", out=Cst[:, :], in0=Cst[:, :], scalar=eg[:, ch:ch + 1], in1=psC[:, :],
                      op0=ALU.mult, op1=ALU.add, waits=[pcv, cw if first else None, prev["cb"] if full else None],
                      inc=(dv_s, 1))
            prev["psC"] = cu
            nu = S.op("dve", "scalar_tensor_tensor", out=nst[:, :], in0=nst[:, :], scalar=eg[:, ch:ch + 1], in1=psD[:, 2:4],
                      op0=ALU.mult, op1=ALU.add, waits=[cu], inc=(dv_s, 1))
            prev["psD"] = nu
            prev["cupd"] = nu
            if full:
                S.op("act", "activation", out=Cb[:, :], in_=Cst[:, :], func=AF.Copy, waits=[nu], inc=(ac_s, 1))
                prev["cb"] = S.op("act", "activation", out=nb[:, :], in_=nst[:, :], func=AF.Copy, inc=(ac_s, 1))
            prev["pe_step"] = pe_last
            step += 1
        prev["hb_read"] = prev["pe_step"]
        S.op("sp", "dma_start", out=stC_out[:, h * 1024:(h + 1) * 1024], in_=Cst[:, :], waits=[prev["cupd"]], inc=(cst_s, 16))
        prev["cstore"] = S.op("sp", "dma_start", out=stn_out[:, h * 2:(h + 1) * 2], in_=nst[:, :], inc=(cst_s, 16))
    S.pending.append(prev["cstore"])
    if full:
        views = [wview(w_out, 0, c * 128, key=("mlo", c)) for c in range(KC)]
        ws = WStream(S, views)
        st = [S.sb(f"st{i}", [128, T], F32) for i in range(2)]
        acc_s = [S.sem(f"acc{i}") for i in range(2)]
        epi_s = S.sem("epi")
        st_free = [None, None]
        pedone = S.sem("pedone")
        banks = [psC, psT]
        brel = [prev["psC"], prev["hsTcopy"]]
        for c in range(KC):
            buf, wwait, j = ws.take()
            bi = c % 2
            pb = banks[bi]
            for kc in range(KC):
                for tt in range(T // 512):
                    lastmm = (kc == KC - 1 and tt == T // 512 - 1)
                    v = S.op("pe", "matmul", out=pb[:, tt * 512:(tt + 1) * 512], lhsT=buf[:, kc, :],
                             rhs=hsT[:, kc, tt * 512:(tt + 1) * 512], start=(kc == 0), stop=(kc == KC - 1),
                             waits=[wwait, brel[bi], prev["hsTcopy"]] if (kc == 0 and tt == 0) else (),
                             inc=(pedone, 1) if lastmm else None)
            ws.release(j, v)
            i = c % 2
            a = S.op("act", "activation", out=st[i][:, :], in_=pb[:, 0:T], func=AF.Copy, waits=[v, st_free[i]], inc=(epi_s, 1))
            brel[bi] = a
            dv = S.op("pool", "dma_start", out=hT[c * 128:(c + 1) * 128, t0:t0 + T], in_=st[i][:, :],
                      accum_op=ALU.add, waits=[a], inc=(acc_s[i], 16))
            st_free[i] = dv
            S.pending.append(dv)
    S.close()


def select_stage(P, gath, sel_d, stC, stn):
    S = Stage(P, "sel")
    W = 8192 + 16
    acc = S.sb("acc", [128, W], F32)
    gb = [S.sb(f"gb{i}", [128, W], F32) for i in range(2)]
    sel = S.sb("sel", [128, 8], F32)
    ls = S.sem("sld")
    gs = [S.sem(f"sg{i}") for i in range(2)]
    ds = S.sem("sdv")
    ss = S.sem("sst")
    sw = S.op("sp", "dma_start", out=sel[:, :], in_=sel_d[:, :], inc=(ls, 16))
    free = [None, None]
    a = None
    for r in range(8):
        i = r % 2
        v = S.op("sp", "dma_start", out=gb[i][:, :], in_=gath[r, :, :], waits=[free[i]], inc=(gs[i], 16))
        if r == 0:
            a = S.op("dve", "tensor_scalar", out=acc[:, :], in0=gb[i][:, :], scalar1=sel[:, 0:1], scalar2=None,
                     op0=ALU.mult, waits=[v, sw], inc=(ds, 1))
        else:
            a = S.op("dve", "scalar_tensor_tensor", out=acc[:, :], in0=gb[i][:, :], scalar=sel[:, r:r + 1], in1=acc[:, :],
                     op0=ALU.mult, op1=ALU.add, waits=[v, a], inc=(ds, 1))
        free[i] = a
    S.op("sp", "dma_start", out=stC[:, :], in_=acc[:, 0:8192], waits=[a], inc=(ss, 16))
    v = S.op("sp", "dma_start", out=stn[:, :], in_=acc[:, 8192:W], inc=(ss, 16))
    S.pending.append(v)
    S.close()


def scale_stage(P, src, sel_d, dst):
    S = Stage(P, "scl")
    buf = S.sb("buf", [128, SW], F32)
    sel = S.sb("sel", [128, 1], F32)
    l1 = S.sem("scl1")
    l2 = S.sem("scl2")
    dv = S.sem("scdv")
    st = S.sem("scst")
    a = S.op("sp", "dma_start", out=sel[:, :], in_=sel_d[:, :], inc=(l1, 16))
    b = S.op("sp", "dma_start", out=buf[:, :], in_=src[:, :], inc=(l2, 16))
    c = S.op("dve", "tensor_scalar", out=buf[:, :], in0=buf[:, :], scalar1=sel[:, 0:1], scalar2=None, op0=ALU.mult,
             waits=[a, b], inc=(dv, 1))
    v = S.op("sp", "dma_start", out=dst[:, :], in_=buf[:, :], waits=[c], inc=(st, 16))
    S.pending.append(v)
    S.close()


SW = 8192 + 16


def build(mode):
    P = Prog()
    EI, EO, IN = "ExternalInput", "ExternalOutput", "Internal"
    d = P.dt
    p1 = mode in ("fused", "p1")
    p2 = mode in ("fused", "p2")
    nm = d("norm_mix", [2, D], F32, EI)
    nf = d("norm_ffn", [2, D], F32, EI)
    if p1:
        xT = d("xT", [D, NTOK], F32, EI)
        gm_w_in = d("gm_w_in", [D, 2 * D], F32, EI)
        gm_w_out = d("gm_w_out", [D, D], F32, EI)
        lng = d("gm_ln_g", [D], F32, EI)
        lnb = d("gm_ln_b", [D], F32, EI)
        wsT = d("gm_w_sT", [8, 128, 128], F32, EI)
        bs = d("gm_b_s", [8, 128], F32, EI)
        up0 = d("ffn_up0", [D, 4 * D], F32, EI)
        dn0 = d("ffn_dn0", [4 * D, D], F32, EI)
        ml_w_in = d("ml_w_in", [D, 12304], F32, EI)
        bgate = d("ml_b_gate", [16], F32, EI)
        zst = d("zst", [128, SW], F32, EI)
    if p2:
        headg = d("ml_head_g", [1, D], F32, EI)
        ml_w_out = d("ml_w_out", [D, D], F32, EI)
        up1 = d("ffn_up1", [D, 4 * D], F32, EI)
        dn1 = d("ffn_dn1", [4 * D, D], F32, EI)
        nfin = d("norm_final", [D], F32, EI)
        outT = d("outT", [D, NTOK], F32, EO)
    if mode == "fused":
        hT = d("hT", [D, NTOK], F32, IN)
        projT = d("projT", [12288, NTOK], BF16, IN)
        gtok = d("gtok", [NTOK, 16], F32, IN)
        stA = d("stA", [128, SW], F32, IN)
        stB = d("stB", [128, SW], F32, IN)
        stI = d("stI", [128, SW], F32, IN)
        xpT = d("xpT", [D, NTOK], F32, EI)
        hpT = d("hpT", [D, NTOK], F32, IN)
        projpT = d("projpT", [12288, NTOK], BF16, IN)
        gtokp = d("gtokp", [NTOK, 16], F32, IN)
        stAp = d("stAp", [128, SW], F32, IN)
        stBp = d("stBp", [128, SW], F32, IN)
        sel = d("sel", [128, 1], F32, EI)
    elif mode == "p1":
        hT = d("hT", [D, NTOK], F32, EO)
        projT = d("projT", [12288, NTOK], BF16, EO)
        gtok = d("gtok", [NTOK, 16], F32, EO)
        stA = d("stA", [128, SW], F32, IN)
        stB = d("stB", [128, SW], F32, EO)
    else:
        hin = d("hin", [D, NTOK], F32, EI)
        hT = d("hT", [D, NTOK], F32, IN)
        projT = d("projT", [12288, NTOK], BF16, EI)
        gtok = d("gtok", [NTOK, 16], F32, EI)
        stI = d("stI", [128, SW], F32, EI)
        stA = d("stA", [128, SW], F32, IN)
        stB = d("stB", [128, SW], F32, IN)

    def CN(t):
        return t[:, 0:8192], t[:, 8192:SW]

    if p1:
        copy_stage(P, hT, xT)
        for t in range(4):
            gmlp_stage(P, hT, t * 512, gm_w_in, gm_w_out, nm[0, :], lng, lnb, wsT, bs)
        for t in range(2):
            ffn_stage(P, hT, t * 1024, 1024, up0, dn0, nf[0, :], lname='0')
        for t in range(2):
            mlproj_stage(P, hT, t * 1024, 1024, ml_w_in, nm[1, :], bgate, projT, gtok)
    if mode == "p1":
        mlrec_stage(P, 0, 1024, False, projT, gtok, *CN(zst), *CN(stA))
        mlrec_stage(P, 1024, 1024, False, projT, gtok, *CN(stA), *CN(stB))
    if mode == "fused":
        copy_stage(P, hpT, xpT)
        for t in range(4):
            gmlp_stage(P, hpT, t * 512, gm_w_in, gm_w_out, nm[0, :], lng, lnb, wsT, bs)
        for t in range(2):
            ffn_stage(P, hpT, t * 1024, 1024, up0, dn0, nf[0, :], lname='0')
        for t in range(2):
            mlproj_stage(P, hpT, t * 1024, 1024, ml_w_in, nm[1, :], bgate, projpT, gtokp, kv_only=True)
        mlrec_stage(P, 0, 1024, False, projpT, gtokp, *CN(zst), *CN(stAp))
        mlrec_stage(P, 1024, 1024, False, projpT, gtokp, *CN(stAp), *CN(stBp))
        scale_stage(P, stBp, sel, stI)
    if mode == "p2":
        copy_stage(P, hT, hin)
    if p2:
        mlrec_stage(P, 0, 1024, True, projT, gtok, *CN(stI), *CN(stA), headg_d=headg, w_out=ml_w_out, hT=hT)
        mlrec_stage(P, 1024, 1024, True, projT, gtok, *CN(stA), *CN(stB), headg_d=headg, w_out=ml_w_out, hT=hT)
        for t in range(2):
            ffn_stage(P, hT, t * 1024, 1024, up1, dn1, nf[1, :], lname='1')
        for t in range(2):
            final_stage(P, hT, outT, t * 1024, 1024, nfin)
    return P


def _c(a):
    return np.ascontiguousarray(a, dtype=np.float32)


MODE = "fused"


def kernel(x, norm_mix, norm_ffn, gm_w_in, gm_ln_g, gm_ln_b, gm_w_s, gm_b_s, gm_w_out,
           ml_w_in, ml_b_gate, ml_head_g, ml_w_out, ffn_w_up, ffn_w_down, norm_final):
    x = np.asarray(x, dtype=np.float32)
    xs = x.reshape(NCORES, NTOK, D)
    xT = [_c(xs[c].T) for c in range(NCORES)]
    com = {"norm_mix": _c(norm_mix), "norm_ffn": _c(norm_ffn)}
    in1 = {"gm_w_in": _c(gm_w_in[0]), "gm_w_out": _c(gm_w_out[0]), "gm_ln_g": _c(gm_ln_g[0]),
           "gm_ln_b": _c(gm_ln_b[0]), "gm_w_sT": _c(np.transpose(np.asarray(gm_w_s[0]), (0, 2, 1))),
           "gm_b_s": _c(gm_b_s[0]), "ffn_up0": _c(ffn_w_up[0]), "ffn_dn0": _c(ffn_w_down[0]),
           "ml_w_in": _c(ml_w_in[0]), "ml_b_gate": _c(ml_b_gate[0]),
           "zst": np.zeros((128, SW), np.float32)}
    in2 = {"ml_head_g": _c(np.asarray(ml_head_g[0]).reshape(1, D)), "ml_w_out": _c(ml_w_out[0]),
           "ffn_up1": _c(ffn_w_up[1]), "ffn_dn1": _c(ffn_w_down[1]), "norm_final": _c(norm_final)}
    if MODE == "fused":
        P = build("fused")
        maps = []
        for c in range(NCORES):
            sel = np.full((128, 1), float(c % 2), np.float32)
            m = dict(com); m.update(in1); m.update(in2); m["xT"] = xT[c]; m["sel"] = sel
            m["xpT"] = xT[c - (c % 2)]
            maps.append(m)
        res = run_bass_kernel_spmd(P.nc, maps, core_ids=list(range(NCORES)))
        outs = [r["outT"] for r in res.results]
    else:
        P1 = build("p1")
        maps = []
        for c in range(NCORES):
            m = dict(com); m.update(in1); m["xT"] = xT[c]
            maps.append(m)
        r1 = run_bass_kernel_spmd(P1.nc, maps, core_ids=list(range(NCORES))).results
        P2 = build("p2")
        maps = []
        for c in range(NCORES):
            m = dict(com); m.update(in2)
            m["hin"] = r1[c]["hT"]; m["projT"] = r1[c]["projT"]; m["gtok"] = r1[c]["gtok"]
            m["stI"] = r1[c - 1]["stB"] if c % 2 == 1 else np.zeros((128, SW), np.float32)
            maps.append(m)
        r2 = run_bass_kernel_spmd(P2.nc, maps, core_ids=list(range(NCORES))).results
        outs = [r["outT"] for r in r2]
    out = np.stack([np.asarray(o, dtype=np.float32).T for o in outs], axis=0)
    return np.ascontiguousarray(out.reshape(4, 4096, D))
```

```python
import numpy as np
from contextlib import ExitStack
import concourse.bass as bass
import concourse.mybir as mybir
from concourse.bass_utils import run_bass_kernel_spmd

F32 = mybir.dt.float32
BF16 = mybir.dt.bfloat16
AF = mybir.ActivationFunctionType
ALU = mybir.AluOpType

D = 4096
KC = 32
NTOK = 2048
EPS = 1e-6
ENGS = ("pe", "act", "dve", "pool", "sp")
WC_BLOCKS = {"gmi": 64, "gmo": 32, "up0": 128, "dn0": 128, "up1": 128, "dn1": 128, "mli": 97, "mlo": 32}
NCORES = 8


class Sem:
    def __init__(self, h, name):
        self.h = h
        self.n = 0
        self.name = name


class Prog:
    def __init__(self, num_devices=None):
        self.nc = bass.Bass("TRN2", target_bir_lowering=False, num_devices=num_devices)
        self.es = ExitStack()
        self.sems = {}
        self.stage_idx = 0
        self.barrier = Sem(None, "barrier")
        self.bar = []
        self.wc = {}
        self.wc_filled = set()
        self.use_cache = True
        self.dram = {}
        E = self.es.enter_context
        nc = self.nc
        self.scrA = E(nc.sbuf_tensor("scrA", [128, 2], F32))
        self.scrD = E(nc.sbuf_tensor("scrD", [128, 2], F32))
        self.scrP = E(nc.sbuf_tensor("scrP", [128, 2], F32))

    def getsem(self, name):
        if name not in self.sems:
            self.sems[name] = Sem(self.es.enter_context(self.nc.semaphore(name)), name)
        return self.sems[name]

    def cache_ap(self, key, nblk_hint=None):
        name, blk = key
        if name not in self.wc:
            t = self.nc.dram_tensor("wc_" + name, [WC_BLOCKS[name], 128, 4096], BF16, kind="Internal")
            self.wc[name] = t.ap()
        return self.wc[name][blk, :, :].rearrange("p (k n) -> p k n", n=128)

    def dt(self, name, shape, dtype, kind):
        t = self.nc.dram_tensor(name, list(shape), dtype, kind=kind)
        self.dram[name] = t.ap()
        return self.dram[name]


class Stage:
    def __init__(self, P, name):
        self.P = P
        self.nc = P.nc
        self.name = f"s{P.stage_idx}{name}"
        P.stage_idx += 1
        self.es = ExitStack()
        self.ops = {e: [] for e in ENGS}
        self.pending = []
        self.psrel = {}
        self.psnext = 0
        if P.bar:
            for e in ENGS:
                self.ops[e].append(("__wait__", (), {}, tuple(P.bar), None))

    def sb(self, name, shape, dt):
        return self.es.enter_context(self.nc.sbuf_tensor(f"{self.name}_{name}", list(shape), dt))

    def psum(self, name, shape, dt=F32):
        return self.es.enter_context(self.nc.psum_tensor(f"{self.name}_{name}", list(shape), dt))

    def sem(self, name):
        return Sem(None, name)

    def op(self, eng, fname, *args, waits=(), inc=None, **kw):
        waits = tuple(w for w in waits if w is not None)
        if inc is not None:
            sem = self.P.getsem(inc[0].name.split("__")[0] + "__" + eng)
            inc = (sem, inc[1])
        self.ops[eng].append((fname, args, kw, waits, inc))
        if inc is not None:
            inc[0].n += inc[1]
            return (inc[0], inc[0].n)
        return None

    def close(self):
        P = self.P
        bs = P.barrier
        b1 = self.op("act", "activation", out=P.scrA[:, 0:1], in_=P.scrA[:, 1:2], func=AF.Copy,
                     waits=self.pending, inc=(bs, 1))
        b2 = self.op("dve", "memset", P.scrD[:, 0:1], 0.0, inc=(bs, 1))
        b3 = self.op("pool", "memset", P.scrP[:, 0:1], 0.0, inc=(bs, 1))
        P.bar = [b1, b2, b3]
        nc = self.nc
        ops = self.ops

        def mk(engname):
            def f(eng):
                for (fname, args, kw, waits, inc) in ops[engname]:
                    for (sem, val) in waits:
                        eng.wait_ge(sem.h, val)
                    if fname == "__wait__":
                        continue
                    ins = getattr(eng, fname)(*args, **kw)
                    if inc is not None:
                        ins.then_inc(inc[0].h, inc[1])
            return f

        with nc.Block() as block:
            block.tensor(mk("pe"))
            block.scalar(mk("act"))
            block.vector(mk("dve"))
            block.gpsimd(mk("pool"))
            block.sync(mk("sp"))
        self.es.close()


class WStream:
    def __init__(self, S, views, NW=3):
        self.S = S
        self.views = views
        self.NW = NW
        self.bufs = [S.sb(f"wb{i}", [128, KC, 128], BF16) for i in range(NW)]
        self.wsem = [S.sem(f"wsem{i}") for i in range(NW)]
        self.loaded = [None] * len(views)
        self.wst = [S.sem(f"wst{i}") for i in range(NW)]
        self.stored = [None] * NW
        self.rel = [None] * NW
        self.next_load = 0
        self.next_use = 0

    def _load(self, j):
        s = j % self.NW
        S = self.S
        P = S.P
        v, ncols, key = self.views[j]
        waits = [self.rel[s], self.stored[s]]
        if key is not None and P.use_cache and key in P.wc_filled:
            self.loaded[j] = S.op("sp", "dma_start", out=self.bufs[s][:, :, :], in_=P.cache_ap(key),
                                  waits=waits, inc=(self.wsem[s], 16))
            self.stored[s] = None
            return
        ld = S.op("pool", "dma_start", out=self.bufs[s][:, :, 0:ncols], in_=v, waits=waits, inc=(self.wsem[s], 16))
        self.loaded[j] = ld
        self.stored[s] = None
        if key is not None and P.use_cache and ncols == 128:
            st = S.op("sp", "dma_start", out=P.cache_ap(key), in_=self.bufs[s][:, :, :], waits=[ld],
                      inc=(self.wst[s], 16))
            self.stored[s] = st
            S.pending.append(st)
            P.wc_filled.add(key)

    def take(self):
        j = self.next_use
        while self.next_load < min(len(self.views), j + self.NW):
            self._load(self.next_load)
            self.next_load += 1
        self.next_use += 1
        return self.bufs[j % self.NW], self.loaded[j], j

    def release(self, j, sv):
        self.rel[j % self.NW] = sv


def wview(W, r0, c0, ncols=128, key=None):
    return (W[r0:r0 + D, c0:c0 + ncols].rearrange("(kc p) n -> p kc n", p=128), ncols, key)


def gemm_ws(S, ws, nblk, rhs, T, epi, ps, first_waits=(), mcols=128):
    pedone = S.sem("pedone")
    nslots = 4096 // T
    ntt = T // 512
    last = None
    for c in range(nblk):
        buf, wwait, j = ws.take()
        slot = S.psnext % nslots
        S.psnext += 1
        waits = [wwait, S.psrel.get(slot)] + (list(first_waits) if c == 0 else [])
        for kc in range(KC):
            for tt in range(ntt):
                lastmm = (kc == KC - 1 and tt == ntt - 1)
                v = S.op("pe", "matmul", out=ps[0:mcols, slot * T + tt * 512: slot * T + (tt + 1) * 512],
                         lhsT=buf[:, kc, 0:mcols], rhs=rhs[:, kc, tt * 512:(tt + 1) * 512],
                         start=(kc == 0), stop=(kc == KC - 1),
                         waits=waits if (kc == 0 and tt == 0) else (),
                         inc=(pedone, 1) if lastmm else None)
        ws.release(j, v)
        S.psrel[slot] = epi(c, ps[0:mcols, slot * T:(slot + 1) * T], v)
        last = S.psrel[slot]
    return last


def consts_basic(S):
    c = {}
    c["ones_bf"] = S.sb("ones_bf", [128, 128], BF16)
    c["eps"] = S.sb("eps_t", [128, 1], F32)
    cs = S.sem("cset")
    S.op("pool", "memset", c["ones_bf"][:, :], 1.0)
    c["rdy"] = S.op("pool", "memset", c["eps"][:, :], EPS, inc=(cs, 1))
    return c


def load_vec(S, dst, src_ap, eng="sp"):
    sm = S.sem("vecld")
    return S.op(eng, "dma_start", out=dst[:, :], in_=src_ap.rearrange("(c p) -> p c", p=128),
                allow_slow_non_contiguous=True, inc=(sm, 16))


def norm_pass(S, cst, hT, t0, T, gvec, gwait, xb, ps, want_sq_only=False):
    xf = [S.sb(f"xf{i}", [128, T], F32) for i in range(2)]
    sq = [S.sb(f"sq{i}", [128, T], BF16) for i in range(2)]
    srt = S.sb("srt", [128, T], F32)
    rstd = S.sb("rstd", [128, T], F32)
    xld = [S.sem(f"xld{i}") for i in range(2)]
    sqrdy = S.sem("sqrdy")
    sqfree_s = S.sem("sqfree")
    nrm = S.sem("nrm")
    slot = S.psnext % (4096 // T)
    S.psnext += 1
    pss = ps[:, slot * T:(slot + 1) * T]
    xfree = [None, None]
    sqfree = [None, None]
    ntt = T // 512
    v_sq = None
    for fc in range(KC):
        i = fc % 2
        v_ld = S.op("sp", "dma_start", out=xf[i][:, :], in_=hT[fc * 128:(fc + 1) * 128, t0:t0 + T],
                    waits=[xfree[i]], inc=(xld[i], 16))
        if not want_sq_only:
            S.op("act", "activation", out=xb[:, fc, :], in_=xf[i][:, :], func=AF.Copy,
                 scale=gvec[:, fc:fc + 1], waits=[v_ld, gwait if fc == 0 else None])
        v_sq = S.op("act", "activation", out=sq[i][:, :], in_=xf[i][:, :], func=AF.Square,
                    waits=[v_ld, sqfree[i]], inc=(sqrdy, 1))
        xfree[i] = v_sq
        for tt in range(ntt):
            w = [v_sq] if tt == 0 else []
            if fc == 0 and tt == 0:
                w += [S.psrel.get(slot), cst["rdy"]]
            v_pe = S.op("pe", "matmul", out=pss[:, tt * 512:(tt + 1) * 512], lhsT=cst["ones_bf"][:, :],
                        rhs=sq[i][:, tt * 512:(tt + 1) * 512], start=(fc == 0), stop=(fc == KC - 1),
                        waits=w, inc=(sqfree_s, 1) if tt == ntt - 1 else None)
        sqfree[i] = v_pe
    v1 = S.op("act", "activation", out=srt[:, :], in_=pss, func=AF.Sqrt, scale=1.0 / D,
              bias=cst["eps"][:, 0:1], waits=[v_pe], inc=(nrm, 1))
    S.psrel[slot] = v1
    v2 = S.op("dve", "reciprocal", out=rstd[:, :], in_=srt[:, :], waits=[v1], inc=(nrm, 1))
    S.norm_scratch = (xf[0], xf[1], srt)
    return rstd, v2, v_sq


def ffn_stage(P, hT, t0, T, w_up, w_dn, gdram, lname='0'):
    S = Stage(P, "ffn")
    cst = consts_basic(S)
    ps = S.psum("ps", [128, 4096], F32)
    xb = S.sb("xb", [128, KC, T], BF16)
    ag = S.sb("ag", [128, KC, T], BF16)
    gv = S.sb("gv", [128, KC], F32)
    gw = load_vec(S, gv, gdram)
    rstd, rrdy, xrdy = norm_pass(S, cst, hT, t0, T, gv, gw, xb, ps)
    NG = 4
    views = []
    for g in range(NG):
        views += [wview(w_up, 0, g * D + c * 128, key=("up" + lname, g * KC + c)) for c in range(KC)]
        views += [wview(w_dn, g * D, c * 128, key=("dn" + lname, g * KC + c)) for c in range(KC)]
    ws = WStream(S, views)
    r1 = [S.sb(f"r1_{i}", [128, T], F32) for i in range(2)]
    st = [S.sb(f"st{i}", [128, T], F32) for i in range(2)]
    epi_s = S.sem("epi")
    acc_s = [S.sem(f"acc{i}") for i in range(2)]
    state = {"k": 0, "ag_done": None, "st_free": [None, None], "r1_free": [None, None], "dn_read": None}

    def up_epi(c, pv, pev):
        i = state["k"] % 2
        state["k"] += 1
        a = S.op("act", "activation", out=r1[i][:, :], in_=pv, func=AF.Relu,
                 waits=[pev, state["r1_free"][i]], inc=(epi_s, 1))
        b = S.op("dve", "tensor_tensor", out=r1[i][:, :], in0=r1[i][:, :], in1=rstd[:, :], op=ALU.mult,
                 waits=[a, rrdy], inc=(epi_s, 1))
        d = S.op("act", "activation", out=ag[:, c, :], in_=r1[i][:, :], func=AF.Square,
                 waits=[b, state["dn_read"] if c == 0 else None], inc=(epi_s, 1))
        state["r1_free"][i] = d
        state["ag_done"] = d
        return a

    def dn_epi(c, pv, pev):
        i = state["k"] % 2
        state["k"] += 1
        a = S.op("act", "activation", out=st[i][:, :], in_=pv, func=AF.Copy,
                 waits=[pev, state["st_free"][i]], inc=(epi_s, 1))
        v = S.op("pool", "dma_start", out=hT[c * 128:(c + 1) * 128, t0:t0 + T], in_=st[i][:, :],
                 accum_op=ALU.add, waits=[a], inc=(acc_s[i], 16))
        state["st_free"][i] = v
        S.pending.append(v)
        return a

    for g in range(NG):
        gemm_ws(S, ws, KC, xb, T, up_epi, ps, first_waits=[xrdy] if g == 0 else [])
        lastdn = gemm_ws(S, ws, KC, ag, T, dn_epi, ps, first_waits=[state["ag_done"]])
        state["dn_read"] = ws.rel[(ws.next_use - 1) % ws.NW]
    S.close()


def final_stage(P, hT, outT, t0, T, gdram):
    S = Stage(P, "fin")
    cst = consts_basic(S)
    ps = S.psum("ps", [128, 4096], F32)
    gv = S.sb("gv", [128, KC], F32)
    gw = load_vec(S, gv, gdram)
    rstd, rrdy, _ = norm_pass(S, cst, hT, t0, T, gv, gw, None, ps, want_sq_only=True)
    xg = [S.sb(f"xg{i}", [128, T], F32) for i in range(2)]
    ld = [S.sem(f"fld{i}") for i in range(2)]
    stq = [S.sem(f"fst{i}") for i in range(2)]
    fe = S.sem("fe")
    free = [None, None]
    for fc in range(KC):
        i = fc % 2
        v_ld = S.op("sp", "dma_start", out=xg[i][:, :], in_=hT[fc * 128:(fc + 1) * 128, t0:t0 + T],
                    waits=[free[i]], inc=(ld[i], 16))
        a = S.op("dve", "scalar_tensor_tensor", out=xg[i][:, :], in0=xg[i][:, :], scalar=gv[:, fc:fc + 1],
                 in1=rstd[:, :], op0=ALU.mult, op1=ALU.mult, waits=[v_ld, rrdy, gw], inc=(fe, 1))
        v = S.op("act", "dma_start", out=outT[fc * 128:(fc + 1) * 128, t0:t0 + T], in_=xg[i][:, :],
                 waits=[a], inc=(stq[i], 16))
        free[i] = v
        S.pending.append(v)
    S.close()


def gmlp_stage(P, hT, t0, w_in, w_out, gdram, lng_d, lnb_d, wsT_d, bs_d):
    T = 512
    NCH = 4
    S = Stage(P, "gm")
    cst = consts_basic(S)
    nc = S.nc
    ps = S.psum("ps", [128, 4096], F32)
    xb = S.sb("xb", [128, KC, T], BF16)
    uT = S.sb("uT", [128, KC, T], BF16)
    vg = S.sb("vg", [128, KC, T], BF16)
    vtk = S.sb("vtk", [128, NCH, D], BF16)
    gv = S.sb("gv", [128, KC], F32)
    lng = S.sb("lng", [128, KC], F32)
    lnb = S.sb("lnb", [128, KC], F32)
    load_vec(S, gv, gdram)
    load_vec(S, lng, lng_d)
    lw = load_vec(S, lnb, lnb_d)
    gw = lw
    ones_f = S.sb("ones_f", [128, 128], F32)
    ident_b = S.sb("ident_b", [128, 128], BF16)
    wT = S.sb("wT", [128, 8, 128], F32)
    wTb = S.sb("wTb", [128, 8, 128], BF16)
    bsrow = S.sb("bsrow", [1, 1024], F32)
    rwbs = S.sb("rwbs", [128, 2, 1024], F32)
    c2 = S.sb("c2", [128, KC, 128], F32)
    cs = S.sem("gmc")
    cl = S.sem("gmcl")
    w1 = S.op("sp", "dma_start", out=wT[:, :, :], in_=wsT_d.rearrange("h s t -> s h t"), inc=(cl, 16))
    cl2 = S.sem("gmcl2")
    w2 = S.op("sp", "dma_start", out=bsrow[:, :], in_=bs_d.rearrange("h t -> (h t)").unsqueeze(0), inc=(cl2, 16))
    S.op("pool", "memset", ones_f[:, :], 1.0)
    S.op("pool", "affine_select", out=ident_b[:, :], in_=cst["ones_bf"][:, :], pattern=[[1, 128]],
         compare_op=ALU.is_equal, fill=0.0, base=0, channel_multiplier=-1, waits=[cst["rdy"]])
    S.op("pool", "affine_select", out=wT[:, :, :], in_=wT[:, :, :], pattern=[[0, 8], [1, 128]],
         compare_op=ALU.is_ge, fill=0.0, base=0, channel_multiplier=-1, waits=[w1])
    k1 = S.op("pool", "tensor_copy", out=wTb[:, :, :], in_=wT[:, :, :], inc=(cs, 1))
    wTf = wT[:, :, :].rearrange("p h t -> p (h t)")
    for half in range(2):
        S.op("pe", "matmul", out=ps[:, 3072 + half * 512:3072 + (half + 1) * 512], lhsT=ones_f[:, :],
             rhs=wTf[:, half * 512:(half + 1) * 512], start=True, stop=True, waits=[k1] if half == 0 else [])
    for half in range(2):
        k2 = S.op("pe", "matmul", out=ps[:, 2048 + half * 512:2048 + (half + 1) * 512], lhsT=ones_f[0:1, :],
                  rhs=bsrow[0:1, half * 512:(half + 1) * 512], start=True, stop=True,
                  waits=[w2] if half == 0 else [], inc=(cs, 1) if half == 1 else None)
    S.op("act", "activation", out=rwbs[:, 0, :], in_=ps[:, 3072:4096], func=AF.Copy, waits=[k2])
    k3 = S.op("act", "activation", out=rwbs[:, 1, :], in_=ps[:, 2048:3072], func=AF.Copy, inc=(cs, 1))
    for s_ in (4, 5, 6, 7):
        S.psrel[s_] = k3
    for fc in range(KC):
        h = fc // 4
        k4 = S.op("dve", "scalar_tensor_tensor", out=c2[:, fc, :], in0=rwbs[:, 0, h * 128:(h + 1) * 128],
                  scalar=lnb[:, fc:fc + 1], in1=rwbs[:, 1, h * 128:(h + 1) * 128], op0=ALU.mult, op1=ALU.add,
                  waits=[k3, lw] if fc == 0 else [], inc=(cs, 1) if fc == KC - 1 else None)
    c2rdy = k4
    rstd, rrdy, xrdy = norm_pass(S, cst, hT, t0, T, gv, gw, xb, ps)
    views = [wview(w_in, 0, c * 128, key=("gmi", c)) for c in range(KC)]
    views += [wview(w_in, 0, D + c * 128, key=("gmi", KC + c)) for c in range(KC)]
    views += [wview(w_out, 0, c * 128, key=("gmo", c)) for c in range(KC)]
    ws = WStream(S, views, NW=2)
    tmp = [S.sb(f"tmp{i}", [128, T], F32) for i in range(2)]
    epi_s = S.sem("epi")
    state = {"k": 0, "tfree": [None, None], "last": None}

    def gelu_epi(dst):
        def epi(c, pv, pev):
            i = state["k"] % 2
            state["k"] += 1
            a = S.op("dve", "tensor_tensor", out=tmp[i][:, :], in0=pv, in1=rstd[:, :], op=ALU.mult,
                     waits=[pev, rrdy, state["tfree"][i]], inc=(epi_s, 1))
            b = S.op("act", "activation", out=dst[:, c, :], in_=tmp[i][:, :], func=AF.Gelu,
                     waits=[a], inc=(epi_s, 1))
            state["tfree"][i] = b
            state["last"] = b
            return a
        return epi

    gemm_ws(S, ws, KC, xb, T, gelu_epi(uT), ps, first_waits=[xrdy])
    gemm_ws(S, ws, KC, xb, T, gelu_epi(vg), ps)
    vdone = state["last"]
    xb_read = ws.rel[(ws.next_use - 1) % ws.NW]
    sqv = [S.sb(f"sqv{i}", [128, T], BF16) for i in range(2)]
    lns = S.sem("lns")
    lnp = S.sem("lnp")
    slot1 = S.psnext % 8
    S.psnext += 1
    slot2 = S.psnext % 8
    S.psnext += 1
    p1 = ps[:, slot1 * T:(slot1 + 1) * T]
    p2 = ps[:, slot2 * T:(slot2 + 1) * T]
    sfree = [None, None]
    for fc in range(KC):
        i = fc % 2
        a = S.op("dve", "tensor_tensor", out=sqv[i][:, :], in0=vg[:, fc, :], in1=vg[:, fc, :], op=ALU.mult,
                 waits=[sfree[i], vdone if fc == 0 else None], inc=(lns, 1))
        S.op("pe", "matmul", out=p1, lhsT=cst["ones_bf"][:, :], rhs=vg[:, fc, :], start=(fc == 0),
             stop=(fc == KC - 1), waits=[vdone, S.psrel.get(slot1), S.psrel.get(slot2)] if fc == 0 else [])
        b = S.op("pe", "matmul", out=p2, lhsT=cst["ones_bf"][:, :], rhs=sqv[i][:, :], start=(fc == 0),
                 stop=(fc == KC - 1), waits=[a], inc=(lnp, 1))
        sfree[i] = b
    mean, var, lrs = S.norm_scratch
    a = S.op("act", "activation", out=mean[:, :], in_=p1, func=AF.Copy, scale=1.0 / D, waits=[b], inc=(lns, 1))
    a2 = S.op("dve", "tensor_tensor", out=var[:, :], in0=mean[:, :], in1=mean[:, :], op=ALU.mult, waits=[a], inc=(lns, 1))
    a3 = S.op("dve", "scalar_tensor_tensor", out=var[:, :], in0=p2, scalar=1.0 / D, in1=var[:, :],
              op0=ALU.mult, op1=ALU.subtract, waits=[a2], inc=(lns, 1))
    S.psrel[slot1] = a3
    S.psrel[slot2] = a3
    a4 = S.op("act", "activation", out=var[:, :], in_=var[:, :], func=AF.Sqrt, bias=cst["eps"][:, 0:1],
              waits=[a3], inc=(lns, 1))
    a5 = S.op("dve", "reciprocal", out=lrs[:, :], in_=var[:, :], waits=[a4], inc=(lns, 1))
    vn = xb
    trs = S.sem("trs")
    trp = S.sem("trp")
    mixp = S.sem("mixp")
    tcp = S.sem("tcp")
    gts = S.sem("gts")
    pst = S.psum
    psb = ps.bitcast(BF16) if hasattr(ps, "bitcast") else None
    for fc in range(KC):
        a = S.op("dve", "tensor_tensor", out=tmp[0][:, :], in0=vg[:, fc, :], in1=mean[:, :], op=ALU.subtract,
                 waits=[a5, xb_read, state["tfree"][0]] if fc == 0 else [state.get("vnw")], inc=(trs, 1))
        state["vnw"] = S.op("dve", "tensor_tensor", out=vn[:, fc, :], in0=tmp[0][:, :], in1=lrs[:, :], op=ALU.mult,
                            waits=[a], inc=(trs, 1))
    vn_done = state["vnw"]
    pstr = ps[:, :].bitcast(BF16)
    cnt = 0
    for n in range(NCH):
        for gq in range(8):
            slot = S.psnext % 8
            S.psnext += 1
            for q in range(4):
                fc = gq * 4 + q
                v = S.op("pe", "transpose", out=pstr[:, slot * 1024 + q * 128: slot * 1024 + (q + 1) * 128],
                         in_=vn[:, fc, n * 128:(n + 1) * 128], identity=ident_b[:, :],
                         waits=[vn_done, S.psrel.get(slot), k1] if q == 0 else [],
                         inc=(trp, 1) if q == 3 else None)
            eng = "act" if cnt % 2 == 0 else "dve"
            if eng == "act":
                e = S.op("act", "activation", out=vtk[:, n, gq * 512:(gq + 1) * 512],
                         in_=pstr[:, slot * 1024: slot * 1024 + 512], func=AF.Copy, waits=[v], inc=(tcp, 1))
            else:
                e = S.op("dve", "tensor_copy", out=vtk[:, n, gq * 512:(gq + 1) * 512],
                         in_=pstr[:, slot * 1024: slot * 1024 + 512], waits=[v], inc=(tcp, 1))
            S.psrel[slot] = e
            cnt += 1
            state["vtk_a" if eng == "act" else "vtk_d"] = e
    vtk_done = [state["vtk_a"], state["vtk_d"]]
    for n in range(NCH):
        for gq in range(8):
            slot = S.psnext % 8
            S.psnext += 1
            h = gq
            for q in range(4):
                fc = gq * 4 + q
                v = S.op("pe", "matmul", out=ps[:, slot * 512 + q * 128: slot * 512 + (q + 1) * 128],
                         lhsT=vtk[:, n, fc * 128:(fc + 1) * 128], rhs=wTb[:, h, :], start=True, stop=True,
                         waits=(vtk_done + [S.psrel.get(slot)]) if q == 0 else [],
                         inc=(mixp, 1) if q == 3 else None)
            for q in range(4):
                fc = gq * 4 + q
                a = S.op("dve", "scalar_tensor_tensor", out=tmp[1][:, q * 128:(q + 1) * 128],
                         in0=ps[:, slot * 512 + q * 128: slot * 512 + (q + 1) * 128], scalar=lng[:, fc:fc + 1],
                         in1=c2[:, fc, :], op0=ALU.mult, op1=ALU.add,
                         waits=[v, c2rdy, state["tfree"][1], state.get("gt")] if q == 0 else [],
                         inc=(gts, 1) if q == 3 else None)
            S.psrel[slot] = a
            uview = uT[:, gq * 4:(gq + 1) * 4, n * 128:(n + 1) * 128]
            state["gt"] = S.op("dve", "tensor_tensor", out=uview, in0=uview,
                               in1=tmp[1][:, :].rearrange("p (q t) -> p q t", q=4), op=ALU.mult, waits=[a], inc=(gts, 1))
    gated_done = state["gt"]
    st = tmp
    acc_s = [S.sem(f"acc{i}") for i in range(2)]
    st_free = [None, None]

    def out_epi(c, pv, pev):
        i = c % 2
        a = S.op("act", "activation", out=st[i][:, :], in_=pv, func=AF.Copy, waits=[pev, st_free[i]], inc=(epi_s, 1))
        v = S.op("pool", "dma_start", out=hT[c * 128:(c + 1) * 128, t0:t0 + T], in_=st[i][:, :],
                 accum_op=ALU.add, waits=[a], inc=(acc_s[i], 16))
        st_free[i] = v
        S.pending.append(v)
        return a

    gemm_ws(S, ws, KC, uT, T, out_epi, ps, first_waits=[gated_done])
    S.close()


def copy_stage(P, dst, src):
    S = Stage(P, "cp")
    cs = S.sem("cpy")
    for q in range(8):
        v = S.op("sp", "dma_start", out=dst[q * 512:(q + 1) * 512, :], in_=src[q * 512:(q + 1) * 512, :], inc=(cs, 16))
    S.pending.append(v)
    S.close()


def mlproj_stage(P, hT, t0, T, w_in, gdram, bgate_d, projT, gates_tok, kv_only=False):
    S = Stage(P, "mlp")
    cst = consts_basic(S)
    ps = S.psum("ps", [128, 4096], F32)
    xb = S.sb("xb", [128, KC, T], BF16)
    gv = S.sb("gv", [128, KC], F32)
    gw = load_vec(S, gv, gdram)
    ident_f = S.sb("ident_f", [128, 128], F32)
    ones_f = S.sb("ones_f", [128, 128], F32)
    S.op("pool", "memset", ones_f[:, :], 1.0)
    cs = S.sem("cset")
    idr = S.op("pool", "affine_select", out=ident_f[:, :], in_=ones_f[:, :], pattern=[[1, 128]],
               compare_op=ALU.is_equal, fill=0.0, base=0, channel_multiplier=-1, waits=[cst["rdy"]], inc=(cs, 1))
    rstd, rrdy, xrdy = norm_pass(S, cst, hT, t0, T, gv, gw, xb, ps)
    blocks = list(range(16, 64)) if kv_only else list(range(96))
    views = [wview(w_in, 0, c * 128, key=("mli", c)) for c in blocks] + [wview(w_in, 0, 12304 - 128, key=("mli", 96))]
    ws = WStream(S, views)
    st = [S.sb(f"st{i}", [128, T], BF16) for i in range(2)]
    tmp = [S.sb(f"tmp{i}", [128, T], F32) for i in range(2)]
    epi_s = S.sem("epi")
    sts = [S.sem(f"pst{i}") for i in range(2)]
    state = {"k": 0, "stfree": [None, None], "tfree": [None, None]}

    def epi(ci, pv, pev):
        c = blocks[ci]
        i = state["k"] % 2
        state["k"] += 1
        if c < 64:
            sc = 0.0625 if c < 16 else 1.0
            a = S.op("dve", "scalar_tensor_tensor", out=st[i][:, :], in0=pv, scalar=sc, in1=rstd[:, :],
                     op0=ALU.mult, op1=ALU.mult, waits=[pev, rrdy, state["stfree"][i]], inc=(epi_s, 1))
            rel = a
            fin = a
        else:
            a = S.op("dve", "tensor_tensor", out=tmp[i][:, :], in0=pv, in1=rstd[:, :], op=ALU.mult,
                     waits=[pev, rrdy, state["tfree"][i]], inc=(epi_s, 1))
            fin = S.op("act", "activation", out=st[i][:, :], in_=tmp[i][:, :], func=AF.Sigmoid,
                       waits=[a, state["stfree"][i]], inc=(epi_s, 1))
            state["tfree"][i] = fin
            rel = a
        v = S.op("sp", "dma_start", out=projT[c * 128:(c + 1) * 128, t0:t0 + T], in_=st[i][:, :],
                 waits=[fin], inc=(sts[i], 16))
        state["stfree"][i] = v
        S.pending.append(v)
        return rel

    gemm_ws(S, ws, len(blocks), xb, T, epi, ps, first_waits=[xrdy])
    nch = T // 128
    gpe = S.sem("gpe")
    gac = S.sem("gac")
    gdv = S.sem("gdv")
    gpo = S.sem("gpo")
    buf, wwait, j = ws.take()
    slot = S.psnext % (4096 // T)
    S.psnext += 1
    psg = ps[:, slot * T: slot * T + nch * 16]
    for n in range(nch):
        for kc in range(KC):
            v = S.op("pe", "matmul", out=psg[:, n * 16:(n + 1) * 16], lhsT=xb[:, kc, n * 128:(n + 1) * 128],
                     rhs=buf[:, kc, 112:128], start=(kc == 0), stop=(kc == KC - 1),
                     waits=[wwait, S.psrel.get(slot)] if (n == 0 and kc == 0) else [],
                     inc=(gpe, 1) if (n == nch - 1 and kc == KC - 1) else None)
    ws.release(j, v)
    gdone = v
    slot2 = S.psnext % (4096 // T)
    S.psnext += 1
    pst = ps[:, slot2 * T: slot2 * T + nch * 128].rearrange("p (n k) -> p n k", k=128)
    for n in range(nch):
        tv = S.op("pe", "transpose", out=pst[:, n, :], in_=rstd[:, n * 128:(n + 1) * 128], identity=ident_f[:, :],
                  waits=[rrdy, idr, S.psrel.get(slot2)] if n == 0 else [], inc=(gpe, 1) if n == nch - 1 else None)
    rt = S.sb("rt", [128, nch], F32)
    a0 = S.op("act", "activation", out=rt[:, :], in_=pst[:, :, 0], func=AF.Copy, waits=[tv], inc=(gac, 1))
    S.psrel[slot2] = a0
    bgrow = S.sb("bgrow", [1, 16], F32)
    bgl2 = S.sem("bgl2")
    bw = S.op("sp", "dma_start", out=bgrow[:, :], in_=bgate_d.unsqueeze(0), inc=(bgl2, 16))
    slot3 = S.psnext % (4096 // T)
    S.psnext += 1
    pbg = ps[:, slot3 * T: slot3 * T + 16]
    bm = S.op("pe", "matmul", out=pbg, lhsT=ones_f[0:1, :], rhs=bgrow[0:1, :], start=True, stop=True,
              waits=[bw, idr, S.psrel.get(slot3)], inc=(gpe, 1))
    bgbc = S.sb("bgbc", [128, 16], F32)
    b0 = S.op("act", "activation", out=bgbc[:, :], in_=pbg, func=AF.Copy, waits=[bm], inc=(gac, 1))
    S.psrel[slot3] = b0
    gt = S.sb("gt", [128, nch, 16], F32)
    cap = S.sb("cap", [128, nch, 16], F32)
    lp = S.sb("lp", [128, nch, 16], F32)
    one1 = S.sb("one1", [128, 1], F32)
    o1 = S.op("pool", "memset", one1[:, :], 1.0, inc=(gpo, 1))
    a = None
    graw = S.sb("graw", [128, nch, 16], F32)
    ev = S.op("act", "activation", out=graw[:, :, :], in_=psg.rearrange("p (n k) -> p n k", k=16), func=AF.Copy,
              waits=[gdone, a0], inc=(gac, 1))
    S.psrel[slot] = ev
    a = None
    for n in range(nch):
        a = S.op("dve", "scalar_tensor_tensor", out=gt[:, n, :], in0=graw[:, n, :], scalar=rt[:, n:n + 1],
                 in1=bgbc[:, :], op0=ALU.mult, op1=ALU.add, waits=[ev, b0] if n == 0 else [a], inc=(gdv, 1))
    c1 = S.op("act", "activation", out=cap[:, :, :], in_=gt[:, :, :], func=AF.Tanh, scale=1.0 / 15.0, waits=[a], inc=(gac, 1))
    c2 = S.op("act", "activation", out=lp[:, :, :], in_=cap[:, :, :], func=AF.Exp, scale=-15.0, waits=[c1], inc=(gac, 1))
    c3 = S.op("act", "activation", out=lp[:, :, :], in_=lp[:, :, :], func=AF.Ln, bias=one1[:, 0:1], waits=[c2, o1], inc=(gac, 1))
    gtk = S.sb("gtk", [128, nch, 16], F32)
    d1 = S.op("dve", "tensor_scalar", out=gtk[:, :, 0:8], in0=cap[:, :, 0:8], scalar1=15.0, scalar2=None, op0=ALU.mult,
              waits=[c3], inc=(gdv, 1))
    d2 = S.op("dve", "tensor_scalar", out=gtk[:, :, 8:16], in0=lp[:, :, 8:16], scalar1=-1.0, scalar2=None, op0=ALU.mult,
              waits=[d1], inc=(gdv, 1))
    gl = S.sem("gld")
    v = S.op("sp", "dma_start", out=gates_tok[t0:t0 + T, :].rearrange("(n p) g -> p n g", p=128), in_=gtk[:, :, :],
             waits=[d2], inc=(gl, 16))
    S.pending.append(v)
    S.close()


def mlrec_stage(P, t0, T, full, projT, gates_tok, stC_in, stn_in, stC_out, stn_out,
                headg_d=None, w_out=None, hT=None):
    S = Stage(P, "mlr" if full else "mls")
    cst = consts_basic(S)
    nch = T // 128
    psA = S.psum("psA", [128, 512], F32)
    psNum = S.psum("psNum", [128, 512], F32)
    psQc = S.psum("psQc", [128, 512], F32)
    psC = S.psum("psC", [128, 1024], F32)
    psD = S.psum("psD", [128, 512], F32)
    psT = S.psum("psT", [128, 1024], F32)
    pT = psT[:, :].bitcast(BF16)
    ones_f = S.sb("ones_f", [128, 128], F32)
    ident_f = S.sb("ident_f", [128, 128], F32)
    ident_b = S.sb("ident_b", [128, 128], BF16)
    Utri = S.sb("Utri", [128, 128], F32)
    negm = S.sb("negm", [128, 128], F32)
    zer = S.sb("zer", [128, 128], F32)
    cs = S.sem("cset")
    S.op("pool", "memset", ones_f[:, :], 1.0)
    S.op("pool", "memset", zer[:, :], 0.0)
    S.op("pool", "affine_select", out=ident_f[:, :], in_=ones_f[:, :], pattern=[[1, 128]],
         compare_op=ALU.is_equal, fill=0.0, base=0, channel_multiplier=-1, waits=[cst["rdy"]])
    S.op("pool", "affine_select", out=ident_b[:, :], in_=cst["ones_bf"][:, :], pattern=[[1, 128]],
         compare_op=ALU.is_equal, fill=0.0, base=0, channel_multiplier=-1)
    S.op("pool", "affine_select", out=Utri[:, :], in_=ones_f[:, :], pattern=[[1, 128]],
         compare_op=ALU.is_ge, fill=0.0, base=0, channel_multiplier=-1)
    crdy = S.op("pool", "affine_select", out=negm[:, :], in_=zer[:, :], pattern=[[1, 128]],
                compare_op=ALU.is_ge, fill=-30000.0, base=0, channel_multiplier=-1, inc=(cs, 1))
    gtk = S.sb("gtk", [128, nch, 16], F32)
    lf = S.sb("lf", [128, nch * 8], F32)
    bcum = S.sb("bcum", [128, nch * 8], F32)
    Gbc = S.sb("Gbc", [128, nch * 8], F32)
    bias_s = S.sb("bias_s", [128, nch * 8], F32)
    ea = S.sb("ea", [128, nch * 8], F32)
    etok = S.sb("etok", [128, nch * 8], F32)
    eg = S.sb("eg", [128, nch * 8], F32)
    gl = S.sem("gld")
    gp = S.sem("gpre")
    v = S.op("sp", "dma_start", out=gtk[:, :, :], in_=gates_tok[t0:t0 + T, :].rearrange("(n p) g -> p n g", p=128),
             inc=(gl, 16))
    lf3 = lf[:, :].rearrange("p (n h) -> p n h", h=8)
    a = S.op("dve", "tensor_copy", out=lf3, in_=gtk[:, :, 8:16], waits=[v], inc=(gp, 1))
    NH = nch * 8
    S.op("pe", "matmul", out=psA[:, 256:256 + NH], lhsT=Utri[:, :], rhs=lf[:, :], start=True, stop=True, waits=[a, crdy])
    b = S.op("pe", "matmul", out=psA[:, 320:320 + NH], lhsT=ones_f[:, :], rhs=lf[:, :], start=True, stop=True, inc=(gp, 1))
    c1 = S.op("act", "activation", out=bcum[:, :], in_=psA[:, 256:256 + NH], func=AF.Copy, waits=[b], inc=(gp, 1))
    c2 = S.op("act", "activation", out=Gbc[:, :], in_=psA[:, 320:320 + NH], func=AF.Copy, waits=[c1], inc=(gp, 1))
    d1 = S.op("dve", "tensor_tensor", out=bias_s[:, :].rearrange("p (n h) -> p n h", h=8), in0=gtk[:, :, 0:8],
              in1=bcum[:, :].rearrange("p (n h) -> p n h", h=8), op=ALU.subtract, waits=[c2], inc=(gp, 1))
    d2 = S.op("dve", "tensor_tensor", out=ea[:, :], in0=bias_s[:, :], in1=Gbc[:, :], op=ALU.add, waits=[d1], inc=(gp, 1))
    e1 = S.op("act", "activation", out=ea[:, :], in_=ea[:, :], func=AF.Exp, waits=[d2], inc=(gp, 1))
    e2 = S.op("act", "activation", out=etok[:, :], in_=bcum[:, :], func=AF.Exp, waits=[e1], inc=(gp, 1))
    grdy = S.op("act", "activation", out=eg[:, :], in_=Gbc[:, :], func=AF.Exp, waits=[e2], inc=(gp, 1))
    ncols = 12 if full else 6
    hb = S.sb("hb", [128, ncols, T], BF16)
    Cst = S.sb("Cst", [128, 1024], F32)
    Cb = S.sb("Cb", [128, 1024], BF16)
    nst = S.sb("nst", [128, 2], F32)
    nb = S.sb("nb", [128, 2], BF16)
    kw = [S.sb(f"kw{i}", [128, 256], BF16) for i in range(2)]
    vtok = [S.sb(f"vtok{i}", [128, 512], BF16) for i in range(2)]
    if full:
        hsT = S.sb("hsT", [128, KC, T], BF16)
        otok = [S.sb(f"otok{i}", [128, 512], BF16) for i in range(2)]
        diagb = [S.sb(f"diagb{i}", [128, 128], F32) for i in range(2)]
        DT = [S.sb(f"DT{i}", [128, 128], F32) for i in range(2)]
        PT = [S.sb(f"PT{i}", [128, 128], BF16) for i in range(2)]
        intra = [S.sb(f"intra{i}", [128, 512], F32) for i in range(2)]
        num = [S.sb(f"num{i}", [128, 512], F32) for i in range(2)]
        junk = S.sb("junk", [128, 512], F32)
        go = [S.sb(f"go{i}", [128, 512], F32) for i in range(2)]
        hs = [S.sb(f"hs{i}", [128, 512], BF16) for i in range(2)]
        sm = [S.sb(f"sm{i}", [128, 8], F32) for i in range(2)]
        hg = S.sb("hg", [128, D], F32)
        hgl = S.sem("hgl")
        hgw = S.op("sp", "dma_start", out=hg[:, :], in_=headg_d[0, :].partition_broadcast(128), inc=(hgl, 16))
    hl = S.sem("hld")
    cl = S.sem("cld")
    cst_s = S.sem("cstore")
    pe_s = S.sem("rpe")
    ac_s = S.sem("ract")
    dv_s = S.sem("rdve")
    po_s = S.sem("rpool")
    prev = {"pe_step": None, "cstore": None, "hb_read": None, "psT_k": None, "psT_v": None, "psT_o": None,
            "psT_h": None, "psA_b": None, "psA_s": None, "psNum": None, "psQc": None, "psC": None, "psD": None,
            "cupd": None, "cb": None, "hsTcopy": None}
    step = 0
    for h in range(8):
        lw = []
        if full:
            rows = [(h * 256, 2, 0), (2048 + h * 256, 2, 2), (4096 + h * 512, 4, 4), (8192 + h * 512, 4, 8)]
        else:
            rows = [(2048 + h * 256, 2, 0), (4096 + h * 512, 4, 2)]
        for (r0, nck, dst) in rows:
            lw.append(S.op("sp", "dma_start", out=hb[:, dst:dst + nck, :],
                           in_=projT[r0:r0 + nck * 128, t0:t0 + T].rearrange("(j p) t -> p j t", p=128),
                           waits=[prev["hb_read"]], inc=(hl, 16)))
        hbw = [lw[-1]]
        cw1 = S.op("sp", "dma_start", out=Cst[:, :], in_=stC_in[:, h * 1024:(h + 1) * 1024],
                   waits=[prev["cstore"]], inc=(cl, 16))
        cw = S.op("sp", "dma_start", out=nst[:, :], in_=stn_in[:, h * 2:(h + 1) * 2], inc=(cl, 16))
        if full:
            prev["cb"] = S.op("act", "activation", out=Cb[:, :], in_=Cst[:, :], func=AF.Copy, waits=[cw, prev["pe_step"]], inc=(ac_s, 1))
            prev["cb"] = S.op("act", "activation", out=nb[:, :], in_=nst[:, :], func=AF.Copy, inc=(ac_s, 1))
        if full:
            qo, ko, vo, oo = 0, 2, 4, 8
        else:
            ko, vo = 0, 2
        for n in range(nch):
            par = step % 2
            ch = n * 8 + h
            tk = slice(n * 128, (n + 1) * 128)
            first = (n == 0)
            for j in range(2):
                S.op("pe", "transpose", out=pT[:, j * 128:(j + 1) * 128], in_=hb[:, ko + j, tk], identity=ident_b[:, :],
                     waits=(hbw + [prev["psT_k"], prev["psT_v"], crdy]) if j == 0 else [])
            for j in range(4):
                tv = S.op("pe", "transpose", out=pT[:, 256 + j * 128:256 + (j + 1) * 128], in_=hb[:, vo + j, tk],
                          identity=ident_b[:, :], waits=[prev["psT_v"]] if j == 0 else [],
                          inc=(pe_s, 1) if j == 3 else None)
            if full:
                for j in range(4):
                    to = S.op("pe", "transpose", out=pT[:, 1024 + j * 128:1024 + (j + 1) * 128], in_=hb[:, oo + j, tk],
                              identity=ident_b[:, :], waits=[prev["psT_o"], prev["psT_h"]] if j == 0 else [],
                              inc=(pe_s, 1) if j == 3 else None)
            kwv = S.op("dve", "tensor_scalar", out=kw[par][:, :], in0=pT[:, 0:256], scalar1=ea[:, ch:ch + 1], scalar2=None,
                       op0=ALU.mult, waits=[tv, grdy, prev["pe_step"]], inc=(dv_s, 1))
            prev["psT_k"] = kwv
            vtv = S.op("act", "activation", out=vtok[par][:, :], in_=pT[:, 256:768], func=AF.Copy,
                       waits=[tv, prev["pe_step"], kwv], inc=(ac_s, 1))
            prev["psT_v"] = vtv
            if full:
                otv = S.op("act", "activation", out=otok[par][:, :], in_=pT[:, 1024:1536], func=AF.Copy,
                           waits=[to], inc=(ac_s, 1))
                prev["psT_o"] = otv
                gov = S.op("pool", "tensor_tensor", out=go[par][:, :], in0=otok[par][:, :], in1=hg[:, h * 512:(h + 1) * 512],
                           op=ALU.mult, waits=[otv, hgw], inc=(po_s, 1))
                dgv = S.op("act", "activation", out=diagb[par][:, :], in_=ident_f[:, :], func=AF.Copy,
                           scale=bcum[:, ch:ch + 1], waits=[grdy], inc=(ac_s, 1))
                S.op("pe", "matmul", out=psA[:, 0:128], lhsT=ones_f[:, :], rhs=diagb[par][:, :], start=True, stop=False,
                     waits=[dgv, prev["psA_b"]])
                pbv = S.op("pe", "matmul", out=psA[:, 0:128], lhsT=ident_f[:, :], rhs=negm[:, :], start=False, stop=True,
                           inc=(pe_s, 1))
                for j in range(2):
                    psv = S.op("pe", "matmul", out=psA[:, 128:256], lhsT=hb[:, ko + j, tk], rhs=hb[:, qo + j, tk],
                               start=(j == 0), stop=(j == 1), waits=[prev["psA_s"]] if j == 0 else [],
                               inc=(pe_s, 1) if j == 1 else None)
                dtv = S.op("act", "activation", out=DT[par][:, :], in_=psA[:, 0:128], func=AF.Exp,
                           bias=bias_s[:, ch:ch + 1], waits=[pbv], inc=(ac_s, 1))
                prev["psA_b"] = dtv
                ptv = S.op("dve", "tensor_tensor", out=PT[par][:, :], in0=psA[:, 128:256], in1=DT[par][:, :], op=ALU.mult,
                           waits=[psv, dtv], inc=(dv_s, 1))
                prev["psA_s"] = ptv
                nmv = S.op("pe", "matmul", out=psNum[:, :], lhsT=PT[par][:, :], rhs=vtok[par][:, :], start=True, stop=True,
                           waits=[ptv, vtv, prev["psNum"]], inc=(pe_s, 1))
                S.op("pe", "matmul", out=psD[:, 0:1], lhsT=PT[par][:, :], rhs=cst["ones_bf"][:, 0:1], start=True, stop=True,
                     waits=[prev["psD"]])
                for j in range(2):
                    S.op("pe", "matmul", out=psQc[:, :], lhsT=hb[:, qo + j, tk], rhs=Cb[:, j * 512:(j + 1) * 512],
                         start=(j == 0), stop=(j == 1), waits=[prev["cb"], prev["psQc"]] if j == 0 else [])
                for j in range(2):
                    qcv = S.op("pe", "matmul", out=psD[:, 1:2], lhsT=hb[:, qo + j, tk], rhs=nb[:, j:j + 1],
                               start=(j == 0), stop=(j == 1), inc=(pe_s, 1) if j == 1 else None)
            for j in range(2):
                S.op("pe", "matmul", out=psC[:, j * 512:(j + 1) * 512], lhsT=kw[par][:, j * 128:(j + 1) * 128],
                     rhs=vtok[par][:, :], start=True, stop=True, waits=[kwv, vtv, prev["psC"]] if j == 0 else [])
            for j in range(2):
                pcv = S.op("pe", "matmul", out=psD[:, 2 + j:3 + j], lhsT=kw[par][:, j * 128:(j + 1) * 128],
                           rhs=cst["ones_bf"][:, 0:1], start=True, stop=True,
                           waits=[prev["psD"]] if (j == 0 and not full) else [],
                           inc=(pe_s, 1) if j == 1 else None)
            pe_last = pcv
            if full:
                inv = S.op("act", "activation", out=intra[par][:, :], in_=psNum[:, :], func=AF.Copy, waits=[nmv], inc=(ac_s, 1))
                prev["psNum"] = inv
                smx = sm[par]
                d0 = S.op("dve", "tensor_copy", out=smx[:, 0:2], in_=psD[:, 0:2], waits=[qcv], inc=(dv_s, 1))
                nuv = S.op("dve", "scalar_tensor_tensor", out=num[par][:, :], in0=psQc[:, :], scalar=etok[:, ch:ch + 1],
                           in1=intra[par][:, :], op0=ALU.mult, op1=ALU.add, waits=[inv, qcv], inc=(dv_s, 1))
                prev["psQc"] = nuv
                x1 = S.op("dve", "scalar_tensor_tensor", out=smx[:, 2:3], in0=smx[:, 1:2], scalar=etok[:, ch:ch + 1],
                          in1=smx[:, 0:1], op0=ALU.mult, op1=ALU.add, waits=[d0], inc=(dv_s, 1))
                x2a = S.op("dve", "scalar_tensor_tensor", out=smx[:, 3:4], in0=smx[:, 2:3], scalar=-1.0,
                           in1=smx[:, 2:3], op0=ALU.mult, op1=ALU.max, waits=[x1], inc=(dv_s, 1))
                x2 = S.op("dve", "tensor_scalar_max", out=smx[:, 3:4], in0=smx[:, 3:4], scalar1=1.0,
                          waits=[x2a], inc=(dv_s, 1))
                x3 = S.op("dve", "reciprocal", out=smx[:, 4:5], in_=smx[:, 3:4], waits=[x2], inc=(dv_s, 1))
                s1 = S.op("act", "activation", out=junk[:, :], in_=num[par][:, :], func=AF.Square, accum_out=smx[:, 5:6],
                          waits=[nuv], inc=(ac_s, 1))
                y1 = S.op("dve", "tensor_tensor", out=smx[:, 6:7], in0=smx[:, 4:5], in1=smx[:, 4:5], op=ALU.mult,
                          waits=[x3], inc=(dv_s, 1))
                y2 = S.op("dve", "scalar_tensor_tensor", out=smx[:, 6:7], in0=smx[:, 5:6], scalar=1.0 / 512.0,
                          in1=smx[:, 6:7], op0=ALU.mult, op1=ALU.mult, waits=[y1, s1], inc=(dv_s, 1))
                y3 = S.op("act", "activation", out=smx[:, 6:7], in_=smx[:, 6:7], func=AF.Sqrt, bias=cst["eps"][:, 0:1],
                          waits=[y2], inc=(ac_s, 1))
                y4 = S.op("dve", "reciprocal", out=smx[:, 7:8], in_=smx[:, 6:7], waits=[y3], inc=(dv_s, 1))
                y5 = S.op("dve", "tensor_tensor", out=smx[:, 7:8], in0=smx[:, 7:8], in1=smx[:, 4:5], op=ALU.mult,
                          waits=[y4], inc=(dv_s, 1))
                hsv = S.op("dve", "scalar_tensor_tensor", out=hs[par][:, :], in0=num[par][:, :], scalar=smx[:, 7:8],
                           in1=go[par][:, :], op0=ALU.mult, op1=ALU.mult, waits=[y5, gov], inc=(dv_s, 1))
                for j in range(4):
                    thv = S.op("pe", "transpose", out=pT[:, 1536 + j * 128:1536 + (j + 1) * 128],
                               in_=hs[par][:, j * 128:(j + 1) * 128], identity=ident_b[:, :],
                               waits=[hsv, prev["psT_h"]] if j == 0 else [], inc=(pe_s, 1) if j == 3 else None)
                pe_last = thv
                hcv = S.op("act", "activation", out=hsT[:, h * 4:(h + 1) * 4, tk],
                           in_=pT[:, 1536:2048].rearrange("p (j t) -> p j t", j=4), func=AF.Copy, waits=[thv], inc=(ac_s, 1))
                prev["psT_h"] = hcv
                prev["hsTcopy"] = hcv
            cu = S.op("dve", "scalar_tensor_tensor", out=Cst[:, :], in0=Cst[:, :], scalar=eg[:, ch:ch + 1], in1=psC[:, :],
                      op0=ALU.mult, op1=ALU.add, waits=[pcv, cw if first else None, prev["cb"] if full else None],
                      inc=(dv_s, 1))
            prev["psC"] = cu
            nu = S.op("dve", "scalar_tensor_tensor", out=nst[:, :], in0=nst[:, :], scalar=eg[:, ch:ch + 1], in1=psD[:, 2:4],
                      op0=ALU.mult, op1=ALU.add, waits=[cu], inc=(dv_s, 1))
            prev["psD"] = nu
            prev["cupd"] = nu
            if full:
                S.op("act", "activation", out=Cb[:, :], in_=Cst[:, :], func=AF.Copy, waits=[nu], inc=(ac_s, 1))
                prev["cb"] = S.op("act", "activation", out=nb[:, :], in_=nst[:, :], func=AF.Copy, inc=(ac_s, 1))
            prev["pe_step"] = pe_last
            step += 1
        prev["hb_read"] = prev["pe_step"]
        S.op("sp", "dma_start", out=stC_out[:, h * 1024:(h + 1) * 1024], in_=Cst[:, :], waits=[prev["cupd"]], inc=(cst_s, 16))
        prev["cstore"] = S.op("sp", "dma_start", out=stn_out[:, h * 2:(h + 1) * 2], in_=nst[:, :], inc=(cst_s, 16))
    S.pending.append(prev["cstore"])
    if full:
        views = [wview(w_out, 0, c * 128, key=("mlo", c)) for c in range(KC)]
        ws = WStream(S, views)
        st = [S.sb(f"st{i}", [128, T], F32) for i in range(2)]
        acc_s = [S.sem(f"acc{i}") for i in range(2)]
        epi_s = S.sem("epi")
        st_free = [None, None]
        pedone = S.sem("pedone")
        banks = [psC, psT]
        brel = [prev["psC"], prev["hsTcopy"]]
        for c in range(KC):
            buf, wwait, j = ws.take()
            bi = c % 2
            pb = banks[bi]
            for kc in range(KC):
                for tt in range(T // 512):
                    lastmm = (kc == KC - 1 and tt == T // 512 - 1)
                    v = S.op("pe", "matmul", out=pb[:, tt * 512:(tt + 1) * 512], lhsT=buf[:, kc, :],
                             rhs=hsT[:, kc, tt * 512:(tt + 1) * 512], start=(kc == 0), stop=(kc == KC - 1),
                             waits=[wwait, brel[bi], prev["hsTcopy"]] if (kc == 0 and tt == 0) else (),
                             inc=(pedone, 1) if lastmm else None)
            ws.release(j, v)
            i = c % 2
            a = S.op("act", "activation", out=st[i][:, :], in_=pb[:, 0:T], func=AF.Copy, waits=[v, st_free[i]], inc=(epi_s, 1))
            brel[bi] = a
            dv = S.op("pool", "dma_start", out=hT[c * 128:(c + 1) * 128, t0:t0 + T], in_=st[i][:, :],
                      accum_op=ALU.add, waits=[a], inc=(acc_s[i], 16))
            st_free[i] = dv
            S.pending.append(dv)
    S.close()


def select_stage(P, gath, sel_d, stC, stn):
    S = Stage(P, "sel")
    W = 8192 + 16
    acc = S.sb("acc", [128, W], F32)
    gb = [S.sb(f"gb{i}", [128, W], F32) for i in range(2)]
    sel = S.sb("sel", [128, 8], F32)
    ls = S.sem("sld")
    gs = [S.sem(f"sg{i}") for i in range(2)]
    ds = S.sem("sdv")
    ss = S.sem("sst")
    sw = S.op("sp", "dma_start", out=sel[:, :], in_=sel_d[:, :], inc=(ls, 16))
    free = [None, None]
    a = None
    for r in range(8):
        i = r % 2
        v = S.op("sp", "dma_start", out=gb[i][:, :], in_=gath[r, :, :], waits=[free[i]], inc=(gs[i], 16))
        if r == 0:
            a = S.op("dve", "tensor_scalar", out=acc[:, :], in0=gb[i][:, :], scalar1=sel[:, 0:1], scalar2=None,
                     op0=ALU.mult, waits=[v, sw], inc=(ds, 1))
        else:
            a = S.op("dve", "scalar_tensor_tensor", out=acc[:, :], in0=gb[i][:, :], scalar=sel[:, r:r + 1], in1=acc[:, :],
                     op0=ALU.mult, op1=ALU.add, waits=[v, a], inc=(ds, 1))
        free[i] = a
    S.op("sp", "dma_start", out=stC[:, :], in_=acc[:, 0:8192], waits=[a], inc=(ss, 16))
    v = S.op("sp", "dma_start", out=stn[:, :], in_=acc[:, 8192:W], inc=(ss, 16))
    S.pending.append(v)
    S.close()


def scale_stage(P, src, sel_d, dst):
    S = Stage(P, "scl")
    buf = S.sb("buf", [128, SW], F32)
    sel = S.sb("sel", [128, 1], F32)
    l1 = S.sem("scl1")
    l2 = S.sem("scl2")
    dv = S.sem("scdv")
    st = S.sem("scst")
    a = S.op("sp", "dma_start", out=sel[:, :], in_=sel_d[:, :], inc=(l1, 16))
    b = S.op("sp", "dma_start", out=buf[:, :], in_=src[:, :], inc=(l2, 16))
    c = S.op("dve", "tensor_scalar", out=buf[:, :], in0=buf[:, :], scalar1=sel[:, 0:1], scalar2=None, op0=ALU.mult,
             waits=[a, b], inc=(dv, 1))
    v = S.op("sp", "dma_start", out=dst[:, :], in_=buf[:, :], waits=[c], inc=(st, 16))
    S.pending.append(v)
    S.close()


SW = 8192 + 16


def build(mode):
    P = Prog()
    EI, EO, IN = "ExternalInput", "ExternalOutput", "Internal"
    d = P.dt
    p1 = mode in ("fused", "p1")
    p2 = mode in ("fused", "p2")
    nm = d("norm_mix", [2, D], F32, EI)
    nf = d("norm_ffn", [2, D], F32, EI)
    if p1:
        xT = d("xT", [D, NTOK], F32, EI)
        gm_w_in = d("gm_w_in", [D, 2 * D], F32, EI)
        gm_w_out = d("gm_w_out", [D, D], F32, EI)
        lng = d("gm_ln_g", [D], F32, EI)
        lnb = d("gm_ln_b", [D], F32, EI)
        wsT = d("gm_w_sT", [8, 128, 128], F32, EI)
        bs = d("gm_b_s", [8, 128], F32, EI)
        up0 = d("ffn_up0", [D, 4 * D], F32, EI)
        dn0 = d("ffn_dn0", [4 * D, D], F32, EI)
        ml_w_in = d("ml_w_in", [D, 12304], F32, EI)
        bgate = d("ml_b_gate", [16], F32, EI)
        zst = d("zst", [128, SW], F32, EI)
    if p2:
        headg = d("ml_head_g", [1, D], F32, EI)
        ml_w_out = d("ml_w_out", [D, D], F32, EI)
        up1 = d("ffn_up1", [D, 4 * D], F32, EI)
        dn1 = d("ffn_dn1", [4 * D, D], F32, EI)
        nfin = d("norm_final", [D], F32, EI)
        outT = d("outT", [D, NTOK], F32, EO)
    if mode == "fused":
        hT = d("hT", [D, NTOK], F32, IN)
        projT = d("projT", [12288, NTOK], BF16, IN)
        gtok = d("gtok", [NTOK, 16], F32, IN)
        stA = d("stA", [128, SW], F32, IN)
        stB = d("stB", [128, SW], F32, IN)
        stI = d("stI", [128, SW], F32, IN)
        xpT = d("xpT", [D, NTOK], F32, EI)
        hpT = d("hpT", [D, NTOK], F32, IN)
        projpT = d("projpT", [12288, NTOK], BF16, IN)
        gtokp = d("gtokp", [NTOK, 16], F32, IN)
        stAp = d("stAp", [128, SW], F32, IN)
        stBp = d("stBp", [128, SW], F32, IN)
        sel = d("sel", [128, 1], F32, EI)
    elif mode == "p1":
        hT = d("hT", [D, NTOK], F32, EO)
        projT = d("projT", [12288, NTOK], BF16, EO)
        gtok = d("gtok", [NTOK, 16], F32, EO)
        stA = d("stA", [128, SW], F32, IN)
        stB = d("stB", [128, SW], F32, EO)
    else:
        hin = d("hin", [D, NTOK], F32, EI)
        hT = d("hT", [D, NTOK], F32, IN)
        projT = d("projT", [12288, NTOK], BF16, EI)
        gtok = d("gtok", [NTOK, 16], F32, EI)
        stI = d("stI", [128, SW], F32, EI)
        stA = d("stA", [128, SW], F32, IN)
        stB = d("stB", [128, SW], F32, IN)

    def CN(t):
        return t[:, 0:8192], t[:, 8192:SW]

    if p1:
        copy_stage(P, hT, xT)
        for t in range(4):
            gmlp_stage(P, hT, t * 512, gm_w_in, gm_w_out, nm[0, :], lng, lnb, wsT, bs)
        for t in range(2):
            ffn_stage(P, hT, t * 1024, 1024, up0, dn0, nf[0, :], lname='0')
        for t in range(2):
            mlproj_stage(P, hT, t * 1024, 1024, ml_w_in, nm[1, :], bgate, projT, gtok)
    if mode == "p1":
        mlrec_stage(P, 0, 1024, False, projT, gtok, *CN(zst), *CN(stA))
        mlrec_stage(P, 1024, 1024, False, projT, gtok, *CN(stA), *CN(stB))
    if mode == "fused":
        copy_stage(P, hpT, xpT)
        for t in range(4):
            gmlp_stage(P, hpT, t * 512, gm_w_in, gm_w_out, nm[0, :], lng, lnb, wsT, bs)
        for t in range(2):
            ffn_stage(P, hpT, t * 1024, 1024, up0, dn0, nf[0, :], lname='0')
        for t in range(2):
            mlproj_stage(P, hpT, t * 1024, 1024, ml_w_in, nm[1, :], bgate, projpT, gtokp, kv_only=True)
        mlrec_stage(P, 0, 1024, False, projpT, gtokp, *CN(zst), *CN(stAp))
        mlrec_stage(P, 1024, 1024, False, projpT, gtokp, *CN(stAp), *CN(stBp))
        scale_stage(P, stBp, sel, stI)
    if mode == "p2":
        copy_stage(P, hT, hin)
    if p2:
        mlrec_stage(P, 0, 1024, True, projT, gtok, *CN(stI), *CN(stA), headg_d=headg, w_out=ml_w_out, hT=hT)
        mlrec_stage(P, 1024, 1024, True, projT, gtok, *CN(stA), *CN(stB), headg_d=headg, w_out=ml_w_out, hT=hT)
        for t in range(2):
            ffn_stage(P, hT, t * 1024, 1024, up1, dn1, nf[1, :], lname='1')
        for t in range(2):
            final_stage(P, hT, outT, t * 1024, 1024, nfin)
    return P


def _c(a):
    return np.ascontiguousarray(a, dtype=np.float32)


MODE = "split"


def kernel(x, norm_mix, norm_ffn, gm_w_in, gm_ln_g, gm_ln_b, gm_w_s, gm_b_s, gm_w_out,
           ml_w_in, ml_b_gate, ml_head_g, ml_w_out, ffn_w_up, ffn_w_down, norm_final):
    x = np.asarray(x, dtype=np.float32)
    xs = x.reshape(NCORES, NTOK, D)
    xT = [_c(xs[c].T) for c in range(NCORES)]
    com = {"norm_mix": _c(norm_mix), "norm_ffn": _c(norm_ffn)}
    in1 = {"gm_w_in": _c(gm_w_in[0]), "gm_w_out": _c(gm_w_out[0]), "gm_ln_g": _c(gm_ln_g[0]),
           "gm_ln_b": _c(gm_ln_b[0]), "gm_w_sT": _c(np.transpose(np.asarray(gm_w_s[0]), (0, 2, 1))),
           "gm_b_s": _c(gm_b_s[0]), "ffn_up0": _c(ffn_w_up[0]), "ffn_dn0": _c(ffn_w_down[0]),
           "ml_w_in": _c(ml_w_in[0]), "ml_b_gate": _c(ml_b_gate[0]),
           "zst": np.zeros((128, SW), np.float32)}
    in2 = {"ml_head_g": _c(np.asarray(ml_head_g[0]).reshape(1, D)), "ml_w_out": _c(ml_w_out[0]),
           "ffn_up1": _c(ffn_w_up[1]), "ffn_dn1": _c(ffn_w_down[1]), "norm_final": _c(norm_final)}
    if MODE == "fused":
        P = build("fused")
        maps = []
        for c in range(NCORES):
            sel = np.full((128, 1), float(c % 2), np.float32)
            m = dict(com); m.update(in1); m.update(in2); m["xT"] = xT[c]; m["sel"] = sel
            m["xpT"] = xT[c - (c % 2)]
            maps.append(m)
        res = run_bass_kernel_spmd(P.nc, maps, core_ids=list(range(NCORES)))
        outs = [r["outT"] for r in res.results]
    else:
        P1 = build("p1")
        maps = []
        for c in range(NCORES):
            m = dict(com); m.update(in1); m["xT"] = xT[c]
            maps.append(m)
        r1 = run_bass_kernel_spmd(P1.nc, maps, core_ids=list(range(NCORES))).results
        P2 = build("p2")
        maps = []
        for c in range(NCORES):
            m = dict(com); m.update(in2)
            m["hin"] = r1[c]["hT"]; m["projT"] = r1[c]["projT"]; m["gtok"] = r1[c]["gtok"]
            m["stI"] = r1[c - 1]["stB"] if c % 2 == 1 else np.zeros((128, SW), np.float32)
            maps.append(m)
        r2 = run_bass_kernel_spmd(P2.nc, maps, core_ids=list(range(NCORES))).results
        outs = [r["outT"] for r in r2]
    out = np.stack([np.asarray(o, dtype=np.float32).T for o in outs], axis=0)
    return np.ascontiguousarray(out.reshape(4, 4096, D))
```
